# Optimizing a Trainium2 kernel written in Bass

```python
import math
import jax, jax.numpy as jnp
from jax import lax
import numpy as np

D_MODEL = 2048
BATCH = 4
SEQ = 4096
DEPTH = 1

RET_HEADS = 8
RET_QK_DIM = 128
RET_V_DIM = 256
RET_QK_WIDTH = RET_HEADS * RET_QK_DIM
RET_V_WIDTH = RET_HEADS * RET_V_DIM
RET_CHUNK = 128
ROPE_BASE = 10000.0
S5_GROUP = 16
S5_WIDTH = D_MODEL // 2
S5_GROUPS = S5_WIDTH // S5_GROUP
S5_STATE = 64
DT_MIN = 1e-3
DT_MAX = 1e-1
D_FF = -(-8 * D_MODEL // (3 * 256)) * 256
NORM_EPS = 1e-6
GN_EPS = 1e-5

IN_SIZES = (RET_QK_WIDTH, RET_QK_WIDTH, RET_V_WIDTH, RET_V_WIDTH, S5_WIDTH, D_MODEL, D_MODEL)
IN_WIDTH = sum(IN_SIZES)
IN_SPLITS = tuple(int(s) for s in np.cumsum(IN_SIZES)[:-1])

kernel_name = "hybrid_retention_s5_gated_block"


def rmsnorm(x, g):
    xf = x.astype(jnp.float32)
    y = xf * lax.rsqrt(jnp.mean(xf * xf, axis=-1, keepdims=True) + NORM_EPS)
    return (y * g.astype(jnp.float32)).astype(x.dtype)


def head_group_norm(y):
    yf = y.astype(jnp.float32)
    mu = jnp.mean(yf, axis=-1, keepdims=True)
    var = jnp.mean(jnp.square(yf - mu), axis=-1, keepdims=True)
    return ((yf - mu) * lax.rsqrt(var + GN_EPS)).astype(y.dtype)


def rope(t, cos, sin):
    t1, t2 = jnp.split(t, 2, axis=-1)
    return jnp.concatenate([t1 * cos - t2 * sin, t1 * sin + t2 * cos], axis=-1)


def retention(q, k, v):
    Bn, L, H, dk = q.shape
    dv = v.shape[-1]
    C = RET_CHUNK
    N = L // C
    dt = q.dtype
    log_g = jnp.log1p(-jnp.exp2(-5.0 - jnp.arange(H, dtype=jnp.float32)))
    idx = jnp.arange(C, dtype=jnp.float32)
    rel = idx[:, None] - idx[None, :]
    decay = jnp.where(rel[None] >= 0,
                      jnp.exp(jnp.maximum(rel, 0.0)[None] * log_g[:, None, None]), 0.0)
    w_state = jnp.exp((C - 1.0 - idx)[:, None] * log_g[None, :])
    w_cross = jnp.exp((idx + 1.0)[:, None] * log_g[None, :])
    chunk_decay = jnp.exp(C * log_g)

    qc = q.reshape(Bn, N, C, H, dk)
    kc = k.reshape(Bn, N, C, H, dk)
    vc = v.reshape(Bn, N, C, H, dv)

    scores = jnp.einsum('bnchd,bnshd->bnhcs', qc, kc) * decay.astype(dt)[None, None]
    inner = jnp.einsum('bnhcs,bnshv->bnchv', scores, vc)

    kv = jnp.einsum('bnchd,ch,bnchv->bnhdv', kc, w_state.astype(dt), vc)
    cd = chunk_decay.astype(dt)[None, :, None, None]

    def step(R, kv_n):
        return cd * R + kv_n, R

    R0 = jnp.zeros((Bn, H, dk, dv), dtype=kv.dtype)
    _, R_prev = lax.scan(step, R0, jnp.moveaxis(kv, 1, 0))
    R_prev = jnp.moveaxis(R_prev, 0, 1)

    cross = jnp.einsum('bnchd,bnhdv->bnchv', qc, R_prev) * w_cross.astype(dt)[None, None, :, :, None]
    return (inner + cross).reshape(Bn, L, H, dv)


def s5_ssm(u, a_re, a_im, log_dt, b_re, b_im, c_re, c_im, d_skip):
    Bn, L, _ = u.shape
    ug = u.reshape(Bn, L, S5_GROUPS, S5_GROUP)
    dt = jnp.exp(log_dt)[:, None]
    mag = jnp.exp(a_re * dt)
    lb_re = mag * jnp.cos(a_im * dt)
    lb_im = mag * jnp.sin(a_im * dt)
    nr = lb_re - 1.0
    den = a_re * a_re + a_im * a_im
    f_re = (nr * a_re + lb_im * a_im) / den
    f_im = (lb_im * a_re - nr * a_im) / den
    bb_re = f_re[..., None] * b_re - f_im[..., None] * b_im
    bb_im = f_re[..., None] * b_im + f_im[..., None] * b_re
    bu_re = jnp.einsum('blgh,gph->blgp', ug, bb_re)
    bu_im = jnp.einsum('blgh,gph->blgp', ug, bb_im)
    A_re = jnp.broadcast_to(lb_re, (L,) + lb_re.shape)
    A_im = jnp.broadcast_to(lb_im, (L,) + lb_im.shape)

    def combine(e1, e2):
        a1r, a1i, b1r, b1i = e1
        a2r, a2i, b2r, b2i = e2
        return (a1r * a2r - a1i * a2i,
                a1r * a2i + a1i * a2r,
                a2r * b1r - a2i * b1i + b2r,
                a2r * b1i + a2i * b1r + b2i)

    def scan_one(br, bi):
        _, _, xr, xi = lax.associative_scan(combine, (A_re, A_im, br, bi), axis=0)
        return xr, xi

    x_re, x_im = jax.vmap(scan_one)(bu_re, bu_im)
    y = (jnp.einsum('blgp,ghp->blgh', x_re, c_re)
         - jnp.einsum('blgp,ghp->blgh', x_im, c_im)
         + d_skip[None, None] * ug)
    return y.reshape(Bn, L, S5_WIDTH)


def token_mixer(h, w_in, w_ret_out, a_re, a_im, log_dt, b_re, b_im, c_re, c_im, d_skip,
                w_s5_glu, w_out, cos, sin):
    Bn, L, _ = h.shape
    proj = h @ w_in
    q, k, v, g_ret, u, gate_r, gate_s = jnp.split(proj, IN_SPLITS, axis=-1)
    q = rope(q.reshape(Bn, L, RET_HEADS, RET_QK_DIM), cos, sin)
    k = rope(k.reshape(Bn, L, RET_HEADS, RET_QK_DIM), cos, sin) * (RET_QK_DIM ** -0.5)
    v = v.reshape(Bn, L, RET_HEADS, RET_V_DIM)
    ret = head_group_norm(retention(q, k, v)).reshape(Bn, L, RET_V_WIDTH)
    y_ret = (jax.nn.silu(g_ret) * ret) @ w_ret_out
    y_ssm = jax.nn.gelu(s5_ssm(u, a_re, a_im, log_dt, b_re, b_im, c_re, c_im, d_skip))
    glu_a, glu_b = jnp.split(y_ssm @ w_s5_glu, 2, axis=-1)
    y_s5 = glu_a * jax.nn.sigmoid(glu_b)
    merged = jax.nn.sigmoid(gate_r) * y_ret + jax.nn.sigmoid(gate_s) * y_s5
    return merged @ w_out


def swiglu_ffn(h, w_ffn_in, w_ffn_out):
    a, b = jnp.split(h @ w_ffn_in, 2, axis=-1)
    return (jax.nn.silu(a) * b) @ w_ffn_out


def setup_inputs(seed: int = 0) -> dict:
    key = jax.random.key(seed)
    ks = jax.random.split(key, 20)
    f32 = jnp.float32

    def dense(k, shape, fan_in):
        return jax.random.normal(k, shape, f32) * (fan_in ** -0.5)

    x = jax.random.normal(ks[0], (BATCH, SEQ, D_MODEL), f32)
    c = jax.random.normal(ks[1], (BATCH, D_MODEL), f32)
    w_ada = dense(ks[2], (DEPTH, D_MODEL, 6 * D_MODEL), D_MODEL)
    b_ada = 0.01 * jax.random.normal(ks[3], (DEPTH, 6 * D_MODEL), f32)
    norm_gains = 1.0 + 0.05 * jax.random.normal(ks[4], (DEPTH, 4, D_MODEL), f32)
    w_in = dense(ks[5], (DEPTH, D_MODEL, IN_WIDTH), D_MODEL)
    w_ret_out = dense(ks[6], (DEPTH, RET_V_WIDTH, D_MODEL), RET_V_WIDTH)
    ssm_a_re = -0.5 + 0.01 * jax.random.normal(ks[7], (DEPTH, S5_GROUPS, S5_STATE), f32)
    ssm_a_im = jnp.broadcast_to(math.pi * jnp.arange(S5_STATE, dtype=f32),
                                (DEPTH, S5_GROUPS, S5_STATE))
    ssm_log_dt = jax.random.uniform(ks[8], (DEPTH, S5_GROUPS), f32,
                                    math.log(DT_MIN), math.log(DT_MAX))
    ssm_b_re = dense(ks[9], (DEPTH, S5_GROUPS, S5_STATE, S5_GROUP), 2 * S5_GROUP)
    ssm_b_im = dense(ks[10], (DEPTH, S5_GROUPS, S5_STATE, S5_GROUP), 2 * S5_GROUP)
    ssm_c_re = 0.5 * jax.random.normal(ks[11], (DEPTH, S5_GROUPS, S5_GROUP, S5_STATE), f32)
    ssm_c_im = 0.5 * jax.random.normal(ks[12], (DEPTH, S5_GROUPS, S5_GROUP, S5_STATE), f32)
    ssm_d = jax.random.normal(ks[13], (DEPTH, S5_GROUPS, S5_GROUP), f32)
    w_s5_glu = dense(ks[14], (DEPTH, S5_WIDTH, 2 * D_MODEL), S5_WIDTH)
    w_out = dense(ks[15], (DEPTH, D_MODEL, D_MODEL), D_MODEL)
    w_ffn_in = dense(ks[16], (DEPTH, D_MODEL, 2 * D_FF), D_MODEL)
    w_ffn_out = dense(ks[17], (DEPTH, D_FF, D_MODEL), D_FF)
    return {"x": x, "c": c, "w_ada": w_ada, "b_ada": b_ada, "norm_gains": norm_gains,
            "w_in": w_in, "w_ret_out": w_ret_out, "ssm_a_re": ssm_a_re, "ssm_a_im": ssm_a_im,
            "ssm_log_dt": ssm_log_dt, "ssm_b_re": ssm_b_re, "ssm_b_im": ssm_b_im,
            "ssm_c_re": ssm_c_re, "ssm_c_im": ssm_c_im, "ssm_d": ssm_d, "w_s5_glu": w_s5_glu,
            "w_out": w_out, "w_ffn_in": w_ffn_in, "w_ffn_out": w_ffn_out}


def reference(x, c, w_ada, b_ada, norm_gains, w_in, w_ret_out, ssm_a_re, ssm_a_im, ssm_log_dt,
              ssm_b_re, ssm_b_im, ssm_c_re, ssm_c_im, ssm_d, w_s5_glu, w_out, w_ffn_in, w_ffn_out):
    L = x.shape[1]
    pos = jnp.arange(L, dtype=jnp.float32)
    inv_freq = ROPE_BASE ** (-jnp.arange(RET_QK_DIM // 2, dtype=jnp.float32) * (2.0 / RET_QK_DIM))
    ang = pos[:, None] * inv_freq[None, :]
    cos = jnp.cos(ang)[None, :, None, :].astype(x.dtype)
    sin = jnp.sin(ang)[None, :, None, :].astype(x.dtype)
    c_act = jax.nn.silu(c)
    for l in range(DEPTH):
        mod = c_act @ w_ada[l] + b_ada[l]
        sh_m, sc_m, gt_m, sh_f, sc_f, gt_f = [m[:, None, :] for m in jnp.split(mod, 6, axis=-1)]
        g = norm_gains[l]
        h = rmsnorm(x, g[0]) * (1.0 + sc_m) + sh_m
        y = token_mixer(h, w_in[l], w_ret_out[l], ssm_a_re[l], ssm_a_im[l], ssm_log_dt[l],
                        ssm_b_re[l], ssm_b_im[l], ssm_c_re[l], ssm_c_im[l], ssm_d[l],
                        w_s5_glu[l], w_out[l], cos, sin)
        x = x + gt_m * rmsnorm(y, g[1])
        h = rmsnorm(x, g[2]) * (1.0 + sc_f) + sh_f
        y = swiglu_ffn(h, w_ffn_in[l], w_ffn_out[l])
        x = x + gt_f * rmsnorm(y, g[3])
    return x
```

```python
import contextlib
import os
import math
import numpy as np
import concourse.bass as bass
import concourse.mybir as mybir
from concourse.bass_utils import run_bass_kernel_spmd

F32 = mybir.dt.float32
BF16 = mybir.dt.bfloat16
I32 = mybir.dt.int32
ALU = mybir.AluOpType
AF = mybir.ActivationFunctionType
ENGS = ("tensor", "vector", "scalar", "gpsimd", "sync")

D = 2048
DFF = 5632
NT = 4
SUB = 32
NSLOT = 2
TWO_PI = float(2 * math.pi)


class Sched:
    def __init__(self):
        self.ops = []
        self.last_writer = {}
        self.readers = {}
        self.lane_last = {}
        self.bar = set()

    def barrier(self):
        last = {}
        for i, o in enumerate(self.ops):
            last[(o['eng'], o['lane'])] = i
        self.bar = set(last.values())

    def op(self, eng, fn, reads=(), writes=(), lane=None):
        idx = len(self.ops)
        deps = set(self.bar)
        for k in reads:
            w = self.last_writer.get(k)
            if w is not None:
                deps.add(w)
        for k in writes:
            w = self.last_writer.get(k)
            if w is not None:
                deps.add(w)
            for r in self.readers.get(k, ()):
                deps.add(r)
        if lane is not None:
            p = self.lane_last.get(lane)
            if p is not None:
                deps.add(p)
            self.lane_last[lane] = idx
        self.ops.append(dict(eng=eng, fn=fn, deps=sorted(deps), lane=lane))
        for k in reads:
            self.readers.setdefault(k, []).append(idx)
        for k in writes:
            self.last_writer[k] = idx
            self.readers[k] = []
        return idx

    def emit(self, nc, final_wait_eng="sync"):
        ops = self.ops
        pos = {}
        cnt = {e: 0 for e in ENGS}
        for i, o in enumerate(ops):
            pos[i] = cnt[o["eng"]]
            cnt[o["eng"]] += 1
        need = set()
        for i, o in enumerate(ops):
            for d in o["deps"]:
                do = ops[d]
                if do["lane"] is not None:
                    continue
                if do["eng"] == o["eng"]:
                    if o["eng"] == "tensor":
                        continue
                    if pos[i] - pos[d] > 3:
                        continue
                need.add(d)
        lanes = sorted(set(o["lane"] for o in ops if o["lane"] is not None), key=str)
        with contextlib.ExitStack() as es:
            esem = {e: es.enter_context(nc.semaphore("ms_" + e)) for e in ENGS}
            lsem = {l: es.enter_context(nc.semaphore("ln_%d" % j)) for j, l in enumerate(lanes)}
            block = es.enter_context(nc.Block())
            msno = {}
            mc = {e: 0 for e in ENGS}
            lane_no = {}
            lc = {l: 0 for l in lanes}
            for i, o in enumerate(ops):
                if o["lane"] is not None:
                    lc[o["lane"]] += 1
                    lane_no[i] = lc[o["lane"]]
                elif i in need:
                    mc[o["eng"]] += 1
                    msno[i] = mc[o["eng"]]

            def make(eng_name):
                def body(eng):
                    waited_e = {e: 0 for e in ENGS}
                    waited_l = {l: 0 for l in lanes}
                    for i, o in enumerate(ops):
                        if o["eng"] != eng_name:
                            continue
                        for d in o["deps"]:
                            do = ops[d]
                            if do["lane"] is not None:
                                v = lane_no[d] * 16
                                if waited_l[do["lane"]] < v:
                                    eng.wait_ge(lsem[do["lane"]], v)
                                    waited_l[do["lane"]] = v
                            elif d in msno:
                                v = msno[d]
                                if waited_e[do["eng"]] < v:
                                    eng.wait_ge(esem[do["eng"]], v)
                                    waited_e[do["eng"]] = v
                        ins = o["fn"](eng)
                        if o["lane"] is not None:
                            ins.then_inc(lsem[o["lane"]], 16)
                        elif i in msno:
                            ins.then_inc(esem[eng_name], 1)
                    if eng_name == final_wait_eng:
                        for l in lanes:
                            if lc[l] > 0:
                                eng.wait_ge(lsem[l], 16 * lc[l])
                return body

            for e in ENGS:
                if cnt[e] > 0 or e == final_wait_eng:
                    getattr(block, e)(make(e))


def build_program():
    nc = bass.Bass("TRN2", target_bir_lowering=False)
    S = Sched()

    def din(name, shape):
        return nc.dram_tensor(name, list(shape), F32, kind="ExternalInput").ap()

    x_d = din("x", [2048, D]); xp_d = din("xp", [2048, D]); flag_d = din("flag", [128, 1])
    cT_d = din("cT", [128, 16]); wada_d = din("w_ada", [D, 6 * D]); badaT_d = din("badaT", [128, 96])
    gains_d = din("gains", [128, 4, 16])
    win_d = din("w_in", [D, 11264]); wro_d = din("w_ret_out", [D, D]); wglu_d = din("w_s5_glu", [1024, 4096])
    wout_d = din("w_out", [D, D]); wfi_d = din("w_ffn_in", [D, 2 * DFF]); wfo_d = din("w_ffn_out", [DFF, D])
    rope_d = [din("rope%d" % i, [128, 32, 64]) for i in range(4)]
    decay_d = din("decayT", [128, 8, 128]); wcross_d = din("wcrossF", [128, 8, 128])
    wstate_d = din("wstate", [128, 8]); cdec_d = din("cdec", [128, 8]); ident_d = din("ident", [128, 128])
    are_d = din("are", [128, 32]); aim_d = din("aim", [128, 32]); ldt_d = din("ldt", [128, 32])
    bre_d = din("bre", [128, 32, 16]); bim_d = din("bim", [128, 32, 16])
    cre_d = din("cre", [128, 32, 128]); cim_d = din("cim", [128, 32, 128]); dfull_d = din("dfull", [128, 8, 128])
    tvec_d = din("tvec", [128, SUB])
    out_d = nc.dram_tensor("out", [2048, D], F32, kind="ExternalOutput").ap()
    DBG = bool(os.environ.get("KDBG"))
    if DBG:
        dbg_ret = nc.dram_tensor("dbg_ret", [128, 16, 512], BF16, kind="ExternalOutput").ap()
        dbg_ys = nc.dram_tensor("dbg_ys", [128, 8, 512], BF16, kind="ExternalOutput").ap()
        dbg_mg = nc.dram_tensor("dbg_mg", [128, 16, 512], BF16, kind="ExternalOutput").ap()
        dbg_x1 = nc.dram_tensor("dbg_x1", [128, 4, D], F32, kind="ExternalOutput").ap()
        dbg_q = nc.dram_tensor("dbg_q", [128, 16, 512], BF16, kind="ExternalOutput").ap()
        dbg_k = nc.dram_tensor("dbg_k", [128, 8, 512], BF16, kind="ExternalOutput").ap()
        dbg_dec = nc.dram_tensor("dbg_dec", [128, 8, 128], BF16, kind="ExternalOutput").ap()
        dbg_dec0 = nc.dram_tensor("dbg_dec0", [128, 8, 128], BF16, kind="ExternalOutput").ap()

    es = contextlib.ExitStack()
    with es:
        def sb(name, shape, dt):
            return es.enter_context(nc.sbuf_tensor("s_" + name, list(shape), dt))

        def ps(name, dt=F32):
            return es.enter_context(nc.psum_tensor(name, [128, 512 if dt == F32 else 1024], dt))

        wring = [sb("wr%d" % i, [128, 4096], BF16) for i in range(NSLOT)]
        es2 = contextlib.ExitStack()
        sbt = lambda name, shape, dt: es2.enter_context(nc.sbuf_tensor("s_" + name, list(shape), dt))
        R = sb("R", [128, 8, 256], F32); Rb = sb("Rb", [128, 8, 256], BF16)
        ropet = [sb("ropet%d" % i, [128, 4, 64], F32) for i in range(4)]
        decayT = sb("decayT", [128, 8, 128], BF16); wcross = sb("wcross", [128, 8, 128], BF16)
        wstate = sb("wstate", [128, 8], F32); cdec = sb("cdec", [128, 8], F32)
        ident = sb("ident", [128, 128], BF16); ones = sb("ones", [128, 128], BF16); onesg = sb("onesg", [128, 128], BF16)
        Gtab = [sb("Gtab%d" % i, [128, D], BF16) for i in range(2)]
        flag = sb("flag", [128, 1], F32)
        AB = sb("AB", [128, 6, 16], F32)
        st4 = sb("st4", [128, 16], F32)
        ssy = sb("ssy", [128, 4, 8], F32)
        tmpA = sb("tmpA", [128, 512], F32); tmpB = sb("tmpB", [128, 512], F32)
        tb1 = sb("tb1", [128, 512], BF16); tb2 = sb("tb2", [128, 512], BF16)
        sTb = [sb("sTb%d" % i, [128, 128], BF16) for i in range(2)]
        ctab = sb("ctab", [128, 32, SUB], BF16); stab = sb("stab", [128, 32, SUB], BF16); rtab = sb("rtab", [128, 32, SUB], F32)
        BTre = sb("BTre", [128, 32, 128], BF16); BTim = sb("BTim", [128, 32, 128], BF16)
        Cre = sb("Cre", [128, 32, 128], BF16); Cim = sb("Cim", [128, 32, 128], BF16); Dfl = sb("Dfl", [128, 8, 128], BF16)
        s5s = sb("s5s", [128, 12, 32], F32)
        car = sb("car", [128, 2, 32], F32)
        bigps = [ps("psb%d" % i) for i in range(4)]
        auxps = [ps("psa%d" % i) for i in range(2)]
        tps = [ps("pst%d" % i, BF16) for i in range(2)]
        cnt = dict(big=0, aux=0, t=0, w=0, ld=0, st=0)

        def nxt(pool):
            i = cnt[pool]; cnt[pool] += 1
            if pool == "big":
                return bigps[i % 4], ("psb", i % 4)
            if pool == "aux":
                return auxps[i % 2], ("psa", i % 2)
            return tps[i % 2], ("pst", i % 2)

        def blk(b0, n):
            return [("ar", b) for b in range(b0, b0 + n)]

        def V(eng, fn, reads, writes):
            S.op(eng, fn, reads=reads, writes=writes)

        def load(dst_ap, src_ap, key, cast=False):
            lane = "ld%d" % (cnt["ld"] % 8); cnt["ld"] += 1
            S.op("gpsimd" if cast else "sync", lambda e: e.dma_start(out=dst_ap, in_=src_ap), writes=[key], lane=lane)

        def wtile(src2d, KC, ncols):
            i = cnt["w"] % NSLOT; cnt["w"] += 1
            view = wring[i][:, 0:KC * ncols].rearrange("p (k n) -> p k n", n=ncols)
            S.op("gpsimd", lambda e: e.dma_start(out=view, in_=src2d.rearrange("(k p) n -> p k n", p=128)),
                 writes=[("wr", i)], lane="w%d" % i)
            return view, ("wr", i)

        cT = sbt("cTs", [128, 16], F32); csb = sbt("csb", [128, 16], BF16); sgc = sbt("sgc", [128, 16], F32)
        modT = sbt("modT", [128, 96], F32); badaT = sbt("badaTs", [128, 96], F32); gains = sbt("gains", [128, 4, 16], F32)
        diagb = sbt("diagb", [128, 128], BF16)
        load(flag[:], flag_d, "flag"); load(cT[:], cT_d, "cT"); load(badaT[:], badaT_d, "badaT"); load(gains[:], gains_d, "gains")
        load(decayT[:], decay_d, "decayT", True); load(wcross[:], wcross_d, "wcross", True)
        load(wstate[:], wstate_d, "wstate"); load(cdec[:], cdec_d, "cdec"); load(ident[:], ident_d, "ident", True)
        load(Cre[:], cre_d, "Cre", True); load(Cim[:], cim_d, "Cim", True); load(Dfl[:], dfull_d, "Dfl", True)
        if DBG:
            S.op("sync", lambda e: e.dma_start(out=dbg_dec0, in_=decayT[:]), reads=["decayT"], lane="dbg4")
        V("vector", lambda e: e.memset(ones[:], 1.0), [], ["ones"])
        V("vector", lambda e: e.memset(onesg[:], 1.0 / 256.0), [], ["onesg"])
        V("vector", lambda e: e.memset(R[:], 0.0), [], ["R"])
        V("vector", lambda e: e.memset(car[:], 0.0), [], ["car"])

        V("scalar", lambda e: e.activation(out=sgc[:], in_=cT[:], func=AF.Sigmoid), ["cT"], ["sgc"])
        V("vector", lambda e: e.tensor_tensor(out=csb[:], in0=cT[:], in1=sgc[:], op=ALU.mult), ["cT", "sgc"], ["csb"])
        pa, pak = auxps[0], ("psa", 0)
        for tI in range(48):
            wv, wk = wtile(wada_d[:, tI * 256:(tI + 1) * 256], 16, 256)

            def mm(e, wv=wv, tI=tI):
                ins = None
                for s in range(2):
                    j = tI * 2 + s
                    for kc in range(16):
                        ins = e.matmul(pa[:, j:j + 1], lhsT=wv[:, kc, s * 128:(s + 1) * 128], rhs=csb[:, kc:kc + 1],
                                       start=(kc == 0), stop=(kc == 15))
                return ins
            V("tensor", mm, [wk, "csb"], [pak])
        V("vector", lambda e: e.tensor_tensor(out=modT[:], in0=pa[:, 0:96], in1=badaT[:], op=ALU.add), [pak, "badaT"], ["modT"])
        V("vector", lambda e: e.scalar_tensor_tensor(out=AB[:, 0, :], in0=modT[:, 16:32], scalar=1.0, in1=gains[:, 0, :], op0=ALU.add, op1=ALU.mult), ["modT", "gains"], ["AB"])
        V("vector", lambda e: e.tensor_copy(out=AB[:, 1, :], in_=modT[:, 0:16]), ["modT"], ["AB"])
        V("vector", lambda e: e.scalar_tensor_tensor(out=AB[:, 2, :], in0=modT[:, 64:80], scalar=1.0, in1=gains[:, 2, :], op0=ALU.add, op1=ALU.mult), ["modT", "gains"], ["AB"])
        V("vector", lambda e: e.tensor_copy(out=AB[:, 3, :], in_=modT[:, 48:64]), ["modT"], ["AB"])
        V("vector", lambda e: e.tensor_tensor(out=AB[:, 4, :], in0=modT[:, 32:48], in1=gains[:, 1, :], op=ALU.mult), ["modT", "gains"], ["AB"])
        V("vector", lambda e: e.tensor_tensor(out=AB[:, 5, :], in0=modT[:, 80:96], in1=gains[:, 3, :], op=ALU.mult), ["modT", "gains"], ["AB"])
        for gi in range(2):
            for kc in range(16):
                V("vector", lambda e, gi=gi, kc=kc: e.tensor_scalar(out=diagb[:], in0=ident[:], scalar1=AB[:, 4 + gi, kc:kc + 1], scalar2=None, op0=ALU.mult), ["ident", "AB"], ["diagb"])
                pb, pbk = nxt("aux")
                V("tensor", lambda e, pb=pb: e.matmul(pb[:, 0:128], lhsT=ones[:], rhs=diagb[:], start=True, stop=True), ["ones", "diagb"], [pbk])
                V("scalar", lambda e, pb=pb, gi=gi, kc=kc: e.activation(out=Gtab[gi][:, kc * 128:(kc + 1) * 128], in_=pb[:, 0:128], func=AF.Copy), [pbk], [("Gtab", gi)])

        sb_main = sb
        sb = sbt
        are = sb("are", [128, 32], F32); aim = sb("aim", [128, 32], F32); ldt = sb("ldt", [128, 32], F32)
        bre = sb("bre", [128, 32, 16], F32); bim = sb("bim", [128, 32, 16], F32)
        tvec = sb("tvec", [128, SUB], F32)
        ti32 = sb("ti32", [128, 32 * SUB], I32)
        phi = tmpAB = sb("phi", [128, 32 * SUB], F32); phr = sb("phr", [128, 32 * SUB], F32); msk = sb("msk", [128, 32 * SUB], F32)
        load(are[:], are_d, "are"); load(aim[:], aim_d, "aim"); load(ldt[:], ldt_d, "ldt")
        load(bre[:], bre_d, "bre"); load(bim[:], bim_d, "bim"); load(tvec[:], tvec_d, "tvec")
        DT, ADT, TH, MAG, CS, SN, FRE, FIM, T1, T2, RC, RS = range(12)
        sv = lambda i: s5s[:, i, :]

        def sincos(out_ap, in_ap, n, shift, rk, wk):
            V("vector", lambda e: e.tensor_scalar(out=phr[:, 0:n], in0=in_ap, scalar1=shift + 64 * TWO_PI, scalar2=None, op0=ALU.add), rk, ["phr"])
            V("vector", lambda e: e.tensor_scalar(out=ti32[:, 0:n], in0=phr[:, 0:n], scalar1=1.0 / TWO_PI, scalar2=None, op0=ALU.mult), ["phr"], ["ti32"])
            V("vector", lambda e: e.tensor_copy(out=msk[:, 0:n], in_=ti32[:, 0:n]), ["ti32"], ["msk"])
            V("vector", lambda e: e.scalar_tensor_tensor(out=phr[:, 0:n], in0=msk[:, 0:n], scalar=-TWO_PI, in1=phr[:, 0:n], op0=ALU.mult, op1=ALU.add), ["msk", "phr"], ["phr"])
            V("vector", lambda e: e.tensor_single_scalar(out=msk[:, 0:n], in_=phr[:, 0:n], scalar=math.pi, op=ALU.is_gt), ["phr"], ["msk"])
            V("vector", lambda e: e.scalar_tensor_tensor(out=phr[:, 0:n], in0=msk[:, 0:n], scalar=-TWO_PI, in1=phr[:, 0:n], op0=ALU.mult, op1=ALU.add), ["msk", "phr"], ["phr"])
            V("vector", lambda e: e.tensor_single_scalar(out=msk[:, 0:n], in_=phr[:, 0:n], scalar=-math.pi, op=ALU.is_lt), ["phr"], ["msk"])
            V("vector", lambda e: e.scalar_tensor_tensor(out=phr[:, 0:n], in0=msk[:, 0:n], scalar=TWO_PI, in1=phr[:, 0:n], op0=ALU.mult, op1=ALU.add), ["msk", "phr"], ["phr"])
            V("scalar", lambda e: e.activation(out=out_ap, in_=phr[:, 0:n], func=AF.Sin), ["phr"], wk)

        V("scalar", lambda e: e.activation(out=sv(DT), in_=ldt[:], func=AF.Exp), ["ldt"], ["s5s"])
        V("vector", lambda e: e.tensor_tensor(out=sv(ADT), in0=are[:], in1=sv(DT), op=ALU.mult), ["are", "s5s"], ["s5s"])
        V("vector", lambda e: e.tensor_tensor(out=sv(TH), in0=aim[:], in1=sv(DT), op=ALU.mult), ["aim", "s5s"], ["s5s"])
        V("scalar", lambda e: e.activation(out=sv(MAG), in_=sv(ADT), func=AF.Exp), ["s5s"], ["s5s"])
        sincos(sv(SN), sv(TH), 32, 0.0, ["s5s"], ["s5s"])
        sincos(sv(CS), sv(TH), 32, math.pi / 2, ["s5s"], ["s5s"])
        V("vector", lambda e: e.tensor_tensor(out=sv(T1), in0=sv(MAG), in1=sv(CS), op=ALU.mult), ["s5s"], ["s5s"])
        V("vector", lambda e: e.tensor_tensor(out=sv(T2), in0=sv(MAG), in1=sv(SN), op=ALU.mult), ["s5s"], ["s5s"])
        V("vector", lambda e: e.tensor_scalar(out=sv(T1), in0=sv(T1), scalar1=-1.0, scalar2=None, op0=ALU.add), ["s5s"], ["s5s"])
        V("vector", lambda e: e.tensor_tensor(out=sv(RC), in0=are[:], in1=are[:], op=ALU.mult), ["are"], ["s5s"])
        V("vector", lambda e: e.tensor_tensor(out=sv(RS), in0=aim[:], in1=aim[:], op=ALU.mult), ["aim"], ["s5s"])
        V("vector", lambda e: e.tensor_tensor(out=sv(RC), in0=sv(RC), in1=sv(RS), op=ALU.add), ["s5s"], ["s5s"])
        V("vector", lambda e: e.reciprocal(out=sv(RS), in_=sv(RC)), ["s5s"], ["s5s"])
        V("vector", lambda e: e.tensor_tensor(out=sv(FRE), in0=sv(T1), in1=are[:], op=ALU.mult), ["s5s", "are"], ["s5s"])
        V("vector", lambda e: e.tensor_tensor(out=sv(RC), in0=sv(T2), in1=aim[:], op=ALU.mult), ["s5s", "aim"], ["s5s"])
        V("vector", lambda e: e.tensor_tensor(out=sv(FRE), in0=sv(FRE), in1=sv(RC), op=ALU.add), ["s5s"], ["s5s"])
        V("vector", lambda e: e.tensor_tensor(out=sv(FRE), in0=sv(FRE), in1=sv(RS), op=ALU.mult), ["s5s"], ["s5s"])
        V("vector", lambda e: e.tensor_tensor(out=sv(FIM), in0=sv(T2), in1=are[:], op=ALU.mult), ["s5s", "are"], ["s5s"])
        V("vector", lambda e: e.tensor_tensor(out=sv(RC), in0=sv(T1), in1=aim[:], op=ALU.mult), ["s5s", "aim"], ["s5s"])
        V("vector", lambda e: e.tensor_tensor(out=sv(FIM), in0=sv(FIM), in1=sv(RC), op=ALU.subtract), ["s5s"], ["s5s"])
        V("vector", lambda e: e.tensor_tensor(out=sv(FIM), in0=sv(FIM), in1=sv(RS), op=ALU.mult), ["s5s"], ["s5s"])
        bbr = sb("bbr", [128, 32, 16], F32); bbi = sb("bbi", [128, 32, 16], F32); bbt = sb("bbt", [128, 32, 16], F32)
        fre_b = s5s[:, FRE, :].unsqueeze(2).to_broadcast([128, 32, 16]); fim_b = s5s[:, FIM, :].unsqueeze(2).to_broadcast([128, 32, 16])
        V("vector", lambda e: e.tensor_tensor(out=bbr[:], in0=bre[:], in1=fre_b, op=ALU.mult), ["bre", "s5s"], ["bbr"])
        V("vector", lambda e: e.tensor_tensor(out=bbt[:], in0=bim[:], in1=fim_b, op=ALU.mult), ["bim", "s5s"], ["bbt"])
        V("vector", lambda e: e.tensor_tensor(out=bbr[:], in0=bbr[:], in1=bbt[:], op=ALU.subtract), ["bbr", "bbt"], ["bbr"])
        V("vector", lambda e: e.tensor_tensor(out=bbi[:], in0=bim[:], in1=fre_b, op=ALU.mult), ["bim", "s5s"], ["bbi"])
        V("vector", lambda e: e.tensor_tensor(out=bbt[:], in0=bre[:], in1=fim_b, op=ALU.mult), ["bre", "s5s", "bbr"], ["bbt"])
        V("vector", lambda e: e.tensor_tensor(out=bbi[:], in0=bbi[:], in1=bbt[:], op=ALU.add), ["bbi", "bbt"], ["bbi"])
        for (bb, BT, nm) in ((bbr, BTre, "BTre"), (bbi, BTim, "BTim")):
            bf = sb("bf_" + nm, [128, 32, 128], BF16)
            V("vector", lambda e, bf=bf: e.memset(bf[:], 0.0), [], ["bf" + nm])
            for g2 in range(2):
                for r in range(4):
                    def cp(e, bf=bf, bb=bb, g2=g2, r=r):
                        dst = bf[g2 * 64:(g2 + 1) * 64, :, r * 32 + g2 * 16: r * 32 + g2 * 16 + 16].rearrange("p (q r) h -> p q r h", r=4)[:, :, r, :]
                        src = bb[g2 * 64:(g2 + 1) * 64, :, :].rearrange("p (q r) h -> p q r h", r=4)[:, :, r, :]
                        return e.tensor_copy(out=dst, in_=src)
                    V("vector", cp, ["bbr", "bbi"], ["bf" + nm])
            for gp in range(32):
                tp_, tk = nxt("t")
                V("tensor", lambda e, tp_=tp_, bf=bf, gp=gp: e.transpose(tp_[:, 0:128], bf[:, gp, :], ident[:]), ["bf" + nm, "ident"], [tk])
                V("scalar", lambda e, tp_=tp_, BT=BT, gp=gp: e.activation(out=BT[:, gp, :], in_=tp_[:, 0:128], func=AF.Copy), [tk], [nm])
        phi3 = phi[:].rearrange("p (g t) -> p g t", t=SUB)
        V("vector", lambda e: e.tensor_tensor(out=phi3, in0=s5s[:, TH, :].unsqueeze(2).to_broadcast([128, 32, SUB]),
                                              in1=tvec[:].unsqueeze(1).to_broadcast([128, 32, SUB]), op=ALU.mult), ["s5s", "tvec"], ["phi"])
        sincos(stab[:].rearrange("p g t -> p (g t)"), phi[:], 32 * SUB, 0.0, ["phi"], ["stab"])
        sincos(ctab[:].rearrange("p g t -> p (g t)"), phi[:], 32 * SUB, math.pi / 2, ["phi"], ["ctab"])
        V("vector", lambda e: e.tensor_copy(out=rtab[:], in_=s5s[:, MAG, :].unsqueeze(2).to_broadcast([128, 32, SUB])), ["s5s"], ["rtab"])
        V("vector", lambda e: e.memset(rtab[:, :, 0:1], 0.0), ["rtab"], ["rtab"])
        V("vector", lambda e: e.tensor_tensor(out=sv(RC), in0=sv(MAG), in1=ctab[:, :, SUB - 1], op=ALU.mult), ["s5s", "ctab"], ["s5s"])
        V("vector", lambda e: e.tensor_tensor(out=sv(RS), in0=sv(MAG), in1=stab[:, :, SUB - 1], op=ALU.mult), ["s5s", "stab"], ["s5s"])

        es2.close()
        sb = sb_main
        S.barrier()
        xt = sb("xt", [128, 4, D], F32)
        hT = sb("hT", [128, 16, 512], BF16)
        ar = sb("arena", [128, 64, 512], BF16)
        XN0, V0, MG0 = 0, 0, 0
        QT0, QS0, UT0, YS0 = 16, 24, 16, 24
        KT0, KS0, SG0 = 32, 40, 32
        RT0, YB0 = 48, 48
        W50 = 32
        xn = ar[:, XN0:XN0 + 16, :].rearrange("p (c q) f -> p c (q f)", c=4)
        vtok = xn
        ysb = ar[:, YB0:YB0 + 16, :].rearrange("p (c q) f -> p c (q f)", c=4)
        ktil = ar[:, KS0:KS0 + 8, :].rearrange("p (c q) f -> p c (q f)", c=4)

        def norm_to_hT(acol, bcol):
            for c in range(4):
                V("scalar", lambda e, c=c: e.activation(out=xn[:, c, :], in_=xt[:, c, :], func=AF.Square, accum_out=st4[:, c:c + 1]),
                  [("xt", c)], blk(XN0 + 4 * c, 4) + ["st4"])
            V("vector", lambda e: e.tensor_scalar(out=st4[:, 4:8], in0=st4[:, 0:4], scalar1=1.0 / D, scalar2=1e-6, op0=ALU.mult, op1=ALU.add), ["st4"], ["st4"])
            V("scalar", lambda e: e.activation(out=st4[:, 4:8], in_=st4[:, 4:8], func=AF.Sqrt), ["st4"], ["st4"])
            V("vector", lambda e: e.reciprocal(out=st4[:, 8:12], in_=st4[:, 4:8]), ["st4"], ["st4"])
            for c in range(4):
                V("scalar", lambda e, c=c: e.activation(out=xn[:, c, :], in_=xt[:, c, :], func=AF.Identity, scale=st4[:, 8 + c:9 + c]),
                  [("xt", c), "st4"], blk(XN0 + 4 * c, 4))
            for kc in range(16):
                tp_, tk = nxt("t")

                def tr(e, tp_=tp_, kc=kc):
                    ins = None
                    for c in range(4):
                        ins = e.transpose(tp_[:, c * 128:(c + 1) * 128], xn[:, c, kc * 128:(kc + 1) * 128], ident[:])
                    ins = e.transpose(tp_[:, 384:512], xn[:, 3, kc * 128:(kc + 1) * 128], ident[:])
                    return ins
                V("tensor", tr, blk(XN0, 16) + ["ident"], [tk])
                V("scalar", lambda e, tp_=tp_, kc=kc: e.activation(out=hT[:, kc, :], in_=tp_[:, 0:512], func=AF.Identity,
                                                                    scale=AB[:, acol, kc:kc + 1], bias=AB[:, bcol, kc:kc + 1]), [tk, "AB"], [("hT", kc)])

        hT_keys = [("hT", kc) for kc in range(16)]

        def proj_fm(w2d, col0, ncols, KC, rhs_fn, rhs_keys, evac):
            for t0 in range(0, ncols, 256):
                wv, wk = wtile(w2d[:, col0 + t0: col0 + t0 + 256], KC, 256)
                for s in range(2):
                    bk, bkk = nxt("big")

                    def mm(e, wv=wv, s=s, bk=bk):
                        ins = None
                        for kc in range(KC):
                            ins = e.matmul(bk[:, 0:512], lhsT=wv[:, kc, s * 128:(s + 1) * 128], rhs=rhs_fn(kc), start=(kc == 0), stop=(kc == KC - 1))
                        return ins
                    V("tensor", mm, [wk] + rhs_keys, [bkk])
                    evac(bk, bkk, t0 // 128 + s)

        def proj_tm(w2d, col0, ncols, lhs_fn, lhs_keys, evac, KC=16):
            for t0 in range(0, ncols, 256):
                wv, wk = wtile(w2d[:, col0 + t0: col0 + t0 + 256], KC, 256)
                for c in range(4):
                    bk, bkk = nxt("big")

                    def mm(e, wv=wv, c=c, bk=bk):
                        ins = None
                        for kc in range(KC):
                            ins = e.matmul(bk[:, 0:256], lhsT=lhs_fn(kc, c), rhs=wv[:, kc, :], start=(kc == 0), stop=(kc == KC - 1))
                        return ins
                    V("tensor", mm, [wk] + lhs_keys, [bkk])
                    evac(bk, bkk, c, t0 // 256)

        def rope_evac(ci, si, dst_fn):
            cosT, sinT = ropet[ci], ropet[si]
            def ev(bk, bkk, c, ti):
                p4 = bk[:, 0:256].rearrange("p (h two d) -> p h two d", two=2, d=64)
                t1, t2 = p4[:, :, 0, :], p4[:, :, 1, :]
                cb = cosT[:, c, :].unsqueeze(1).to_broadcast([128, 2, 64]); sbb = sinT[:, c, :].unsqueeze(1).to_broadcast([128, 2, 64])
                A = tmpA[:, 0:128].rearrange("p (h d) -> p h d", d=64); B = tmpB[:, 0:128].rearrange("p (h d) -> p h d", d=64)
                o4 = tb1[:, 0:256].rearrange("p (h two d) -> p h two d", two=2, d=64)
                rk = [bkk, ("ropet", ci), ("ropet", si)]
                V("vector", lambda e: e.tensor_tensor(out=A, in0=t1, in1=cb, op=ALU.mult), rk, ["tmpA"])
                V("vector", lambda e: e.tensor_tensor(out=B, in0=t2, in1=sbb, op=ALU.mult), rk, ["tmpB"])
                V("vector", lambda e: e.tensor_tensor(out=o4[:, :, 0, :], in0=A, in1=B, op=ALU.subtract), ["tmpA", "tmpB"], ["tb1"])
                V("vector", lambda e: e.tensor_tensor(out=A, in0=t1, in1=sbb, op=ALU.mult), rk, ["tmpA"])
                V("vector", lambda e: e.tensor_tensor(out=B, in0=t2, in1=cb, op=ALU.mult), rk, ["tmpB"])
                V("vector", lambda e: e.tensor_tensor(out=o4[:, :, 1, :], in0=A, in1=B, op=ALU.add), ["tmpA", "tmpB"], ["tb1"])
                dst_fn(c, ti)
            return ev

        def s5_sub(ti_, sc, main):
            tc0 = sc * SUB
            WB = 32 * SUB // 512
            W = [ar[:, W50 + WB * i: W50 + WB * i + WB, :].rearrange("p a f -> p (a f)").rearrange("p (g t) -> p g t", t=SUB) for i in range(4)]
            Wk = [blk(W50 + WB * i, WB) for i in range(4)]
            for i in range(2):
                b0 = W50 + 4 * WB + 2 * WB * i
                W.append(ar[:, b0:b0 + 2 * WB, :].rearrange("p a f -> p (a f)").bitcast(F32).rearrange("p (g t) -> p g t", t=SUB))
                Wk.append(blk(b0, 2 * WB))
            bur, bui, ta, tb, wr_, wi_ = W
            for (BT, dst, dk, nm) in ((BTre, bur, Wk[0], "BTre"), (BTim, bui, Wk[1], "BTim")):
                for kc in range(8):
                    pb, pbk = nxt("aux")

                    def mm(e, pb=pb, BT=BT, kc=kc):
                        ins = None
                        for r in range(4):
                            ins = e.matmul(pb[:, r * SUB:(r + 1) * SUB], lhsT=BT[:, 4 * kc + r, :], rhs=ar[:, UT0 + kc, tc0:tc0 + SUB], start=True, stop=True)
                        return ins
                    V("tensor", mm, [nm, ("ar", UT0 + kc)], [pbk])
                    V("scalar", lambda e, pb=pb, dst=dst, kc=kc: e.activation(out=dst[:, 4 * kc:4 * kc + 4, :], in_=pb[:, 0:4 * SUB].rearrange("p (g t) -> p g t", t=SUB), func=AF.Copy), [pbk], dk)
            TT_ = lambda o, a, b, op, rk, wk: V("vector", lambda e: e.tensor_tensor(out=o, in0=a, in1=b, op=op), rk, wk)
            TT_(ta[:], ctab[:], bur[:], ALU.mult, ["ctab"] + Wk[0], Wk[2])
            TT_(tb[:], stab[:], bui[:], ALU.mult, ["stab"] + Wk[1], Wk[3])
            TT_(ta[:], ta[:], tb[:], ALU.add, Wk[2] + Wk[3], Wk[2])
            TT_(tb[:], ctab[:], bui[:], ALU.mult, ["ctab"] + Wk[1], Wk[3])
            TT_(wr_[:], stab[:], bur[:], ALU.mult, ["stab"] + Wk[0], Wk[4])
            TT_(tb[:], tb[:], wr_[:], ALU.subtract, Wk[3] + Wk[4], Wk[3])
            TT_(ta[:, :, 0], ta[:, :, 0], car[:, 0, :], ALU.add, Wk[2] + ["car"], Wk[2])
            TT_(tb[:, :, 0], tb[:, :, 0], car[:, 1, :], ALU.add, Wk[3] + ["car"], Wk[3])
            fl = lambda a: a.rearrange("p g t -> p (g t)")
            V("vector", lambda e: e.tensor_tensor_scan(out=fl(wr_[:]), data0=fl(rtab[:]), data1=fl(ta[:]), initial=0.0, op0=ALU.mult, op1=ALU.add), ["rtab"] + Wk[2], Wk[4])
            V("vector", lambda e: e.tensor_tensor_scan(out=fl(wi_[:]), data0=fl(rtab[:]), data1=fl(tb[:]), initial=0.0, op0=ALU.mult, op1=ALU.add), ["rtab"] + Wk[3], Wk[5])
            we_r, we_i = wr_[:, :, SUB - 1], wi_[:, :, SUB - 1]
            TT_(sv(T1), sv(RC), we_r, ALU.mult, ["s5s"] + Wk[4], ["s5t"])
            TT_(sv(T2), sv(RS), we_i, ALU.mult, ["s5s"] + Wk[5], ["s5t2"])
            TT_(car[:, 0, :], sv(T1), sv(T2), ALU.subtract, ["s5t", "s5t2"], ["car"])
            TT_(sv(T1), sv(RS), we_r, ALU.mult, ["s5s"] + Wk[4], ["s5t"])
            TT_(sv(T2), sv(RC), we_i, ALU.mult, ["s5s"] + Wk[5], ["s5t2"])
            TT_(car[:, 1, :], sv(T1), sv(T2), ALU.add, ["s5t", "s5t2"], ["car"])
            if not main:
                return
            TT_(bur[:], ctab[:], wr_[:], ALU.mult, ["ctab"] + Wk[4], Wk[0])
            TT_(bui[:], stab[:], wi_[:], ALU.mult, ["stab"] + Wk[5], Wk[1])
            TT_(ta[:], bur[:], bui[:], ALU.subtract, Wk[0] + Wk[1], Wk[2])
            TT_(bur[:], stab[:], wr_[:], ALU.mult, ["stab"] + Wk[4], Wk[0])
            TT_(bui[:], ctab[:], wi_[:], ALU.mult, ["ctab"] + Wk[5], Wk[1])
            V("vector", lambda e: e.scalar_tensor_tensor(out=tb[:], in0=bur[:], scalar=-1.0, in1=bui[:], op0=ALU.mult, op1=ALU.subtract), Wk[0] + Wk[1], Wk[3])
            pb, pbk = nxt("aux")

            def ym(e, pb=pb):
                ins = None
                for kc in range(8):
                    o = pb[:, kc * SUB:(kc + 1) * SUB]
                    for r in range(4):
                        gp = 4 * kc + r
                        e.matmul(o, lhsT=Cre[:, gp, :], rhs=ta[:, gp, :], start=(r == 0), stop=False)
                        e.matmul(o, lhsT=Cim[:, gp, :], rhs=tb[:, gp, :], start=False, stop=False)
                    ins = e.matmul(o, lhsT=Dfl[:, kc, :], rhs=ar[:, UT0 + kc, tc0:tc0 + SUB], start=False, stop=True)
                return ins
            V("tensor", ym, ["Cre", "Cim", "Dfl"] + Wk[2] + Wk[3] + blk(UT0, 8), [pbk])
            V("scalar", lambda e: e.activation(out=tmpA[:, 0:8 * SUB], in_=pb[:, 0:8 * SUB], func=AF.Square), [pbk], ["tmpA"])
            V("vector", lambda e: e.tensor_scalar(out=tmpA[:, 0:8 * SUB], in0=tmpA[:, 0:8 * SUB], scalar1=0.044715, scalar2=1.0, op0=ALU.mult, op1=ALU.add), ["tmpA"], ["tmpA"])
            V("vector", lambda e: e.tensor_tensor(out=tmpA[:, 0:8 * SUB], in0=tmpA[:, 0:8 * SUB], in1=pb[:, 0:8 * SUB], op=ALU.mult), ["tmpA", pbk], ["tmpA"])
            V("scalar", lambda e: e.activation(out=tmpB[:, 0:8 * SUB], in_=tmpA[:, 0:8 * SUB], func=AF.Sigmoid, scale=1.5957691216057308), ["tmpA"], ["tmpB"])
            V("vector", lambda e: e.tensor_tensor(out=ar[:, YS0:YS0 + 8, tc0:tc0 + SUB], in0=tmpB[:, 0:8 * SUB].rearrange("p (k t) -> p k t", t=SUB),
                                                  in1=pb[:, 0:8 * SUB].rearrange("p (k t) -> p k t", t=SUB), op=ALU.mult), ["tmpB", pbk], blk(YS0, 8))

        def post_norm_residual(gi, nparts):
            V("vector", lambda e: e.tensor_reduce(out=st4[:, 0:4], in_=ssy[:, :, 0:nparts], axis=mybir.AxisListType.X, op=ALU.add), ["ssy"], ["st4"])
            V("vector", lambda e: e.tensor_scalar(out=st4[:, 4:8], in0=st4[:, 0:4], scalar1=1.0 / D, scalar2=1e-6, op0=ALU.mult, op1=ALU.add), ["st4"], ["st4"])
            V("scalar", lambda e: e.activation(out=st4[:, 4:8], in_=st4[:, 4:8], func=AF.Sqrt), ["st4"], ["st4"])
            V("vector", lambda e: e.reciprocal(out=st4[:, 12:16], in_=st4[:, 4:8]), ["st4"], ["st4"])
            for c in range(4):
                V("vector", lambda e, c=c: e.scalar_tensor_tensor(out=ysb[:, c, :], in0=ysb[:, c, :], scalar=st4[:, 12 + c:13 + c], in1=Gtab[gi][:], op0=ALU.mult, op1=ALU.mult),
                  blk(YB0 + 4 * c, 4) + ["st4", ("Gtab", gi)], blk(YB0 + 4 * c, 4))
                V("vector", lambda e, c=c: e.tensor_tensor(out=xt[:, c, :], in0=xt[:, c, :], in1=ysb[:, c, :], op=ALU.add), blk(YB0 + 4 * c, 4) + [("xt", c)], [("xt", c)])

        def ytok_evac(bk, bkk, c, ti, width, col0, part):
            V("scalar", lambda e: e.activation(out=ysb[:, c, col0:col0 + width], in_=bk[:, 0:width], func=AF.Copy), [bkk], blk(YB0 + 4 * c, 4))
            V("scalar", lambda e: e.activation(out=tmpA[:, 0:width], in_=bk[:, 0:width], func=AF.Square, accum_out=ssy[:, c, part:part + 1]), [bkk], ["tmpA", "ssy"])

        def tile(ti_, main):
            src = x_d if main else xp_d
            gch0 = (16 if main else 0) + 4 * ti_
            for c in range(4):
                S.op("sync", lambda e, c=c: e.dma_start(out=xt[:, c, :], in_=src[(ti_ * 4 + c) * 128:(ti_ * 4 + c + 1) * 128, :]), writes=[("xt", c)], lane="x%d" % c)
            for i in range(4):
                S.op("sync", lambda e, i=i: e.dma_start(out=ropet[i][:], in_=rope_d[i][:, gch0:gch0 + 4, :]), writes=[("ropet", i)], lane="rp%d" % i)
            norm_to_hT(0, 1)
            hfn = lambda kc, c: hT[:, kc, c * 128:(c + 1) * 128]
            if main:
                def qdst(c, ti):
                    for hh in range(2):
                        h = 2 * ti + hh
                        tp_, tk = nxt("t")
                        V("tensor", lambda e, tp_=tp_, hh=hh: (e.transpose(tp_[:, 0:128], tb1[:, hh * 128:(hh + 1) * 128], ident[:]), e.transpose(tp_[:, 0:128], tb1[:, hh * 128:(hh + 1) * 128], ident[:]))[1], ["tb1", "ident"], [tk])
                        V("scalar", lambda e, tp_=tp_, h=h: e.activation(out=ar[:, QT0 + h, c * 128:(c + 1) * 128], in_=tp_[:, 0:128], func=AF.Copy), [tk], [("ar", QT0 + h)])
                        V("vector", lambda e, h=h: e.tensor_tensor(out=ar[:, QS0 + h, c * 128:(c + 1) * 128], in0=ar[:, QT0 + h, c * 128:(c + 1) * 128], in1=wcross[:, h, :], op=ALU.mult), [("ar", QT0 + h), "wcross"], [("ar", QS0 + h)])
                proj_tm(win_d, 0, 1024, hfn, hT_keys, rope_evac(0, 1, qdst))

            def kdst(c, ti):
                for hh in range(2):
                    h = 2 * ti + hh
                    if main:
                        tp_, tk = nxt("t")
                        V("tensor", lambda e, tp_=tp_, hh=hh: (e.transpose(tp_[:, 0:128], tb1[:, hh * 128:(hh + 1) * 128], ident[:]), e.transpose(tp_[:, 0:128], tb1[:, hh * 128:(hh + 1) * 128], ident[:]))[1], ["tb1", "ident"], [tk])
                        V("scalar", lambda e, tp_=tp_, h=h: e.activation(out=ar[:, KT0 + h, c * 128:(c + 1) * 128], in_=tp_[:, 0:128], func=AF.Copy), [tk], [("ar", KT0 + h)])
                    V("vector", lambda e, h=h, hh=hh: e.tensor_scalar(out=ktil[:, c, h * 128:(h + 1) * 128], in0=tb1[:, hh * 128:(hh + 1) * 128], scalar1=wstate[:, h:h + 1], scalar2=None, op0=ALU.mult),
                      ["tb1", "wstate"], blk(KS0 + 2 * c, 2))
            proj_tm(win_d, 1024, 1024, hfn, hT_keys, rope_evac(2, 3, kdst))
            if main and DBG and ti_ == 0:
                S.op("sync", lambda e: e.dma_start(out=dbg_q, in_=ar[:, QT0:QT0 + 16, :]), reads=blk(QT0, 16), lane="dbg6")
                S.op("sync", lambda e: e.dma_start(out=dbg_k, in_=ar[:, KT0:KT0 + 8, :]), reads=blk(KT0, 8), lane="dbg7")
            proj_tm(win_d, 2048, 2048, hfn, hT_keys,
                    lambda bk, bkk, c, ti: V("scalar", lambda e: e.activation(out=vtok[:, c, ti * 256:(ti + 1) * 256], in_=bk[:, 0:256], func=AF.Copy), [bkk], blk(V0 + 4 * c, 4)))
            for c in range(4):
                cs = slice(c * 128, (c + 1) * 128)
                for h in range(8):
                    if main:
                        pb, pbk = nxt("aux")
                        V("tensor", lambda e, pb=pb, h=h, cs=cs: e.matmul(pb[:, 0:128], lhsT=ar[:, KT0 + h, cs], rhs=ar[:, QT0 + h, cs], start=True, stop=True), [("ar", KT0 + h), ("ar", QT0 + h)], [pbk])
                        sT = sTb[h % 2]; sTk = ("sTb", h % 2)
                        V("vector", lambda e, pb=pb, h=h, sT=sT: e.tensor_tensor(out=sT[:], in0=pb[:, 0:128], in1=decayT[:, h, :], op=ALU.mult), [pbk, "decayT"], [sTk])
                        pr, prk = nxt("aux")

                        def rm(e, pr=pr, h=h, cs=cs, sT=sT, c=c):
                            ins = None
                            for vh in range(2):
                                o = pr[:, vh * 128:(vh + 1) * 128]
                                e.matmul(o, lhsT=vtok[:, c, h * 256 + vh * 128: h * 256 + (vh + 1) * 128], rhs=sT[:], start=True, stop=False)
                                ins = e.matmul(o, lhsT=Rb[:, h, vh * 128:(vh + 1) * 128], rhs=ar[:, QS0 + h, cs], start=False, stop=True)
                            return ins
                        V("tensor", rm, blk(V0 + 4 * c, 4) + [sTk, "Rb", ("ar", QS0 + h)], [prk])
                        V("scalar", lambda e, pr=pr: e.activation(out=tb1[:, 0:256], in_=pr[:, 0:256], func=AF.Copy), [prk], ["tb1"])
                        V("scalar", lambda e, pr=pr: e.activation(out=tb2[:, 0:256], in_=pr[:, 0:256], func=AF.Square), [prk], ["tb2"])
                        pq, pqk = nxt("aux")

                        def sm(e, pq=pq):
                            e.matmul(pq[:, 0:128], lhsT=onesg[:], rhs=tb1[:, 0:128], start=True, stop=False)
                            e.matmul(pq[:, 0:128], lhsT=onesg[:], rhs=tb1[:, 128:256], start=False, stop=True)
                            e.matmul(pq[:, 128:256], lhsT=onesg[:], rhs=tb2[:, 0:128], start=True, stop=False)
                            return e.matmul(pq[:, 128:256], lhsT=onesg[:], rhs=tb2[:, 128:256], start=False, stop=True)
                        V("tensor", sm, ["onesg", "tb1", "tb2"], [pqk])
                        mA, vA = tmpA[:, 0:128], tmpA[:, 128:256]
                        V("vector", lambda e, pq=pq: e.tensor_copy(out=tmpA[:, 0:256], in_=pq[:, 0:256]), [pqk], ["tmpA"])
                        V("vector", lambda e: e.tensor_tensor(out=tmpB[:, 0:128], in0=mA, in1=mA, op=ALU.mult), ["tmpA"], ["tmpB"])
                        V("vector", lambda e: e.tensor_tensor(out=vA, in0=vA, in1=tmpB[:, 0:128], op=ALU.subtract), ["tmpA", "tmpB"], ["tmpA"])
                        V("vector", lambda e: e.tensor_scalar(out=vA, in0=vA, scalar1=1e-5, scalar2=None, op0=ALU.add), ["tmpA"], ["tmpA"])
                        V("scalar", lambda e: e.activation(out=vA, in_=vA, func=AF.Sqrt), ["tmpA"], ["tmpA"])
                        V("vector", lambda e: e.reciprocal(out=tmpB[:, 0:128], in_=vA), ["tmpA"], ["tmpB"])
                        for vh in range(2):
                            V("vector", lambda e, pr=pr, vh=vh: e.tensor_tensor(out=tmpB[:, 128 + vh * 128:256 + vh * 128], in0=pr[:, vh * 128:(vh + 1) * 128], in1=mA, op=ALU.subtract), [prk, "tmpA"], ["tmpB"])
                            V("vector", lambda e, vh=vh, h=h, cs=cs: e.tensor_tensor(out=ar[:, RT0 + 2 * h + vh, cs], in0=tmpB[:, 128 + vh * 128:256 + vh * 128], in1=tmpB[:, 0:128], op=ALU.mult),
                              ["tmpB", "tmpB"], [("ar", RT0 + 2 * h + vh)])
                for h2 in range(4):
                    pk, pkk = nxt("aux")

                    def km(e, pk=pk, h2=h2, c=c):
                        ins = None
                        for hh in range(2):
                            h = 2 * h2 + hh
                            ins = e.matmul(pk[:, hh * 256:(hh + 1) * 256], lhsT=ktil[:, c, h * 128:(h + 1) * 128], rhs=vtok[:, c, h * 256:(h + 1) * 256], start=True, stop=True)
                        return ins
                    V("tensor", km, blk(KS0 + 2 * c, 2) + blk(V0 + 4 * c, 4), [pkk])
                    for hh in range(2):
                        h = 2 * h2 + hh
                        V("vector", lambda e, pk=pk, h=h, hh=hh: e.scalar_tensor_tensor(out=R[:, h, :], in0=R[:, h, :], scalar=cdec[:, h:h + 1], in1=pk[:, hh * 256:(hh + 1) * 256], op0=ALU.mult, op1=ALU.add),
                          [pkk, "R", "cdec"], ["R"])
                if main:
                    V("scalar", lambda e: e.activation(out=Rb[:], in_=R[:], func=AF.Copy), ["R"], ["Rb"])
            if main and DBG and ti_ == 0:
                S.op("sync", lambda e: e.dma_start(out=dbg_ret, in_=ar[:, RT0:RT0 + 16, :]), reads=blk(RT0, 16), lane="dbg0")
            if main:
                def gev(bk, bkk, j):
                    V("scalar", lambda e: e.activation(out=tb1[:], in_=bk[:, 0:512], func=AF.Sigmoid), [bkk], ["tb1"])
                    V("vector", lambda e: e.tensor_tensor(out=tb2[:], in0=bk[:, 0:512], in1=tb1[:], op=ALU.mult), [bkk, "tb1"], ["tb2"])
                    V("vector", lambda e: e.tensor_tensor(out=ar[:, RT0 + j, :], in0=ar[:, RT0 + j, :], in1=tb2[:], op=ALU.mult), ["tb2", ("ar", RT0 + j)], [("ar", RT0 + j)])
                proj_fm(win_d, 4096, 2048, 16, lambda kc: hT[:, kc, :], hT_keys, gev)
                proj_fm(win_d, 7168, 2048, 16, lambda kc: hT[:, kc, :], hT_keys,
                        lambda bk, bkk, j: V("scalar", lambda e: e.activation(out=ar[:, MG0 + j, :], in_=bk[:, 0:512], func=AF.Sigmoid), [bkk], [("ar", MG0 + j)]))
                proj_fm(wro_d, 0, 2048, 16, lambda kc: ar[:, RT0 + kc, :], blk(RT0, 16),
                        lambda bk, bkk, j: V("vector", lambda e: e.tensor_tensor(out=ar[:, MG0 + j, :], in0=ar[:, MG0 + j, :], in1=bk[:, 0:512], op=ALU.mult), [bkk, ("ar", MG0 + j)], [("ar", MG0 + j)]))
            proj_fm(win_d, 6144, 1024, 16, lambda kc: hT[:, kc, :], hT_keys,
                    lambda bk, bkk, j: V("scalar", lambda e: e.activation(out=ar[:, UT0 + j, :], in_=bk[:, 0:512], func=AF.Copy), [bkk], [("ar", UT0 + j)]))
            for sc in range(512 // SUB):
                s5_sub(ti_, sc, main)
            if not main:
                return
            if DBG and ti_ == 0:
                S.op("sync", lambda e: e.dma_start(out=dbg_ys, in_=ar[:, YS0:YS0 + 8, :]), reads=blk(YS0, 8), lane="dbg1")
            proj_fm(win_d, 9216, 2048, 16, lambda kc: hT[:, kc, :], hT_keys,
                    lambda bk, bkk, j: V("scalar", lambda e: e.activation(out=ar[:, SG0 + j, :], in_=bk[:, 0:512], func=AF.Sigmoid), [bkk], [("ar", SG0 + j)]))
            for t0 in range(0, 2048, 256):
                wa, wak = wtile(wglu_d[:, t0:t0 + 256], 8, 256)
                wb, wbk = wtile(wglu_d[:, 2048 + t0:2048 + t0 + 256], 8, 256)
                for s in range(2):
                    j = t0 // 128 + s
                    ba, bak = nxt("big"); bb_, bbk = nxt("big")

                    def mg(e, wa=wa, wb=wb, s=s, ba=ba, bb_=bb_):
                        ins = None
                        for kc in range(8):
                            e.matmul(ba[:, 0:512], lhsT=wa[:, kc, s * 128:(s + 1) * 128], rhs=ar[:, YS0 + kc, :], start=(kc == 0), stop=(kc == 7))
                        for kc in range(8):
                            ins = e.matmul(bb_[:, 0:512], lhsT=wb[:, kc, s * 128:(s + 1) * 128], rhs=ar[:, YS0 + kc, :], start=(kc == 0), stop=(kc == 7))
                        return ins
                    V("tensor", mg, [wak, wbk] + blk(YS0, 8), [bak, bbk])
                    V("scalar", lambda e, bb_=bb_: e.activation(out=tb1[:], in_=bb_[:, 0:512], func=AF.Sigmoid), [bbk], ["tb1"])
                    V("vector", lambda e, ba=ba: e.tensor_tensor(out=tb2[:], in0=ba[:, 0:512], in1=tb1[:], op=ALU.mult), [bak, "tb1"], ["tb2"])
                    V("vector", lambda e, j=j: e.tensor_tensor(out=tb2[:], in0=tb2[:], in1=ar[:, SG0 + j, :], op=ALU.mult), ["tb2", ("ar", SG0 + j)], ["tb2"])
                    V("vector", lambda e, j=j: e.tensor_tensor(out=ar[:, MG0 + j, :], in0=ar[:, MG0 + j, :], in1=tb2[:], op=ALU.add), ["tb2", ("ar", MG0 + j)], [("ar", MG0 + j)])
            if DBG and ti_ == 0:
                S.op("sync", lambda e: e.dma_start(out=dbg_mg, in_=ar[:, MG0:MG0 + 16, :]), reads=blk(MG0, 16), lane="dbg2")
            proj_tm(wout_d, 0, 2048, lambda kc, c: ar[:, MG0 + kc, c * 128:(c + 1) * 128], blk(MG0, 16),
                    lambda bk, bkk, c, ti: ytok_evac(bk, bkk, c, ti, 256, ti * 256, ti))
            post_norm_residual(0, 8)
            if DBG and ti_ == 0:
                S.op("sync", lambda e: e.dma_start(out=dbg_x1, in_=xt[:]), reads=[("xt", c_) for c_ in range(4)], lane="dbg3")
            norm_to_hT(2, 3)
            for t0 in range(0, DFF, 256):
                wa, wak = wtile(wfi_d[:, t0:t0 + 256], 16, 256)
                wb, wbk = wtile(wfi_d[:, DFF + t0:DFF + t0 + 256], 16, 256)
                for s in range(2):
                    j = t0 // 128 + s
                    ba, bak = nxt("big"); bb_, bbk = nxt("big")

                    def mf(e, wa=wa, wb=wb, s=s, ba=ba, bb_=bb_):
                        ins = None
                        for kc in range(16):
                            e.matmul(ba[:, 0:512], lhsT=wa[:, kc, s * 128:(s + 1) * 128], rhs=hT[:, kc, :], start=(kc == 0), stop=(kc == 15))
                        for kc in range(16):
                            ins = e.matmul(bb_[:, 0:512], lhsT=wb[:, kc, s * 128:(s + 1) * 128], rhs=hT[:, kc, :], start=(kc == 0), stop=(kc == 15))
                        return ins
                    V("tensor", mf, [wak, wbk] + hT_keys, [bak, bbk])
                    V("scalar", lambda e, ba=ba: e.activation(out=tb1[:], in_=ba[:, 0:512], func=AF.Sigmoid), [bak], ["tb1"])
                    V("vector", lambda e, ba=ba: e.tensor_tensor(out=tb2[:], in0=ba[:, 0:512], in1=tb1[:], op=ALU.mult), [bak, "tb1"], ["tb2"])
                    V("vector", lambda e, bb_=bb_, j=j: e.tensor_tensor(out=ar[:, j, :], in0=tb2[:], in1=bb_[:, 0:512], op=ALU.mult), ["tb2", bbk], [("ar", j)])
            for ns in range(4):
                banks = [nxt("big") for _ in range(4)]
                for kg in range(11):
                    wv, wk = wtile(wfo_d[kg * 512:(kg + 1) * 512, ns * 512:(ns + 1) * 512], 4, 512)
                    for c in range(4):
                        bk, bkk = banks[c]

                        def mo(e, wv=wv, bk=bk, c=c, kg=kg):
                            ins = None
                            for i in range(4):
                                ins = e.matmul(bk[:, 0:512], lhsT=ar[:, kg * 4 + i, c * 128:(c + 1) * 128], rhs=wv[:, i, :], start=(kg == 0 and i == 0), stop=(kg == 10 and i == 3))
                            return ins
                        V("tensor", mo, [wk] + blk(kg * 4, 4), [bkk])
                for c in range(4):
                    bk, bkk = banks[c]
                    ytok_evac(bk, bkk, c, 0, 512, ns * 512, ns)
            post_norm_residual(1, 4)
            for c in range(4):
                S.op("sync", lambda e, c=c: e.dma_start(out=out_d[(ti_ * 4 + c) * 128:(ti_ * 4 + c + 1) * 128, :], in_=xt[:, c, :]), reads=[("xt", c)], lane="o%d" % c)

        for tp in range(NT):
            tile(tp, False)
        V("vector", lambda e: e.tensor_scalar(out=R[:].rearrange("p h v -> p (h v)"), in0=R[:].rearrange("p h v -> p (h v)"), scalar1=flag[:, 0:1], scalar2=None, op0=ALU.mult), ["R", "flag"], ["R"])
        V("vector", lambda e: e.tensor_scalar(out=car[:].rearrange("p a g -> p (a g)"), in0=car[:].rearrange("p a g -> p (a g)"), scalar1=flag[:, 0:1], scalar2=None, op0=ALU.mult), ["car", "flag"], ["car"])
        V("scalar", lambda e: e.activation(out=Rb[:], in_=R[:], func=AF.Copy), ["R"], ["Rb"])
        for tm in range(NT):
            tile(tm, True)
        if DBG:
            S.op("sync", lambda e: e.dma_start(out=dbg_dec, in_=decayT[:]), reads=["decayT"], lane="dbg5")
        print("sbuf bytes remaining:", nc.sbuf_bytes_remaining)
        S.emit(nc)
    return nc


_CACHE = {}


def _consts():
    H, C, dk = 8, 128, 128
    log_g = np.log1p(-np.exp2(-5.0 - np.arange(H, dtype=np.float64)))
    idx = np.arange(C, dtype=np.float64)
    rel = idx[None, :] - idx[:, None]
    decayT = np.where(rel[:, None, :] >= 0, np.exp(np.maximum(rel, 0.0)[:, None, :] * log_g[None, :, None]), 0.0)
    wcross = np.broadcast_to(np.exp((idx + 1.0)[None, None, :] * log_g[None, :, None]), (128, H, C))
    wstate = np.exp((C - 1.0 - idx)[:, None] * log_g[None, :])
    cdec = np.broadcast_to(np.exp(C * log_g)[None, :], (128, H))
    pos = np.arange(4096, dtype=np.float32)
    inv_freq = (np.float32(10000.0) ** (-np.arange(64, dtype=np.float32) * np.float32(2.0 / 128))).astype(np.float32)
    ang = (pos[:, None] * inv_freq[None, :]).astype(np.float32)
    cos = np.cos(ang).astype(np.float32).reshape(32, 128, 64).transpose(1, 0, 2)
    sin = np.sin(ang).astype(np.float32).reshape(32, 128, 64).transpose(1, 0, 2)
    ks = np.float32(dk ** -0.5)
    f = lambda a: np.ascontiguousarray(a, dtype=np.float32)
    return dict(decayT=f(decayT), wcrossF=f(wcross), wstate=f(wstate), cdec=f(cdec), cos=f(cos), sin=f(sin), cosk=f(cos * ks), sink=f(sin * ks),
                ident=f(np.eye(128)), tvec=f(np.broadcast_to(np.arange(1, SUB + 1, dtype=np.float32)[None, :], (128, SUB))))


def kernel(x, c, w_ada, b_ada, norm_gains, w_in, w_ret_out, ssm_a_re, ssm_a_im, ssm_log_dt,
           ssm_b_re, ssm_b_im, ssm_c_re, ssm_c_im, ssm_d, w_s5_glu, w_out, w_ffn_in, w_ffn_out):
    f = lambda a: np.ascontiguousarray(np.asarray(a), dtype=np.float32)
    x = f(x); c = f(c)
    if "nc" not in _CACHE:
        _CACHE["nc"] = build_program()
        _CACHE["k"] = _consts()
    nc, K = _CACHE["nc"], _CACHE["k"]

    def gl(a):
        return f(np.asarray(a).reshape(32, 2, 64).transpose(1, 2, 0).reshape(128, 32))
    are, aim = gl(ssm_a_re[0]), gl(ssm_a_im[0])
    ldt = gl(np.broadcast_to(np.asarray(ssm_log_dt[0])[:, None], (64, 64)))
    bl = lambda a: f(np.asarray(a).reshape(32, 2, 64, 16).transpose(1, 2, 0, 3).reshape(128, 32, 16))
    bre, bim = bl(ssm_b_re[0]), bl(ssm_b_im[0])

    def cl(a):
        a = np.asarray(a).reshape(32, 2, 16, 64)
        o = np.zeros((2, 64, 32, 128), np.float32)
        for gp in range(32):
            for g2 in range(2):
                c0 = (gp % 4) * 32 + g2 * 16
                o[g2, :, gp, c0:c0 + 16] = a[gp, g2].T
        return o.reshape(128, 32, 128)
    cre, cim = cl(ssm_c_re[0]), cl(ssm_c_im[0])
    dfull = np.zeros((128, 8, 128), np.float32)
    dv = np.asarray(ssm_d[0]).reshape(8, 128)
    for kc in range(8):
        dfull[np.arange(128), kc, np.arange(128)] = dv[kc]
    gains = f(np.asarray(norm_gains[0]).reshape(4, 16, 128).transpose(2, 0, 1))
    badaT = f(np.asarray(b_ada[0]).reshape(96, 128).T)
    shared = dict(w_ada=f(w_ada[0]), badaT=badaT, gains=gains, w_in=f(w_in[0]), w_ret_out=f(w_ret_out[0]), w_s5_glu=f(w_s5_glu[0]),
                  w_out=f(w_out[0]), w_ffn_in=f(w_ffn_in[0]), w_ffn_out=f(w_ffn_out[0]),
                  decayT=K["decayT"], wcrossF=K["wcrossF"], wstate=K["wstate"], cdec=K["cdec"], ident=K["ident"], tvec=K["tvec"],
                  are=are, aim=aim, ldt=ldt, bre=bre, bim=bim, cre=cre, cim=cim, dfull=dfull)
    in_maps = []
    for core in range(8):
        b, half = core // 2, core % 2
        m = dict(shared)
        m["x"] = x[b, half * 2048:(half + 1) * 2048]
        m["xp"] = x[b, 0:2048]
        m["flag"] = np.full((128, 1), float(half), np.float32)
        m["cT"] = f(c[b].reshape(16, 128).T)
        for i, nm in enumerate(("cos", "sin", "cosk", "sink")):
            t = K[nm]
            r = np.empty((128, 32, 64), np.float32)
            r[:, 0:16] = t[:, 0:16]
            r[:, 16:32] = t[:, half * 16:(half + 1) * 16]
            m["rope%d" % i] = r
        in_maps.append(m)
    res = run_bass_kernel_spmd(nc, in_maps, core_ids=list(range(8)))
    _CACHE["res"] = res
    out = np.empty((4, 4096, D), np.float32)
    for core in range(8):
        b, half = core // 2, core % 2
        out[b, half * 2048:(half + 1) * 2048] = res.results[core]["out"]
    return out
```

```python
import contextlib
import os
import math
import numpy as np
import concourse.bass as bass
import concourse.mybir as mybir
from concourse.bass_utils import run_bass_kernel_spmd

F32 = mybir.dt.float32
BF16 = mybir.dt.bfloat16
I32 = mybir.dt.int32
ALU = mybir.AluOpType
AF = mybir.ActivationFunctionType
ENGS = ("tensor", "vector", "scalar", "gpsimd", "sync")

D = 2048
DFF = 5632
NT = 4
SUB = 32
NSLOT = 2
TWO_PI = float(2 * math.pi)


class Sched:
    def __init__(self):
        self.ops = []
        self.last_writer = {}
        self.readers = {}
        self.lane_last = {}
        self.bar = set()

    def barrier(self):
        last = {}
        for i, o in enumerate(self.ops):
            last[(o['eng'], o['lane'])] = i
        self.bar = set(last.values())

    def op(self, eng, fn, reads=(), writes=(), lane=None):
        idx = len(self.ops)
        deps = set(self.bar)
        for k in reads:
            w = self.last_writer.get(k)
            if w is not None:
                deps.add(w)
        for k in writes:
            w = self.last_writer.get(k)
            if w is not None:
                deps.add(w)
            for r in self.readers.get(k, ()):
                deps.add(r)
        if lane is not None:
            p = self.lane_last.get(lane)
            if p is not None:
                deps.add(p)
            self.lane_last[lane] = idx
        self.ops.append(dict(eng=eng, fn=fn, deps=sorted(deps), lane=lane))
        for k in reads:
            self.readers.setdefault(k, []).append(idx)
        for k in writes:
            self.last_writer[k] = idx
            self.readers[k] = []
        return idx

    def emit(self, nc, final_wait_eng="sync"):
        ops = self.ops
        pos = {}
        cnt = {e: 0 for e in ENGS}
        for i, o in enumerate(ops):
            pos[i] = cnt[o["eng"]]
            cnt[o["eng"]] += 1
        need = set()
        for i, o in enumerate(ops):
            for d in o["deps"]:
                do = ops[d]
                if do["lane"] is not None:
                    continue
                if do["eng"] == o["eng"]:
                    if o["eng"] == "tensor":
                        continue
                    if pos[i] - pos[d] > 3:
                        continue
                need.add(d)
        lanes = sorted(set(o["lane"] for o in ops if o["lane"] is not None), key=str)
        with contextlib.ExitStack() as es:
            esem = {e: es.enter_context(nc.semaphore("ms_" + e)) for e in ENGS}
            lsem = {l: es.enter_context(nc.semaphore("ln_%d" % j)) for j, l in enumerate(lanes)}
            block = es.enter_context(nc.Block())
            msno = {}
            mc = {e: 0 for e in ENGS}
            lane_no = {}
            lc = {l: 0 for l in lanes}
            for i, o in enumerate(ops):
                if o["lane"] is not None:
                    lc[o["lane"]] += 1
                    lane_no[i] = lc[o["lane"]]
                elif i in need:
                    mc[o["eng"]] += 1
                    msno[i] = mc[o["eng"]]

            def make(eng_name):
                def body(eng):
                    waited_e = {e: 0 for e in ENGS}
                    waited_l = {l: 0 for l in lanes}
                    for i, o in enumerate(ops):
                        if o["eng"] != eng_name:
                            continue
                        for d in o["deps"]:
                            do = ops[d]
                            if do["lane"] is not None:
                                v = lane_no[d] * 16
                                if waited_l[do["lane"]] < v:
                                    eng.wait_ge(lsem[do["lane"]], v)
                                    waited_l[do["lane"]] = v
                            elif d in msno:
                                v = msno[d]
                                if waited_e[do["eng"]] < v:
                                    eng.wait_ge(esem[do["eng"]], v)
                                    waited_e[do["eng"]] = v
                        ins = o["fn"](eng)
                        if o["lane"] is not None:
                            ins.then_inc(lsem[o["lane"]], 16)
                        elif i in msno:
                            ins.then_inc(esem[eng_name], 1)
                    if eng_name == final_wait_eng:
                        for l in lanes:
                            if lc[l] > 0:
                                eng.wait_ge(lsem[l], 16 * lc[l])
                return body

            for e in ENGS:
                if cnt[e] > 0 or e == final_wait_eng:
                    getattr(block, e)(make(e))


def build_program():
    nc = bass.Bass("TRN2", target_bir_lowering=False)
    S = Sched()

    def din(name, shape):
        return nc.dram_tensor(name, list(shape), F32, kind="ExternalInput").ap()

    x_d = din("x", [2048, D]); xp_d = din("xp", [2048, D]); flag_d = din("flag", [128, 1])
    cT_d = din("cT", [128, 16]); wada_d = din("w_ada", [D, 6 * D]); badaT_d = din("badaT", [128, 96])
    gains_d = din("gains", [128, 4, 16])
    win_d = din("w_in", [D, 11264]); wro_d = din("w_ret_out", [D, D]); wglu_d = din("w_s5_glu", [1024, 4096])
    wout_d = din("w_out", [D, D]); wfi_d = din("w_ffn_in", [D, 2 * DFF]); wfo_d = din("w_ffn_out", [DFF, D])
    rope_d = [din("rope%d" % i, [128, 32, 64]) for i in range(4)]
    decay_d = din("decayT", [128, 8, 128]); wcross_d = din("wcrossF", [128, 8, 128])
    wstate_d = din("wstate", [128, 8]); cdec_d = din("cdec", [128, 8]); ident_d = din("ident", [128, 128])
    are_d = din("are", [128, 32]); aim_d = din("aim", [128, 32]); ldt_d = din("ldt", [128, 32])
    bre_d = din("bre", [128, 32, 16]); bim_d = din("bim", [128, 32, 16])
    cre_d = din("cre", [128, 32, 128]); cim_d = din("cim", [128, 32, 128]); dfull_d = din("dfull", [128, 8, 128])
    tvec_d = din("tvec", [128, SUB])
    out_d = nc.dram_tensor("out", [2048, D], F32, kind="ExternalOutput").ap()
    DBG = bool(os.environ.get("KDBG"))
    if DBG:
        dbg_ret = nc.dram_tensor("dbg_ret", [128, 16, 512], BF16, kind="ExternalOutput").ap()
        dbg_ys = nc.dram_tensor("dbg_ys", [128, 8, 512], BF16, kind="ExternalOutput").ap()
        dbg_mg = nc.dram_tensor("dbg_mg", [128, 16, 512], BF16, kind="ExternalOutput").ap()
        dbg_x1 = nc.dram_tensor("dbg_x1", [128, 4, D], F32, kind="ExternalOutput").ap()
        dbg_q = nc.dram_tensor("dbg_q", [128, 16, 512], BF16, kind="ExternalOutput").ap()
        dbg_k = nc.dram_tensor("dbg_k", [128, 8, 512], BF16, kind="ExternalOutput").ap()
        dbg_dec = nc.dram_tensor("dbg_dec", [128, 8, 128], BF16, kind="ExternalOutput").ap()
        dbg_dec0 = nc.dram_tensor("dbg_dec0", [128, 8, 128], BF16, kind="ExternalOutput").ap()

    es = contextlib.ExitStack()
    with es:
        def sb(name, shape, dt):
            return es.enter_context(nc.sbuf_tensor("s_" + name, list(shape), dt))

        def ps(name, dt=F32):
            return es.enter_context(nc.psum_tensor(name, [128, 512 if dt == F32 else 1024], dt))

        wring = [sb("wr%d" % i, [128, 4096], BF16) for i in range(NSLOT)]
        es2 = contextlib.ExitStack()
        sbt = lambda name, shape, dt: es2.enter_context(nc.sbuf_tensor("s_" + name, list(shape), dt))
        R = sb("R", [128, 8, 256], F32); Rb = sb("Rb", [128, 8, 256], BF16)
        ropet = [sb("ropet%d" % i, [128, 4, 64], F32) for i in range(4)]
        decayT = sb("decayT", [128, 8, 128], BF16); wcross = sb("wcross", [128, 8, 128], BF16)
        wstate = sb("wstate", [128, 8], F32); cdec = sb("cdec", [128, 8], F32)
        ident = sb("ident", [128, 128], BF16); ones = sb("ones", [128, 128], BF16); onesg = sb("onesg", [128, 128], BF16)
        Gtab = [sb("Gtab%d" % i, [128, D], BF16) for i in range(2)]
        flag = sb("flag", [128, 1], F32)
        AB = sb("AB", [128, 6, 16], F32)
        st4 = sb("st4", [128, 16], F32)
        ssy = sb("ssy", [128, 4, 8], F32)
        tmpA = sb("tmpA", [128, 512], F32); tmpB = sb("tmpB", [128, 512], F32)
        tb1 = sb("tb1", [128, 512], BF16); tb2 = sb("tb2", [128, 512], BF16)
        sTb = [sb("sTb%d" % i, [128, 128], BF16) for i in range(2)]
        ctab = sb("ctab", [128, 32, SUB], BF16); stab = sb("stab", [128, 32, SUB], BF16); rtab = sb("rtab", [128, 32, SUB], F32)
        BTre = sb("BTre", [128, 32, 128], BF16); BTim = sb("BTim", [128, 32, 128], BF16)
        Cre = sb("Cre", [128, 32, 128], BF16); Cim = sb("Cim", [128, 32, 128], BF16); Dfl = sb("Dfl", [128, 8, 128], BF16)
        s5s = sb("s5s", [128, 12, 32], F32)
        car = sb("car", [128, 2, 32], F32)
        bigps = [ps("psb%d" % i) for i in range(4)]
        auxps = [ps("psa%d" % i) for i in range(2)]
        tps = [ps("pst%d" % i, BF16) for i in range(2)]
        cnt = dict(big=0, aux=0, aux6=0, t=0, w=0, ld=0, st=0)

        def nxt(pool):
            i = cnt[pool]; cnt[pool] += 1
            if pool == "big":
                return bigps[i % 4], ("psb", i % 4)
            if pool == "aux":
                return auxps[i % 2], ("psa", i % 2)
            if pool == "aux6":
                j = i % 6
                return (auxps[j], ("psa", j)) if j < 2 else (bigps[j - 2], ("psb", j - 2))
            return tps[i % 2], ("pst", i % 2)

        def blk(b0, n):
            return [("ar", b) for b in range(b0, b0 + n)]

        def V(eng, fn, reads, writes):
            S.op(eng, fn, reads=reads, writes=writes)

        def load(dst_ap, src_ap, key, cast=False):
            lane = "ld%d" % (cnt["ld"] % 8); cnt["ld"] += 1
            S.op("gpsimd" if cast else "sync", lambda e: e.dma_start(out=dst_ap, in_=src_ap), writes=[key], lane=lane)

        def wtile(src2d, KC, ncols):
            i = cnt["w"] % NSLOT; cnt["w"] += 1
            view = wring[i][:, 0:KC * ncols].rearrange("p (k n) -> p k n", n=ncols)
            S.op("gpsimd", lambda e: e.dma_start(out=view, in_=src2d.rearrange("(k p) n -> p k n", p=128)),
                 writes=[("wr", i)], lane="w%d" % i)
            return view, ("wr", i)

        cT = sbt("cTs", [128, 16], F32); csb = sbt("csb", [128, 16], BF16); sgc = sbt("sgc", [128, 16], F32)
        modT = sbt("modT", [128, 96], F32); badaT = sbt("badaTs", [128, 96], F32); gains = sbt("gains", [128, 4, 16], F32)
        diagb = sbt("diagb", [128, 128], BF16)
        load(flag[:], flag_d, "flag"); load(cT[:], cT_d, "cT"); load(badaT[:], badaT_d, "badaT"); load(gains[:], gains_d, "gains")
        load(decayT[:], decay_d, "decayT", True); load(wcross[:], wcross_d, "wcross", True)
        load(wstate[:], wstate_d, "wstate"); load(cdec[:], cdec_d, "cdec"); load(ident[:], ident_d, "ident", True)
        load(Cre[:], cre_d, "Cre", True); load(Cim[:], cim_d, "Cim", True); load(Dfl[:], dfull_d, "Dfl", True)
        if DBG:
            S.op("sync", lambda e: e.dma_start(out=dbg_dec0, in_=decayT[:]), reads=["decayT"], lane="dbg4")
        V("vector", lambda e: e.memset(ones[:], 1.0), [], ["ones"])
        V("vector", lambda e: e.memset(onesg[:], 1.0 / 256.0), [], ["onesg"])
        V("vector", lambda e: e.memset(R[:], 0.0), [], ["R"])
        V("vector", lambda e: e.memset(car[:], 0.0), [], ["car"])

        V("scalar", lambda e: e.activation(out=sgc[:], in_=cT[:], func=AF.Sigmoid), ["cT"], ["sgc"])
        V("vector", lambda e: e.tensor_tensor(out=csb[:], in0=cT[:], in1=sgc[:], op=ALU.mult), ["cT", "sgc"], ["csb"])
        pa, pak = auxps[0], ("psa", 0)
        for tI in range(48):
            wv, wk = wtile(wada_d[:, tI * 256:(tI + 1) * 256], 16, 256)

            def mm(e, wv=wv, tI=tI):
                ins = None
                for s in range(2):
                    j = tI * 2 + s
                    for kc in range(16):
                        ins = e.matmul(pa[:, j:j + 1], lhsT=wv[:, kc, s * 128:(s + 1) * 128], rhs=csb[:, kc:kc + 1],
                                       start=(kc == 0), stop=(kc == 15))
                return ins
            V("tensor", mm, [wk, "csb"], [pak])
        V("vector", lambda e: e.tensor_tensor(out=modT[:], in0=pa[:, 0:96], in1=badaT[:], op=ALU.add), [pak, "badaT"], ["modT"])
        V("vector", lambda e: e.scalar_tensor_tensor(out=AB[:, 0, :], in0=modT[:, 16:32], scalar=1.0, in1=gains[:, 0, :], op0=ALU.add, op1=ALU.mult), ["modT", "gains"], ["AB"])
        V("vector", lambda e: e.tensor_copy(out=AB[:, 1, :], in_=modT[:, 0:16]), ["modT"], ["AB"])
        V("vector", lambda e: e.scalar_tensor_tensor(out=AB[:, 2, :], in0=modT[:, 64:80], scalar=1.0, in1=gains[:, 2, :], op0=ALU.add, op1=ALU.mult), ["modT", "gains"], ["AB"])
        V("vector", lambda e: e.tensor_copy(out=AB[:, 3, :], in_=modT[:, 48:64]), ["modT"], ["AB"])
        V("vector", lambda e: e.tensor_tensor(out=AB[:, 4, :], in0=modT[:, 32:48], in1=gains[:, 1, :], op=ALU.mult), ["modT", "gains"], ["AB"])
        V("vector", lambda e: e.tensor_tensor(out=AB[:, 5, :], in0=modT[:, 80:96], in1=gains[:, 3, :], op=ALU.mult), ["modT", "gains"], ["AB"])
        for gi in range(2):
            for kc in range(16):
                V("vector", lambda e, gi=gi, kc=kc: e.tensor_scalar(out=diagb[:], in0=ident[:], scalar1=AB[:, 4 + gi, kc:kc + 1], scalar2=None, op0=ALU.mult), ["ident", "AB"], ["diagb"])
                pb, pbk = nxt("aux")
                V("tensor", lambda e, pb=pb: e.matmul(pb[:, 0:128], lhsT=ones[:], rhs=diagb[:], start=True, stop=True), ["ones", "diagb"], [pbk])
                V("scalar", lambda e, pb=pb, gi=gi, kc=kc: e.activation(out=Gtab[gi][:, kc * 128:(kc + 1) * 128], in_=pb[:, 0:128], func=AF.Copy), [pbk], [("Gtab", gi)])

        sb_main = sb
        sb = sbt
        are = sb("are", [128, 32], F32); aim = sb("aim", [128, 32], F32); ldt = sb("ldt", [128, 32], F32)
        bre = sb("bre", [128, 32, 16], F32); bim = sb("bim", [128, 32, 16], F32)
        tvec = sb("tvec", [128, SUB], F32)
        ti32 = sb("ti32", [128, 32 * SUB], I32)
        phi = tmpAB = sb("phi", [128, 32 * SUB], F32); phr = sb("phr", [128, 32 * SUB], F32); msk = sb("msk", [128, 32 * SUB], F32)
        load(are[:], are_d, "are"); load(aim[:], aim_d, "aim"); load(ldt[:], ldt_d, "ldt")
        load(bre[:], bre_d, "bre"); load(bim[:], bim_d, "bim"); load(tvec[:], tvec_d, "tvec")
        DT, ADT, TH, MAG, CS, SN, FRE, FIM, T1, T2, RC, RS = range(12)
        sv = lambda i: s5s[:, i, :]

        def sincos(out_ap, in_ap, n, shift, rk, wk):
            V("vector", lambda e: e.tensor_scalar(out=phr[:, 0:n], in0=in_ap, scalar1=shift + 64 * TWO_PI, scalar2=None, op0=ALU.add), rk, ["phr"])
            V("vector", lambda e: e.tensor_scalar(out=ti32[:, 0:n], in0=phr[:, 0:n], scalar1=1.0 / TWO_PI, scalar2=None, op0=ALU.mult), ["phr"], ["ti32"])
            V("vector", lambda e: e.tensor_copy(out=msk[:, 0:n], in_=ti32[:, 0:n]), ["ti32"], ["msk"])
            V("vector", lambda e: e.scalar_tensor_tensor(out=phr[:, 0:n], in0=msk[:, 0:n], scalar=-TWO_PI, in1=phr[:, 0:n], op0=ALU.mult, op1=ALU.add), ["msk", "phr"], ["phr"])
            V("vector", lambda e: e.tensor_single_scalar(out=msk[:, 0:n], in_=phr[:, 0:n], scalar=math.pi, op=ALU.is_gt), ["phr"], ["msk"])
            V("vector", lambda e: e.scalar_tensor_tensor(out=phr[:, 0:n], in0=msk[:, 0:n], scalar=-TWO_PI, in1=phr[:, 0:n], op0=ALU.mult, op1=ALU.add), ["msk", "phr"], ["phr"])
            V("vector", lambda e: e.tensor_single_scalar(out=msk[:, 0:n], in_=phr[:, 0:n], scalar=-math.pi, op=ALU.is_lt), ["phr"], ["msk"])
            V("vector", lambda e: e.scalar_tensor_tensor(out=phr[:, 0:n], in0=msk[:, 0:n], scalar=TWO_PI, in1=phr[:, 0:n], op0=ALU.mult, op1=ALU.add), ["msk", "phr"], ["phr"])
            V("scalar", lambda e: e.activation(out=out_ap, in_=phr[:, 0:n], func=AF.Sin), ["phr"], wk)

        V("scalar", lambda e: e.activation(out=sv(DT), in_=ldt[:], func=AF.Exp), ["ldt"], ["s5s"])
        V("vector", lambda e: e.tensor_tensor(out=sv(ADT), in0=are[:], in1=sv(DT), op=ALU.mult), ["are", "s5s"], ["s5s"])
        V("vector", lambda e: e.tensor_tensor(out=sv(TH), in0=aim[:], in1=sv(DT), op=ALU.mult), ["aim", "s5s"], ["s5s"])
        V("scalar", lambda e: e.activation(out=sv(MAG), in_=sv(ADT), func=AF.Exp), ["s5s"], ["s5s"])
        sincos(sv(SN), sv(TH), 32, 0.0, ["s5s"], ["s5s"])
        sincos(sv(CS), sv(TH), 32, math.pi / 2, ["s5s"], ["s5s"])
        V("vector", lambda e: e.tensor_tensor(out=sv(T1), in0=sv(MAG), in1=sv(CS), op=ALU.mult), ["s5s"], ["s5s"])
        V("vector", lambda e: e.tensor_tensor(out=sv(T2), in0=sv(MAG), in1=sv(SN), op=ALU.mult), ["s5s"], ["s5s"])
        V("vector", lambda e: e.tensor_scalar(out=sv(T1), in0=sv(T1), scalar1=-1.0, scalar2=None, op0=ALU.add), ["s5s"], ["s5s"])
        V("vector", lambda e: e.tensor_tensor(out=sv(RC), in0=are[:], in1=are[:], op=ALU.mult), ["are"], ["s5s"])
        V("vector", lambda e: e.tensor_tensor(out=sv(RS), in0=aim[:], in1=aim[:], op=ALU.mult), ["aim"], ["s5s"])
        V("vector", lambda e: e.tensor_tensor(out=sv(RC), in0=sv(RC), in1=sv(RS), op=ALU.add), ["s5s"], ["s5s"])
        V("vector", lambda e: e.reciprocal(out=sv(RS), in_=sv(RC)), ["s5s"], ["s5s"])
        V("vector", lambda e: e.tensor_tensor(out=sv(FRE), in0=sv(T1), in1=are[:], op=ALU.mult), ["s5s", "are"], ["s5s"])
        V("vector", lambda e: e.tensor_tensor(out=sv(RC), in0=sv(T2), in1=aim[:], op=ALU.mult), ["s5s", "aim"], ["s5s"])
        V("vector", lambda e: e.tensor_tensor(out=sv(FRE), in0=sv(FRE), in1=sv(RC), op=ALU.add), ["s5s"], ["s5s"])
        V("vector", lambda e: e.tensor_tensor(out=sv(FRE), in0=sv(FRE), in1=sv(RS), op=ALU.mult), ["s5s"], ["s5s"])
        V("vector", lambda e: e.tensor_tensor(out=sv(FIM), in0=sv(T2), in1=are[:], op=ALU.mult), ["s5s", "are"], ["s5s"])
        V("vector", lambda e: e.tensor_tensor(out=sv(RC), in0=sv(T1), in1=aim[:], op=ALU.mult), ["s5s", "aim"], ["s5s"])
        V("vector", lambda e: e.tensor_tensor(out=sv(FIM), in0=sv(FIM), in1=sv(RC), op=ALU.subtract), ["s5s"], ["s5s"])
        V("vector", lambda e: e.tensor_tensor(out=sv(FIM), in0=sv(FIM), in1=sv(RS), op=ALU.mult), ["s5s"], ["s5s"])
        bbr = sb("bbr", [128, 32, 16], F32); bbi = sb("bbi", [128, 32, 16], F32); bbt = sb("bbt", [128, 32, 16], F32)
        fre_b = s5s[:, FRE, :].unsqueeze(2).to_broadcast([128, 32, 16]); fim_b = s5s[:, FIM, :].unsqueeze(2).to_broadcast([128, 32, 16])
        V("vector", lambda e: e.tensor_tensor(out=bbr[:], in0=bre[:], in1=fre_b, op=ALU.mult), ["bre", "s5s"], ["bbr"])
        V("vector", lambda e: e.tensor_tensor(out=bbt[:], in0=bim[:], in1=fim_b, op=ALU.mult), ["bim", "s5s"], ["bbt"])
        V("vector", lambda e: e.tensor_tensor(out=bbr[:], in0=bbr[:], in1=bbt[:], op=ALU.subtract), ["bbr", "bbt"], ["bbr"])
        V("vector", lambda e: e.tensor_tensor(out=bbi[:], in0=bim[:], in1=fre_b, op=ALU.mult), ["bim", "s5s"], ["bbi"])
        V("vector", lambda e: e.tensor_tensor(out=bbt[:], in0=bre[:], in1=fim_b, op=ALU.mult), ["bre", "s5s", "bbr"], ["bbt"])
        V("vector", lambda e: e.tensor_tensor(out=bbi[:], in0=bbi[:], in1=bbt[:], op=ALU.add), ["bbi", "bbt"], ["bbi"])
        for (bb, BT, nm) in ((bbr, BTre, "BTre"), (bbi, BTim, "BTim")):
            bf = sb("bf_" + nm, [128, 32, 128], BF16)
            V("vector", lambda e, bf=bf: e.memset(bf[:], 0.0), [], ["bf" + nm])
            for g2 in range(2):
                for r in range(4):
                    def cp(e, bf=bf, bb=bb, g2=g2, r=r):
                        dst = bf[g2 * 64:(g2 + 1) * 64, :, r * 32 + g2 * 16: r * 32 + g2 * 16 + 16].rearrange("p (q r) h -> p q r h", r=4)[:, :, r, :]
                        src = bb[g2 * 64:(g2 + 1) * 64, :, :].rearrange("p (q r) h -> p q r h", r=4)[:, :, r, :]
                        return e.tensor_copy(out=dst, in_=src)
                    V("vector", cp, ["bbr", "bbi"], ["bf" + nm])
            for gp in range(32):
                tp_, tk = nxt("t")
                V("tensor", lambda e, tp_=tp_, bf=bf, gp=gp: e.transpose(tp_[:, 0:128], bf[:, gp, :], ident[:]), ["bf" + nm, "ident"], [tk])
                V("scalar", lambda e, tp_=tp_, BT=BT, gp=gp: e.activation(out=BT[:, gp, :], in_=tp_[:, 0:128], func=AF.Copy), [tk], [nm])
        phi3 = phi[:].rearrange("p (g t) -> p g t", t=SUB)
        V("vector", lambda e: e.tensor_tensor(out=phi3, in0=s5s[:, TH, :].unsqueeze(2).to_broadcast([128, 32, SUB]),
                                              in1=tvec[:].unsqueeze(1).to_broadcast([128, 32, SUB]), op=ALU.mult), ["s5s", "tvec"], ["phi"])
        sincos(stab[:].rearrange("p g t -> p (g t)"), phi[:], 32 * SUB, 0.0, ["phi"], ["stab"])
        sincos(ctab[:].rearrange("p g t -> p (g t)"), phi[:], 32 * SUB, math.pi / 2, ["phi"], ["ctab"])
        V("vector", lambda e: e.tensor_copy(out=rtab[:], in_=s5s[:, MAG, :].unsqueeze(2).to_broadcast([128, 32, SUB])), ["s5s"], ["rtab"])
        V("vector", lambda e: e.memset(rtab[:, :, 0:1], 0.0), ["rtab"], ["rtab"])
        V("vector", lambda e: e.tensor_tensor(out=sv(RC), in0=sv(MAG), in1=ctab[:, :, SUB - 1], op=ALU.mult), ["s5s", "ctab"], ["s5s"])
        V("vector", lambda e: e.tensor_tensor(out=sv(RS), in0=sv(MAG), in1=stab[:, :, SUB - 1], op=ALU.mult), ["s5s", "stab"], ["s5s"])

        es2.close()
        sb = sb_main
        S.barrier()
        xt = sb("xt", [128, 4, D], F32)
        hT = sb("hT", [128, 16, 512], BF16)
        ar = sb("arena", [128, 64, 512], BF16)
        XN0, V0, MG0 = 0, 0, 0
        QT0, QS0, UT0, YS0 = 16, 24, 16, 24
        KT0, KS0, SG0 = 32, 40, 32
        RT0, YB0 = 48, 48
        W50 = 32
        xn = ar[:, XN0:XN0 + 16, :].rearrange("p (c q) f -> p c (q f)", c=4)
        vtok = xn
        ysb = ar[:, YB0:YB0 + 16, :].rearrange("p (c q) f -> p c (q f)", c=4)
        ktil = ar[:, KS0:KS0 + 8, :].rearrange("p (c q) f -> p c (q f)", c=4)

        def norm_to_hT(acol, bcol):
            for c in range(4):
                V("scalar", lambda e, c=c: e.activation(out=xn[:, c, :], in_=xt[:, c, :], func=AF.Square, accum_out=st4[:, c:c + 1]),
                  [("xt", c)], blk(XN0 + 4 * c, 4) + ["st4"])
            V("vector", lambda e: e.tensor_scalar(out=st4[:, 4:8], in0=st4[:, 0:4], scalar1=1.0 / D, scalar2=1e-6, op0=ALU.mult, op1=ALU.add), ["st4"], ["st4"])
            V("scalar", lambda e: e.activation(out=st4[:, 4:8], in_=st4[:, 4:8], func=AF.Sqrt), ["st4"], ["st4"])
            V("vector", lambda e: e.reciprocal(out=st4[:, 8:12], in_=st4[:, 4:8]), ["st4"], ["st4"])
            for c in range(4):
                V("scalar", lambda e, c=c: e.activation(out=xn[:, c, :], in_=xt[:, c, :], func=AF.Identity, scale=st4[:, 8 + c:9 + c]),
                  [("xt", c), "st4"], blk(XN0 + 4 * c, 4))
            for kc in range(16):
                tp_, tk = nxt("t")

                def tr(e, tp_=tp_, kc=kc):
                    ins = None
                    for c in range(4):
                        ins = e.transpose(tp_[:, c * 128:(c + 1) * 128], xn[:, c, kc * 128:(kc + 1) * 128], ident[:])
                    ins = e.transpose(tp_[:, 384:512], xn[:, 3, kc * 128:(kc + 1) * 128], ident[:])
                    return ins
                V("tensor", tr, blk(XN0, 16) + ["ident"], [tk])
                V("scalar", lambda e, tp_=tp_, kc=kc: e.activation(out=hT[:, kc, :], in_=tp_[:, 0:512], func=AF.Identity,
                                                                    scale=AB[:, acol, kc:kc + 1], bias=AB[:, bcol, kc:kc + 1]), [tk, "AB"], [("hT", kc)])

        hT_keys = [("hT", kc) for kc in range(16)]

        def proj_fm(w2d, col0, ncols, KC, rhs_fn, rhs_keys, evac):
            for t0 in range(0, ncols, 256):
                wv, wk = wtile(w2d[:, col0 + t0: col0 + t0 + 256], KC, 256)
                for s in range(2):
                    bk, bkk = nxt("big")

                    def mm(e, wv=wv, s=s, bk=bk):
                        ins = None
                        for kc in range(KC):
                            ins = e.matmul(bk[:, 0:512], lhsT=wv[:, kc, s * 128:(s + 1) * 128], rhs=rhs_fn(kc), start=(kc == 0), stop=(kc == KC - 1))
                        return ins
                    V("tensor", mm, [wk] + rhs_keys, [bkk])
                    evac(bk, bkk, t0 // 128 + s)

        def proj_tm(w2d, col0, ncols, lhs_fn, lhs_keys, evac, KC=16):
            for t0 in range(0, ncols, 256):
                wv, wk = wtile(w2d[:, col0 + t0: col0 + t0 + 256], KC, 256)
                for c in range(4):
                    bk, bkk = nxt("big")

                    def mm(e, wv=wv, c=c, bk=bk):
                        ins = None
                        for kc in range(KC):
                            ins = e.matmul(bk[:, 0:256], lhsT=lhs_fn(kc, c), rhs=wv[:, kc, :], start=(kc == 0), stop=(kc == KC - 1))
                        return ins
                    V("tensor", mm, [wk] + lhs_keys, [bkk])
                    evac(bk, bkk, c, t0 // 256)

        def rope_evac(ci, si, dst_fn):
            cosT, sinT = ropet[ci], ropet[si]
            def ev(bk, bkk, c, ti):
                p4 = bk[:, 0:256].rearrange("p (h two d) -> p h two d", two=2, d=64)
                t1, t2 = p4[:, :, 0, :], p4[:, :, 1, :]
                cb = cosT[:, c, :].unsqueeze(1).to_broadcast([128, 2, 64]); sbb = sinT[:, c, :].unsqueeze(1).to_broadcast([128, 2, 64])
                A = tmpA[:, 0:128].rearrange("p (h d) -> p h d", d=64); B = tmpB[:, 0:128].rearrange("p (h d) -> p h d", d=64)
                o4 = tb1[:, 0:256].rearrange("p (h two d) -> p h two d", two=2, d=64)
                rk = [bkk, ("ropet", ci), ("ropet", si)]
                V("vector", lambda e: e.tensor_tensor(out=A, in0=t1, in1=cb, op=ALU.mult), rk, ["tmpA"])
                V("vector", lambda e: e.tensor_tensor(out=B, in0=t2, in1=sbb, op=ALU.mult), rk, ["tmpB"])
                V("vector", lambda e: e.tensor_tensor(out=o4[:, :, 0, :], in0=A, in1=B, op=ALU.subtract), ["tmpA", "tmpB"], ["tb1"])
                V("vector", lambda e: e.tensor_tensor(out=A, in0=t1, in1=sbb, op=ALU.mult), rk, ["tmpA"])
                V("vector", lambda e: e.tensor_tensor(out=B, in0=t2, in1=cb, op=ALU.mult), rk, ["tmpB"])
                V("vector", lambda e: e.tensor_tensor(out=o4[:, :, 1, :], in0=A, in1=B, op=ALU.add), ["tmpA", "tmpB"], ["tb1"])
                dst_fn(c, ti)
            return ev

        def s5_sub(ti_, sc, main):
            tc0 = sc * SUB
            WB = 32 * SUB // 512
            W = [ar[:, W50 + WB * i: W50 + WB * i + WB, :].rearrange("p a f -> p (a f)").rearrange("p (g t) -> p g t", t=SUB) for i in range(4)]
            Wk = [blk(W50 + WB * i, WB) for i in range(4)]
            for i in range(2):
                b0 = W50 + 4 * WB + 2 * WB * i
                W.append(ar[:, b0:b0 + 2 * WB, :].rearrange("p a f -> p (a f)").bitcast(F32).rearrange("p (g t) -> p g t", t=SUB))
                Wk.append(blk(b0, 2 * WB))
            bur, bui, ta, tb, wr_, wi_ = W
            for (BT, dst, dk, nm) in ((BTre, bur, Wk[0], "BTre"), (BTim, bui, Wk[1], "BTim")):
                for kc in range(8):
                    pb, pbk = nxt("aux6")

                    def mm(e, pb=pb, BT=BT, kc=kc):
                        ins = None
                        for r in range(4):
                            ins = e.matmul(pb[:, r * SUB:(r + 1) * SUB], lhsT=BT[:, 4 * kc + r, :], rhs=ar[:, UT0 + kc, tc0:tc0 + SUB], start=True, stop=True)
                        return ins
                    V("tensor", mm, [nm, ("ar", UT0 + kc)], [pbk])
                    V("scalar", lambda e, pb=pb, dst=dst, kc=kc: e.activation(out=dst[:, 4 * kc:4 * kc + 4, :], in_=pb[:, 0:4 * SUB].rearrange("p (g t) -> p g t", t=SUB), func=AF.Copy), [pbk], dk)
            TT_ = lambda o, a, b, op, rk, wk: V("vector", lambda e: e.tensor_tensor(out=o, in0=a, in1=b, op=op), rk, wk)
            TT_(ta[:], ctab[:], bur[:], ALU.mult, ["ctab"] + Wk[0], Wk[2])
            TT_(tb[:], stab[:], bui[:], ALU.mult, ["stab"] + Wk[1], Wk[3])
            TT_(ta[:], ta[:], tb[:], ALU.add, Wk[2] + Wk[3], Wk[2])
            TT_(tb[:], ctab[:], bui[:], ALU.mult, ["ctab"] + Wk[1], Wk[3])
            TT_(wr_[:], stab[:], bur[:], ALU.mult, ["stab"] + Wk[0], Wk[4])
            TT_(tb[:], tb[:], wr_[:], ALU.subtract, Wk[3] + Wk[4], Wk[3])
            TT_(ta[:, :, 0], ta[:, :, 0], car[:, 0, :], ALU.add, Wk[2] + ["car"], Wk[2])
            TT_(tb[:, :, 0], tb[:, :, 0], car[:, 1, :], ALU.add, Wk[3] + ["car"], Wk[3])
            fl = lambda a: a.rearrange("p g t -> p (g t)")
            V("vector", lambda e: e.tensor_tensor_scan(out=fl(wr_[:]), data0=fl(rtab[:]), data1=fl(ta[:]), initial=0.0, op0=ALU.mult, op1=ALU.add), ["rtab"] + Wk[2], Wk[4])
            V("vector", lambda e: e.tensor_tensor_scan(out=fl(wi_[:]), data0=fl(rtab[:]), data1=fl(tb[:]), initial=0.0, op0=ALU.mult, op1=ALU.add), ["rtab"] + Wk[3], Wk[5])
            we_r, we_i = wr_[:, :, SUB - 1], wi_[:, :, SUB - 1]
            TT_(sv(T1), sv(RC), we_r, ALU.mult, ["s5s"] + Wk[4], ["s5t"])
            TT_(sv(T2), sv(RS), we_i, ALU.mult, ["s5s"] + Wk[5], ["s5t2"])
            TT_(car[:, 0, :], sv(T1), sv(T2), ALU.subtract, ["s5t", "s5t2"], ["car"])
            TT_(sv(T1), sv(RS), we_r, ALU.mult, ["s5s"] + Wk[4], ["s5t"])
            TT_(sv(T2), sv(RC), we_i, ALU.mult, ["s5s"] + Wk[5], ["s5t2"])
            TT_(car[:, 1, :], sv(T1), sv(T2), ALU.add, ["s5t", "s5t2"], ["car"])
            if not main:
                return
            TT_(bur[:], ctab[:], wr_[:], ALU.mult, ["ctab"] + Wk[4], Wk[0])
            TT_(bui[:], stab[:], wi_[:], ALU.mult, ["stab"] + Wk[5], Wk[1])
            TT_(ta[:], bur[:], bui[:], ALU.subtract, Wk[0] + Wk[1], Wk[2])
            TT_(bur[:], stab[:], wr_[:], ALU.mult, ["stab"] + Wk[4], Wk[0])
            TT_(bui[:], ctab[:], wi_[:], ALU.mult, ["ctab"] + Wk[5], Wk[1])
            V("vector", lambda e: e.scalar_tensor_tensor(out=tb[:], in0=bur[:], scalar=-1.0, in1=bui[:], op0=ALU.mult, op1=ALU.subtract), Wk[0] + Wk[1], Wk[3])
            pb, pbk = nxt("aux6")

            def ym(e, pb=pb):
                ins = None
                for kc in range(8):
                    o = pb[:, kc * SUB:(kc + 1) * SUB]
                    for r in range(4):
                        gp = 4 * kc + r
                        e.matmul(o, lhsT=Cre[:, gp, :], rhs=ta[:, gp, :], start=(r == 0), stop=False)
                        e.matmul(o, lhsT=Cim[:, gp, :], rhs=tb[:, gp, :], start=False, stop=False)
                    ins = e.matmul(o, lhsT=Dfl[:, kc, :], rhs=ar[:, UT0 + kc, tc0:tc0 + SUB], start=False, stop=True)
                return ins
            V("tensor", ym, ["Cre", "Cim", "Dfl"] + Wk[2] + Wk[3] + blk(UT0, 8), [pbk])
            V("scalar", lambda e: e.activation(out=tmpA[:, 0:8 * SUB], in_=pb[:, 0:8 * SUB], func=AF.Square), [pbk], ["tmpA"])
            V("vector", lambda e: e.tensor_scalar(out=tmpA[:, 0:8 * SUB], in0=tmpA[:, 0:8 * SUB], scalar1=0.044715, scalar2=1.0, op0=ALU.mult, op1=ALU.add), ["tmpA"], ["tmpA"])
            V("vector", lambda e: e.tensor_tensor(out=tmpA[:, 0:8 * SUB], in0=tmpA[:, 0:8 * SUB], in1=pb[:, 0:8 * SUB], op=ALU.mult), ["tmpA", pbk], ["tmpA"])
            V("scalar", lambda e: e.activation(out=tmpB[:, 0:8 * SUB], in_=tmpA[:, 0:8 * SUB], func=AF.Sigmoid, scale=1.5957691216057308), ["tmpA"], ["tmpB"])
            V("vector", lambda e: e.tensor_tensor(out=ar[:, YS0:YS0 + 8, tc0:tc0 + SUB], in0=tmpB[:, 0:8 * SUB].rearrange("p (k t) -> p k t", t=SUB),
                                                  in1=pb[:, 0:8 * SUB].rearrange("p (k t) -> p k t", t=SUB), op=ALU.mult), ["tmpB", pbk], blk(YS0, 8))

        def post_norm_residual(gi, nparts):
            V("vector", lambda e: e.tensor_reduce(out=st4[:, 0:4], in_=ssy[:, :, 0:nparts], axis=mybir.AxisListType.X, op=ALU.add), ["ssy"], ["st4"])
            V("vector", lambda e: e.tensor_scalar(out=st4[:, 4:8], in0=st4[:, 0:4], scalar1=1.0 / D, scalar2=1e-6, op0=ALU.mult, op1=ALU.add), ["st4"], ["st4"])
            V("scalar", lambda e: e.activation(out=st4[:, 4:8], in_=st4[:, 4:8], func=AF.Sqrt), ["st4"], ["st4"])
            V("vector", lambda e: e.reciprocal(out=st4[:, 12:16], in_=st4[:, 4:8]), ["st4"], ["st4"])
            for c in range(4):
                V("vector", lambda e, c=c: e.scalar_tensor_tensor(out=ysb[:, c, :], in0=ysb[:, c, :], scalar=st4[:, 12 + c:13 + c], in1=Gtab[gi][:], op0=ALU.mult, op1=ALU.mult),
                  blk(YB0 + 4 * c, 4) + ["st4", ("Gtab", gi)], blk(YB0 + 4 * c, 4))
                V("vector", lambda e, c=c: e.tensor_tensor(out=xt[:, c, :], in0=xt[:, c, :], in1=ysb[:, c, :], op=ALU.add), blk(YB0 + 4 * c, 4) + [("xt", c)], [("xt", c)])

        def ytok_evac(bk, bkk, c, ti, width, col0, part):
            V("scalar", lambda e: e.activation(out=ysb[:, c, col0:col0 + width], in_=bk[:, 0:width], func=AF.Copy), [bkk], blk(YB0 + 4 * c, 4))
            V("scalar", lambda e: e.activation(out=tmpA[:, 0:width], in_=bk[:, 0:width], func=AF.Square, accum_out=ssy[:, c, part:part + 1]), [bkk], ["tmpA", "ssy"])

        def tile(ti_, main):
            src = x_d if main else xp_d
            gch0 = (16 if main else 0) + 4 * ti_
            for c in range(4):
                S.op("sync", lambda e, c=c: e.dma_start(out=xt[:, c, :], in_=src[(ti_ * 4 + c) * 128:(ti_ * 4 + c + 1) * 128, :]), writes=[("xt", c)], lane="x%d" % c)
            for i in range(4):
                S.op("sync", lambda e, i=i: e.dma_start(out=ropet[i][:], in_=rope_d[i][:, gch0:gch0 + 4, :]), writes=[("ropet", i)], lane="rp%d" % i)
            norm_to_hT(0, 1)
            hfn = lambda kc, c: hT[:, kc, c * 128:(c + 1) * 128]
            if main:
                def qdst(c, ti):
                    for hh in range(2):
                        h = 2 * ti + hh
                        tp_, tk = nxt("t")
                        V("tensor", lambda e, tp_=tp_, hh=hh: (e.transpose(tp_[:, 0:128], tb1[:, hh * 128:(hh + 1) * 128], ident[:]), e.transpose(tp_[:, 0:128], tb1[:, hh * 128:(hh + 1) * 128], ident[:]))[1], ["tb1", "ident"], [tk])
                        V("scalar", lambda e, tp_=tp_, h=h: e.activation(out=ar[:, QT0 + h, c * 128:(c + 1) * 128], in_=tp_[:, 0:128], func=AF.Copy), [tk], [("ar", QT0 + h)])
                        V("vector", lambda e, h=h: e.tensor_tensor(out=ar[:, QS0 + h, c * 128:(c + 1) * 128], in0=ar[:, QT0 + h, c * 128:(c + 1) * 128], in1=wcross[:, h, :], op=ALU.mult), [("ar", QT0 + h), "wcross"], [("ar", QS0 + h)])
                proj_tm(win_d, 0, 1024, hfn, hT_keys, rope_evac(0, 1, qdst))

            def kdst(c, ti):
                for hh in range(2):
                    h = 2 * ti + hh
                    if main:
                        tp_, tk = nxt("t")
                        V("tensor", lambda e, tp_=tp_, hh=hh: (e.transpose(tp_[:, 0:128], tb1[:, hh * 128:(hh + 1) * 128], ident[:]), e.transpose(tp_[:, 0:128], tb1[:, hh * 128:(hh + 1) * 128], ident[:]))[1], ["tb1", "ident"], [tk])
                        V("scalar", lambda e, tp_=tp_, h=h: e.activation(out=ar[:, KT0 + h, c * 128:(c + 1) * 128], in_=tp_[:, 0:128], func=AF.Copy), [tk], [("ar", KT0 + h)])
                    V("vector", lambda e, h=h, hh=hh: e.tensor_scalar(out=ktil[:, c, h * 128:(h + 1) * 128], in0=tb1[:, hh * 128:(hh + 1) * 128], scalar1=wstate[:, h:h + 1], scalar2=None, op0=ALU.mult),
                      ["tb1", "wstate"], blk(KS0 + 2 * c, 2))
            proj_tm(win_d, 1024, 1024, hfn, hT_keys, rope_evac(2, 3, kdst))
            if main and DBG and ti_ == 0:
                S.op("sync", lambda e: e.dma_start(out=dbg_q, in_=ar[:, QT0:QT0 + 16, :]), reads=blk(QT0, 16), lane="dbg6")
                S.op("sync", lambda e: e.dma_start(out=dbg_k, in_=ar[:, KT0:KT0 + 8, :]), reads=blk(KT0, 8), lane="dbg7")
            proj_tm(win_d, 2048, 2048, hfn, hT_keys,
                    lambda bk, bkk, c, ti: V("scalar", lambda e: e.activation(out=vtok[:, c, ti * 256:(ti + 1) * 256], in_=bk[:, 0:256], func=AF.Copy), [bkk], blk(V0 + 4 * c, 4)))
            for c in range(4):
                cs = slice(c * 128, (c + 1) * 128)
                for h in range(8):
                    if main:
                        pb, pbk = nxt("aux6")
                        V("tensor", lambda e, pb=pb, h=h, cs=cs: e.matmul(pb[:, 0:128], lhsT=ar[:, KT0 + h, cs], rhs=ar[:, QT0 + h, cs], start=True, stop=True), [("ar", KT0 + h), ("ar", QT0 + h)], [pbk])
                        sT = sTb[h % 2]; sTk = ("sTb", h % 2)
                        V("vector", lambda e, pb=pb, h=h, sT=sT: e.tensor_tensor(out=sT[:], in0=pb[:, 0:128], in1=decayT[:, h, :], op=ALU.mult), [pbk, "decayT"], [sTk])
                        pr, prk = nxt("aux6")

                        def rm(e, pr=pr, h=h, cs=cs, sT=sT, c=c):
                            ins = None
                            for vh in range(2):
                                o = pr[:, vh * 128:(vh + 1) * 128]
                                e.matmul(o, lhsT=vtok[:, c, h * 256 + vh * 128: h * 256 + (vh + 1) * 128], rhs=sT[:], start=True, stop=False)
                                ins = e.matmul(o, lhsT=Rb[:, h, vh * 128:(vh + 1) * 128], rhs=ar[:, QS0 + h, cs], start=False, stop=True)
                            return ins
                        V("tensor", rm, blk(V0 + 4 * c, 4) + [sTk, "Rb", ("ar", QS0 + h)], [prk])
                        V("scalar", lambda e, pr=pr: e.activation(out=tb1[:, 0:256], in_=pr[:, 0:256], func=AF.Copy), [prk], ["tb1"])
                        V("scalar", lambda e, pr=pr: e.activation(out=tb2[:, 0:256], in_=pr[:, 0:256], func=AF.Square), [prk], ["tb2"])
                        pq, pqk = nxt("aux6")

                        def sm(e, pq=pq):
                            e.matmul(pq[:, 0:128], lhsT=onesg[:], rhs=tb1[:, 0:128], start=True, stop=False)
                            e.matmul(pq[:, 0:128], lhsT=onesg[:], rhs=tb1[:, 128:256], start=False, stop=True)
                            e.matmul(pq[:, 128:256], lhsT=onesg[:], rhs=tb2[:, 0:128], start=True, stop=False)
                            return e.matmul(pq[:, 128:256], lhsT=onesg[:], rhs=tb2[:, 128:256], start=False, stop=True)
                        V("tensor", sm, ["onesg", "tb1", "tb2"], [pqk])
                        mA, vA = tmpA[:, 0:128], tmpA[:, 128:256]
                        V("vector", lambda e, pq=pq: e.tensor_copy(out=tmpA[:, 0:256], in_=pq[:, 0:256]), [pqk], ["tmpA"])
                        V("vector", lambda e: e.tensor_tensor(out=tmpB[:, 0:128], in0=mA, in1=mA, op=ALU.mult), ["tmpA"], ["tmpB"])
                        V("vector", lambda e: e.tensor_tensor(out=vA, in0=vA, in1=tmpB[:, 0:128], op=ALU.subtract), ["tmpA", "tmpB"], ["tmpA"])
                        V("vector", lambda e: e.tensor_scalar(out=vA, in0=vA, scalar1=1e-5, scalar2=None, op0=ALU.add), ["tmpA"], ["tmpA"])
                        V("scalar", lambda e: e.activation(out=vA, in_=vA, func=AF.Sqrt), ["tmpA"], ["tmpA"])
                        V("vector", lambda e: e.reciprocal(out=tmpB[:, 0:128], in_=vA), ["tmpA"], ["tmpB"])
                        for vh in range(2):
                            V("vector", lambda e, pr=pr, vh=vh: e.tensor_tensor(out=tmpB[:, 128 + vh * 128:256 + vh * 128], in0=pr[:, vh * 128:(vh + 1) * 128], in1=mA, op=ALU.subtract), [prk, "tmpA"], ["tmpB"])
                            V("vector", lambda e, vh=vh, h=h, cs=cs: e.tensor_tensor(out=ar[:, RT0 + 2 * h + vh, cs], in0=tmpB[:, 128 + vh * 128:256 + vh * 128], in1=tmpB[:, 0:128], op=ALU.mult),
                              ["tmpB", "tmpB"], [("ar", RT0 + 2 * h + vh)])
                for h2 in range(4):
                    pk, pkk = nxt("aux6")

                    def km(e, pk=pk, h2=h2, c=c):
                        ins = None
                        for hh in range(2):
                            h = 2 * h2 + hh
                            ins = e.matmul(pk[:, hh * 256:(hh + 1) * 256], lhsT=ktil[:, c, h * 128:(h + 1) * 128], rhs=vtok[:, c, h * 256:(h + 1) * 256], start=True, stop=True)
                        return ins
                    V("tensor", km, blk(KS0 + 2 * c, 2) + blk(V0 + 4 * c, 4), [pkk])
                    for hh in range(2):
                        h = 2 * h2 + hh
                        V("vector", lambda e, pk=pk, h=h, hh=hh: e.scalar_tensor_tensor(out=R[:, h, :], in0=R[:, h, :], scalar=cdec[:, h:h + 1], in1=pk[:, hh * 256:(hh + 1) * 256], op0=ALU.mult, op1=ALU.add),
                          [pkk, "R", "cdec"], ["R"])
                if main:
                    V("scalar", lambda e: e.activation(out=Rb[:], in_=R[:], func=AF.Copy), ["R"], ["Rb"])
            if main and DBG and ti_ == 0:
                S.op("sync", lambda e: e.dma_start(out=dbg_ret, in_=ar[:, RT0:RT0 + 16, :]), reads=blk(RT0, 16), lane="dbg0")
            if main:
                def gev(bk, bkk, j):
                    V("scalar", lambda e: e.activation(out=tb1[:], in_=bk[:, 0:512], func=AF.Sigmoid), [bkk], ["tb1"])
                    V("vector", lambda e: e.tensor_tensor(out=tb2[:], in0=bk[:, 0:512], in1=tb1[:], op=ALU.mult), [bkk, "tb1"], ["tb2"])
                    V("vector", lambda e: e.tensor_tensor(out=ar[:, RT0 + j, :], in0=ar[:, RT0 + j, :], in1=tb2[:], op=ALU.mult), ["tb2", ("ar", RT0 + j)], [("ar", RT0 + j)])
                proj_fm(win_d, 4096, 2048, 16, lambda kc: hT[:, kc, :], hT_keys, gev)
                proj_fm(win_d, 7168, 2048, 16, lambda kc: hT[:, kc, :], hT_keys,
                        lambda bk, bkk, j: V("scalar", lambda e: e.activation(out=ar[:, MG0 + j, :], in_=bk[:, 0:512], func=AF.Sigmoid), [bkk], [("ar", MG0 + j)]))
                proj_fm(wro_d, 0, 2048, 16, lambda kc: ar[:, RT0 + kc, :], blk(RT0, 16),
                        lambda bk, bkk, j: V("vector", lambda e: e.tensor_tensor(out=ar[:, MG0 + j, :], in0=ar[:, MG0 + j, :], in1=bk[:, 0:512], op=ALU.mult), [bkk, ("ar", MG0 + j)], [("ar", MG0 + j)]))
            proj_fm(win_d, 6144, 1024, 16, lambda kc: hT[:, kc, :], hT_keys,
                    lambda bk, bkk, j: V("scalar", lambda e: e.activation(out=ar[:, UT0 + j, :], in_=bk[:, 0:512], func=AF.Copy), [bkk], [("ar", UT0 + j)]))
            for sc in range(512 // SUB):
                s5_sub(ti_, sc, main)
            if not main:
                return
            if DBG and ti_ == 0:
                S.op("sync", lambda e: e.dma_start(out=dbg_ys, in_=ar[:, YS0:YS0 + 8, :]), reads=blk(YS0, 8), lane="dbg1")
            proj_fm(win_d, 9216, 2048, 16, lambda kc: hT[:, kc, :], hT_keys,
                    lambda bk, bkk, j: V("scalar", lambda e: e.activation(out=ar[:, SG0 + j, :], in_=bk[:, 0:512], func=AF.Sigmoid), [bkk], [("ar", SG0 + j)]))
            for t0 in range(0, 2048, 256):
                wa, wak = wtile(wglu_d[:, t0:t0 + 256], 8, 256)
                wb, wbk = wtile(wglu_d[:, 2048 + t0:2048 + t0 + 256], 8, 256)
                for s in range(2):
                    j = t0 // 128 + s
                    ba, bak = nxt("big"); bb_, bbk = nxt("big")

                    def mg(e, wa=wa, wb=wb, s=s, ba=ba, bb_=bb_):
                        ins = None
                        for kc in range(8):
                            e.matmul(ba[:, 0:512], lhsT=wa[:, kc, s * 128:(s + 1) * 128], rhs=ar[:, YS0 + kc, :], start=(kc == 0), stop=(kc == 7))
                        for kc in range(8):
                            ins = e.matmul(bb_[:, 0:512], lhsT=wb[:, kc, s * 128:(s + 1) * 128], rhs=ar[:, YS0 + kc, :], start=(kc == 0), stop=(kc == 7))
                        return ins
                    V("tensor", mg, [wak, wbk] + blk(YS0, 8), [bak, bbk])
                    V("scalar", lambda e, bb_=bb_: e.activation(out=tb1[:], in_=bb_[:, 0:512], func=AF.Sigmoid), [bbk], ["tb1"])
                    V("vector", lambda e, ba=ba: e.tensor_tensor(out=tb2[:], in0=ba[:, 0:512], in1=tb1[:], op=ALU.mult), [bak, "tb1"], ["tb2"])
                    V("vector", lambda e, j=j: e.tensor_tensor(out=tb2[:], in0=tb2[:], in1=ar[:, SG0 + j, :], op=ALU.mult), ["tb2", ("ar", SG0 + j)], ["tb2"])
                    V("vector", lambda e, j=j: e.tensor_tensor(out=ar[:, MG0 + j, :], in0=ar[:, MG0 + j, :], in1=tb2[:], op=ALU.add), ["tb2", ("ar", MG0 + j)], [("ar", MG0 + j)])
            if DBG and ti_ == 0:
                S.op("sync", lambda e: e.dma_start(out=dbg_mg, in_=ar[:, MG0:MG0 + 16, :]), reads=blk(MG0, 16), lane="dbg2")
            proj_tm(wout_d, 0, 2048, lambda kc, c: ar[:, MG0 + kc, c * 128:(c + 1) * 128], blk(MG0, 16),
                    lambda bk, bkk, c, ti: ytok_evac(bk, bkk, c, ti, 256, ti * 256, ti))
            post_norm_residual(0, 8)
            if DBG and ti_ == 0:
                S.op("sync", lambda e: e.dma_start(out=dbg_x1, in_=xt[:]), reads=[("xt", c_) for c_ in range(4)], lane="dbg3")
            norm_to_hT(2, 3)
            for t0 in range(0, DFF, 256):
                wa, wak = wtile(wfi_d[:, t0:t0 + 256], 16, 256)
                wb, wbk = wtile(wfi_d[:, DFF + t0:DFF + t0 + 256], 16, 256)
                for s in range(2):
                    j = t0 // 128 + s
                    ba, bak = nxt("big"); bb_, bbk = nxt("big")

                    def mf(e, wa=wa, wb=wb, s=s, ba=ba, bb_=bb_):
                        ins = None
                        for kc in range(16):
                            e.matmul(ba[:, 0:512], lhsT=wa[:, kc, s * 128:(s + 1) * 128], rhs=hT[:, kc, :], start=(kc == 0), stop=(kc == 15))
                        for kc in range(16):
                            ins = e.matmul(bb_[:, 0:512], lhsT=wb[:, kc, s * 128:(s + 1) * 128], rhs=hT[:, kc, :], start=(kc == 0), stop=(kc == 15))
                        return ins
                    V("tensor", mf, [wak, wbk] + hT_keys, [bak, bbk])
                    V("scalar", lambda e, ba=ba: e.activation(out=tb1[:], in_=ba[:, 0:512], func=AF.Sigmoid), [bak], ["tb1"])
                    V("vector", lambda e, ba=ba: e.tensor_tensor(out=tb2[:], in0=ba[:, 0:512], in1=tb1[:], op=ALU.mult), [bak, "tb1"], ["tb2"])
                    V("vector", lambda e, bb_=bb_, j=j: e.tensor_tensor(out=ar[:, j, :], in0=tb2[:], in1=bb_[:, 0:512], op=ALU.mult), ["tb2", bbk], [("ar", j)])
            for ns in range(4):
                banks = [nxt("big") for _ in range(4)]
                for kg in range(11):
                    wv, wk = wtile(wfo_d[kg * 512:(kg + 1) * 512, ns * 512:(ns + 1) * 512], 4, 512)
                    for c in range(4):
                        bk, bkk = banks[c]

                        def mo(e, wv=wv, bk=bk, c=c, kg=kg):
                            ins = None
                            for i in range(4):
                                ins = e.matmul(bk[:, 0:512], lhsT=ar[:, kg * 4 + i, c * 128:(c + 1) * 128], rhs=wv[:, i, :], start=(kg == 0 and i == 0), stop=(kg == 10 and i == 3))
                            return ins
                        V("tensor", mo, [wk] + blk(kg * 4, 4), [bkk])
                for c in range(4):
                    bk, bkk = banks[c]
                    ytok_evac(bk, bkk, c, 0, 512, ns * 512, ns)
            post_norm_residual(1, 4)
            for c in range(4):
                S.op("sync", lambda e, c=c: e.dma_start(out=out_d[(ti_ * 4 + c) * 128:(ti_ * 4 + c + 1) * 128, :], in_=xt[:, c, :]), reads=[("xt", c)], lane="o%d" % c)

        for tp in range(NT):
            tile(tp, False)
        V("vector", lambda e: e.tensor_scalar(out=R[:].rearrange("p h v -> p (h v)"), in0=R[:].rearrange("p h v -> p (h v)"), scalar1=flag[:, 0:1], scalar2=None, op0=ALU.mult), ["R", "flag"], ["R"])
        V("vector", lambda e: e.tensor_scalar(out=car[:].rearrange("p a g -> p (a g)"), in0=car[:].rearrange("p a g -> p (a g)"), scalar1=flag[:, 0:1], scalar2=None, op0=ALU.mult), ["car", "flag"], ["car"])
        V("scalar", lambda e: e.activation(out=Rb[:], in_=R[:], func=AF.Copy), ["R"], ["Rb"])
        for tm in range(NT):
            tile(tm, True)
        if DBG:
            S.op("sync", lambda e: e.dma_start(out=dbg_dec, in_=decayT[:]), reads=["decayT"], lane="dbg5")
        print("sbuf bytes remaining:", nc.sbuf_bytes_remaining)
        S.emit(nc)
    return nc


_CACHE = {}


def _consts():
    H, C, dk = 8, 128, 128
    log_g = np.log1p(-np.exp2(-5.0 - np.arange(H, dtype=np.float64)))
    idx = np.arange(C, dtype=np.float64)
    rel = idx[None, :] - idx[:, None]
    decayT = np.where(rel[:, None, :] >= 0, np.exp(np.maximum(rel, 0.0)[:, None, :] * log_g[None, :, None]), 0.0)
    wcross = np.broadcast_to(np.exp((idx + 1.0)[None, None, :] * log_g[None, :, None]), (128, H, C))
    wstate = np.exp((C - 1.0 - idx)[:, None] * log_g[None, :])
    cdec = np.broadcast_to(np.exp(C * log_g)[None, :], (128, H))
    pos = np.arange(4096, dtype=np.float32)
    inv_freq = (np.float32(10000.0) ** (-np.arange(64, dtype=np.float32) * np.float32(2.0 / 128))).astype(np.float32)
    ang = (pos[:, None] * inv_freq[None, :]).astype(np.float32)
    cos = np.cos(ang).astype(np.float32).reshape(32, 128, 64).transpose(1, 0, 2)
    sin = np.sin(ang).astype(np.float32).reshape(32, 128, 64).transpose(1, 0, 2)
    ks = np.float32(dk ** -0.5)
    f = lambda a: np.ascontiguousarray(a, dtype=np.float32)
    return dict(decayT=f(decayT), wcrossF=f(wcross), wstate=f(wstate), cdec=f(cdec), cos=f(cos), sin=f(sin), cosk=f(cos * ks), sink=f(sin * ks),
                ident=f(np.eye(128)), tvec=f(np.broadcast_to(np.arange(1, SUB + 1, dtype=np.float32)[None, :], (128, SUB))))


def kernel(x, c, w_ada, b_ada, norm_gains, w_in, w_ret_out, ssm_a_re, ssm_a_im, ssm_log_dt,
           ssm_b_re, ssm_b_im, ssm_c_re, ssm_c_im, ssm_d, w_s5_glu, w_out, w_ffn_in, w_ffn_out):
    f = lambda a: np.ascontiguousarray(np.asarray(a), dtype=np.float32)
    x = f(x); c = f(c)
    if "nc" not in _CACHE:
        _CACHE["nc"] = build_program()
        _CACHE["k"] = _consts()
    nc, K = _CACHE["nc"], _CACHE["k"]

    def gl(a):
        return f(np.asarray(a).reshape(32, 2, 64).transpose(1, 2, 0).reshape(128, 32))
    are, aim = gl(ssm_a_re[0]), gl(ssm_a_im[0])
    ldt = gl(np.broadcast_to(np.asarray(ssm_log_dt[0])[:, None], (64, 64)))
    bl = lambda a: f(np.asarray(a).reshape(32, 2, 64, 16).transpose(1, 2, 0, 3).reshape(128, 32, 16))
    bre, bim = bl(ssm_b_re[0]), bl(ssm_b_im[0])

    def cl(a):
        a = np.asarray(a).reshape(32, 2, 16, 64)
        o = np.zeros((2, 64, 32, 128), np.float32)
        for gp in range(32):
            for g2 in range(2):
                c0 = (gp % 4) * 32 + g2 * 16
                o[g2, :, gp, c0:c0 + 16] = a[gp, g2].T
        return o.reshape(128, 32, 128)
    cre, cim = cl(ssm_c_re[0]), cl(ssm_c_im[0])
    dfull = np.zeros((128, 8, 128), np.float32)
    dv = np.asarray(ssm_d[0]).reshape(8, 128)
    for kc in range(8):
        dfull[np.arange(128), kc, np.arange(128)] = dv[kc]
    gains = f(np.asarray(norm_gains[0]).reshape(4, 16, 128).transpose(2, 0, 1))
    badaT = f(np.asarray(b_ada[0]).reshape(96, 128).T)
    shared = dict(w_ada=f(w_ada[0]), badaT=badaT, gains=gains, w_in=f(w_in[0]), w_ret_out=f(w_ret_out[0]), w_s5_glu=f(w_s5_glu[0]),
                  w_out=f(w_out[0]), w_ffn_in=f(w_ffn_in[0]), w_ffn_out=f(w_ffn_out[0]),
                  decayT=K["decayT"], wcrossF=K["wcrossF"], wstate=K["wstate"], cdec=K["cdec"], ident=K["ident"], tvec=K["tvec"],
                  are=are, aim=aim, ldt=ldt, bre=bre, bim=bim, cre=cre, cim=cim, dfull=dfull)
    in_maps = []
    for core in range(8):
        b, half = core // 2, core % 2
        m = dict(shared)
        m["x"] = x[b, half * 2048:(half + 1) * 2048]
        m["xp"] = x[b, 0:2048]
        m["flag"] = np.full((128, 1), float(half), np.float32)
        m["cT"] = f(c[b].reshape(16, 128).T)
        for i, nm in enumerate(("cos", "sin", "cosk", "sink")):
            t = K[nm]
            r = np.empty((128, 32, 64), np.float32)
            r[:, 0:16] = t[:, 0:16]
            r[:, 16:32] = t[:, half * 16:(half + 1) * 16]
            m["rope%d" % i] = r
        in_maps.append(m)
    res = run_bass_kernel_spmd(nc, in_maps, core_ids=list(range(8)))
    _CACHE["res"] = res
    out = np.empty((4, 4096, D), np.float32)
    for core in range(8):
        b, half = core // 2, core % 2
        out[b, half * 2048:(half + 1) * 2048] = res.results[core]["out"]
    return out
```

```python
import contextlib
import os
import math
import numpy as np
import concourse.bass as bass
import concourse.mybir as mybir
from concourse.bass_utils import run_bass_kernel_spmd

F32 = mybir.dt.float32
BF16 = mybir.dt.bfloat16
I32 = mybir.dt.int32
ALU = mybir.AluOpType
AF = mybir.ActivationFunctionType
ENGS = ("tensor", "vector", "scalar", "gpsimd", "sync")

D = 2048
DFF = 5632
NT = 4
SUB = 32
NSLOT = 2
TWO_PI = float(2 * math.pi)


class Sched:
    def __init__(self):
        self.ops = []
        self.last_writer = {}
        self.readers = {}
        self.lane_last = {}
        self.bar = set()

    def barrier(self):
        last = {}
        for i, o in enumerate(self.ops):
            last[(o['eng'], o['lane'])] = i
        self.bar = set(last.values())

    def op(self, eng, fn, reads=(), writes=(), lane=None):
        idx = len(self.ops)
        deps = set(self.bar)
        for k in reads:
            w = self.last_writer.get(k)
            if w is not None:
                deps.add(w)
        for k in writes:
            w = self.last_writer.get(k)
            if w is not None:
                deps.add(w)
            for r in self.readers.get(k, ()):
                deps.add(r)
        if lane is not None:
            p = self.lane_last.get(lane)
            if p is not None:
                deps.add(p)
            self.lane_last[lane] = idx
        self.ops.append(dict(eng=eng, fn=fn, deps=sorted(deps), lane=lane))
        for k in reads:
            self.readers.setdefault(k, []).append(idx)
        for k in writes:
            self.last_writer[k] = idx
            self.readers[k] = []
        return idx

    def emit(self, nc, final_wait_eng="sync"):
        ops = self.ops
        pos = {}
        cnt = {e: 0 for e in ENGS}
        for i, o in enumerate(ops):
            pos[i] = cnt[o["eng"]]
            cnt[o["eng"]] += 1
        need = set()
        for i, o in enumerate(ops):
            for d in o["deps"]:
                do = ops[d]
                if do["lane"] is not None:
                    continue
                if do["eng"] == o["eng"]:
                    if o["eng"] == "tensor":
                        continue
                    if pos[i] - pos[d] > 3:
                        continue
                need.add(d)
        lanes = sorted(set(o["lane"] for o in ops if o["lane"] is not None), key=str)
        with contextlib.ExitStack() as es:
            esem = {e: es.enter_context(nc.semaphore("ms_" + e)) for e in ENGS}
            lsem = {l: es.enter_context(nc.semaphore("ln_%d" % j)) for j, l in enumerate(lanes)}
            block = es.enter_context(nc.Block())
            msno = {}
            mc = {e: 0 for e in ENGS}
            lane_no = {}
            lc = {l: 0 for l in lanes}
            for i, o in enumerate(ops):
                if o["lane"] is not None:
                    lc[o["lane"]] += 1
                    lane_no[i] = lc[o["lane"]]
                elif i in need:
                    mc[o["eng"]] += 1
                    msno[i] = mc[o["eng"]]

            def make(eng_name):
                def body(eng):
                    waited_e = {e: 0 for e in ENGS}
                    waited_l = {l: 0 for l in lanes}
                    for i, o in enumerate(ops):
                        if o["eng"] != eng_name:
                            continue
                        for d in o["deps"]:
                            do = ops[d]
                            if do["lane"] is not None:
                                v = lane_no[d] * 16
                                if waited_l[do["lane"]] < v:
                                    eng.wait_ge(lsem[do["lane"]], v)
                                    waited_l[do["lane"]] = v
                            elif d in msno:
                                v = msno[d]
                                if waited_e[do["eng"]] < v:
                                    eng.wait_ge(esem[do["eng"]], v)
                                    waited_e[do["eng"]] = v
                        ins = o["fn"](eng)
                        if o["lane"] is not None:
                            ins.then_inc(lsem[o["lane"]], 16)
                        elif i in msno:
                            ins.then_inc(esem[eng_name], 1)
                    if eng_name == final_wait_eng:
                        for l in lanes:
                            if lc[l] > 0:
                                eng.wait_ge(lsem[l], 16 * lc[l])
                return body

            for e in ENGS:
                if cnt[e] > 0 or e == final_wait_eng:
                    getattr(block, e)(make(e))


def build_program():
    nc = bass.Bass("TRN2", target_bir_lowering=False)
    S = Sched()

    def din(name, shape):
        return nc.dram_tensor(name, list(shape), F32, kind="ExternalInput").ap()

    x_d = din("x", [2048, D]); xp_d = din("xp", [2048, D]); flag_d = din("flag", [128, 1])
    cT_d = din("cT", [128, 16]); wada_d = din("w_ada", [D, 6 * D]); badaT_d = din("badaT", [128, 96])
    gains_d = din("gains", [128, 4, 16])
    win_d = din("w_in", [D, 11264]); wro_d = din("w_ret_out", [D, D]); wglu_d = din("w_s5_glu", [1024, 4096])
    wout_d = din("w_out", [D, D]); wfi_d = din("w_ffn_in", [D, 2 * DFF]); wfo_d = din("w_ffn_out", [DFF, D])
    rope_d = [din("rope%d" % i, [128, 32, 64]) for i in range(4)]
    decay_d = din("decayT", [128, 8, 128]); wcross_d = din("wcrossF", [128, 8, 128])
    wstate_d = din("wstate", [128, 8]); cdec_d = din("cdec", [128, 8]); ident_d = din("ident", [128, 128])
    are_d = din("are", [128, 32]); aim_d = din("aim", [128, 32]); ldt_d = din("ldt", [128, 32])
    bre_d = din("bre", [128, 32, 16]); bim_d = din("bim", [128, 32, 16])
    cre_d = din("cre", [128, 32, 128]); cim_d = din("cim", [128, 32, 128]); dfull_d = din("dfull", [128, 8, 128])
    tvec_d = din("tvec", [128, SUB])
    out_d = nc.dram_tensor("out", [2048, D], F32, kind="ExternalOutput").ap()
    DBG = bool(os.environ.get("KDBG"))
    if DBG:
        dbg_ret = nc.dram_tensor("dbg_ret", [128, 16, 512], BF16, kind="ExternalOutput").ap()
        dbg_ys = nc.dram_tensor("dbg_ys", [128, 8, 512], BF16, kind="ExternalOutput").ap()
        dbg_mg = nc.dram_tensor("dbg_mg", [128, 16, 512], BF16, kind="ExternalOutput").ap()
        dbg_x1 = nc.dram_tensor("dbg_x1", [128, 4, D], F32, kind="ExternalOutput").ap()
        dbg_q = nc.dram_tensor("dbg_q", [128, 16, 512], BF16, kind="ExternalOutput").ap()
        dbg_k = nc.dram_tensor("dbg_k", [128, 8, 512], BF16, kind="ExternalOutput").ap()
        dbg_dec = nc.dram_tensor("dbg_dec", [128, 8, 128], BF16, kind="ExternalOutput").ap()
        dbg_dec0 = nc.dram_tensor("dbg_dec0", [128, 8, 128], BF16, kind="ExternalOutput").ap()

    es = contextlib.ExitStack()
    with es:
        def sb(name, shape, dt):
            return es.enter_context(nc.sbuf_tensor("s_" + name, list(shape), dt))

        def ps(name, dt=F32):
            return es.enter_context(nc.psum_tensor(name, [128, 512 if dt == F32 else 1024], dt))

        wring = [sb("wr%d" % i, [128, 4096], BF16) for i in range(NSLOT)]
        es2 = contextlib.ExitStack()
        sbt = lambda name, shape, dt: es2.enter_context(nc.sbuf_tensor("s_" + name, list(shape), dt))
        R = sb("R", [128, 8, 256], F32); Rb = sb("Rb", [128, 8, 256], BF16)
        ropet = [sb("ropet%d" % i, [128, 4, 64], F32) for i in range(4)]
        decayT = sb("decayT", [128, 8, 128], BF16); wcross = sb("wcross", [128, 8, 128], BF16)
        wstate = sb("wstate", [128, 8], F32); cdec = sb("cdec", [128, 8], F32)
        ident = sb("ident", [128, 128], BF16); ones = sb("ones", [128, 128], BF16); onesg = sb("onesg", [128, 128], BF16)
        Gtab = [sb("Gtab%d" % i, [128, D], BF16) for i in range(2)]
        flag = sb("flag", [128, 1], F32)
        AB = sb("AB", [128, 6, 16], F32)
        st4 = sb("st4", [128, 16], F32)
        ssy = sb("ssy", [128, 4, 8], F32)
        tmpA = sb("tmpA", [128, 512], F32); tmpB = sb("tmpB", [128, 512], F32)
        tb1 = sb("tb1", [128, 512], BF16); tb2 = sb("tb2", [128, 512], BF16)
        sTb = [sb("sTb%d" % i, [128, 128], BF16) for i in range(2)]
        ctab = sb("ctab", [128, 32, SUB], BF16); stab = sb("stab", [128, 32, SUB], BF16); rtab = sb("rtab", [128, 32, SUB], F32)
        BTre = sb("BTre", [128, 32, 128], BF16); BTim = sb("BTim", [128, 32, 128], BF16)
        Cre = sb("Cre", [128, 32, 128], BF16); Cim = sb("Cim", [128, 32, 128], BF16); Dfl = sb("Dfl", [128, 8, 128], BF16)
        s5s = sb("s5s", [128, 12, 32], F32)
        car = sb("car", [128, 2, 32], F32)
        bigps = [ps("psb%d" % i) for i in range(4)]
        auxps = [ps("psa%d" % i) for i in range(2)]
        tps = [ps("pst%d" % i, BF16) for i in range(2)]
        cnt = dict(big=0, aux=0, aux6=0, t=0, w=0, ld=0, st=0)

        def nxt(pool):
            i = cnt[pool]; cnt[pool] += 1
            if pool == "big":
                return bigps[i % 4], ("psb", i % 4)
            if pool == "aux":
                return auxps[i % 2], ("psa", i % 2)
            if pool == "aux6":
                j = i % 6
                return (auxps[j], ("psa", j)) if j < 2 else (bigps[j - 2], ("psb", j - 2))
            return tps[i % 2], ("pst", i % 2)

        def blk(b0, n):
            return [("ar", b) for b in range(b0, b0 + n)]

        def V(eng, fn, reads, writes):
            S.op(eng, fn, reads=reads, writes=writes)

        def load(dst_ap, src_ap, key, cast=False):
            lane = ("ldg%d" if cast else "lds%d") % (cnt["ld"] % 4); cnt["ld"] += 1
            S.op("gpsimd" if cast else "sync", lambda e: e.dma_start(out=dst_ap, in_=src_ap), writes=[key], lane=lane)

        def wtile(src2d, KC, ncols):
            i = cnt["w"] % NSLOT; cnt["w"] += 1
            view = wring[i][:, 0:KC * ncols].rearrange("p (k n) -> p k n", n=ncols)
            S.op("gpsimd", lambda e: e.dma_start(out=view, in_=src2d.rearrange("(k p) n -> p k n", p=128)),
                 writes=[("wr", i)], lane="w%d" % i)
            return view, ("wr", i)

        cT = sbt("cTs", [128, 16], F32); csb = sbt("csb", [128, 16], BF16); sgc = sbt("sgc", [128, 16], F32)
        modT = sbt("modT", [128, 96], F32); badaT = sbt("badaTs", [128, 96], F32); gains = sbt("gains", [128, 4, 16], F32)
        diagb = sbt("diagb", [128, 128], BF16)
        load(flag[:], flag_d, "flag"); load(cT[:], cT_d, "cT"); load(badaT[:], badaT_d, "badaT"); load(gains[:], gains_d, "gains")
        load(decayT[:], decay_d, "decayT", True); load(wcross[:], wcross_d, "wcross", True)
        load(wstate[:], wstate_d, "wstate"); load(cdec[:], cdec_d, "cdec"); load(ident[:], ident_d, "ident", True)
        load(Cre[:], cre_d, "Cre", True); load(Cim[:], cim_d, "Cim", True); load(Dfl[:], dfull_d, "Dfl", True)
        if DBG:
            S.op("sync", lambda e: e.dma_start(out=dbg_dec0, in_=decayT[:]), reads=["decayT"], lane="dbg4")
        V("vector", lambda e: e.memset(ones[:], 1.0), [], ["ones"])
        V("vector", lambda e: e.memset(onesg[:], 1.0 / 256.0), [], ["onesg"])
        V("vector", lambda e: e.memset(R[:], 0.0), [], ["R"])
        V("vector", lambda e: e.memset(car[:], 0.0), [], ["car"])

        V("scalar", lambda e: e.activation(out=sgc[:], in_=cT[:], func=AF.Sigmoid), ["cT"], ["sgc"])
        V("vector", lambda e: e.tensor_tensor(out=csb[:], in0=cT[:], in1=sgc[:], op=ALU.mult), ["cT", "sgc"], ["csb"])
        pa, pak = auxps[0], ("psa", 0)
        for tI in range(48):
            wv, wk = wtile(wada_d[:, tI * 256:(tI + 1) * 256], 16, 256)

            def mm(e, wv=wv, tI=tI):
                ins = None
                for s in range(2):
                    j = tI * 2 + s
                    for kc in range(16):
                        ins = e.matmul(pa[:, j:j + 1], lhsT=wv[:, kc, s * 128:(s + 1) * 128], rhs=csb[:, kc:kc + 1],
                                       start=(kc == 0), stop=(kc == 15))
                return ins
            V("tensor", mm, [wk, "csb"], [pak])
        V("vector", lambda e: e.tensor_tensor(out=modT[:], in0=pa[:, 0:96], in1=badaT[:], op=ALU.add), [pak, "badaT"], ["modT"])
        V("vector", lambda e: e.scalar_tensor_tensor(out=AB[:, 0, :], in0=modT[:, 16:32], scalar=1.0, in1=gains[:, 0, :], op0=ALU.add, op1=ALU.mult), ["modT", "gains"], ["AB"])
        V("vector", lambda e: e.tensor_copy(out=AB[:, 1, :], in_=modT[:, 0:16]), ["modT"], ["AB"])
        V("vector", lambda e: e.scalar_tensor_tensor(out=AB[:, 2, :], in0=modT[:, 64:80], scalar=1.0, in1=gains[:, 2, :], op0=ALU.add, op1=ALU.mult), ["modT", "gains"], ["AB"])
        V("vector", lambda e: e.tensor_copy(out=AB[:, 3, :], in_=modT[:, 48:64]), ["modT"], ["AB"])
        V("vector", lambda e: e.tensor_tensor(out=AB[:, 4, :], in0=modT[:, 32:48], in1=gains[:, 1, :], op=ALU.mult), ["modT", "gains"], ["AB"])
        V("vector", lambda e: e.tensor_tensor(out=AB[:, 5, :], in0=modT[:, 80:96], in1=gains[:, 3, :], op=ALU.mult), ["modT", "gains"], ["AB"])
        for gi in range(2):
            for kc in range(16):
                V("vector", lambda e, gi=gi, kc=kc: e.tensor_scalar(out=diagb[:], in0=ident[:], scalar1=AB[:, 4 + gi, kc:kc + 1], scalar2=None, op0=ALU.mult), ["ident", "AB"], ["diagb"])
                pb, pbk = nxt("aux")
                V("tensor", lambda e, pb=pb: e.matmul(pb[:, 0:128], lhsT=ones[:], rhs=diagb[:], start=True, stop=True), ["ones", "diagb"], [pbk])
                V("scalar", lambda e, pb=pb, gi=gi, kc=kc: e.activation(out=Gtab[gi][:, kc * 128:(kc + 1) * 128], in_=pb[:, 0:128], func=AF.Copy), [pbk], [("Gtab", gi)])

        sb_main = sb
        sb = sbt
        are = sb("are", [128, 32], F32); aim = sb("aim", [128, 32], F32); ldt = sb("ldt", [128, 32], F32)
        bre = sb("bre", [128, 32, 16], F32); bim = sb("bim", [128, 32, 16], F32)
        tvec = sb("tvec", [128, SUB], F32)
        ti32 = sb("ti32", [128, 32 * SUB], I32)
        phi = tmpAB = sb("phi", [128, 32 * SUB], F32); phr = sb("phr", [128, 32 * SUB], F32); msk = sb("msk", [128, 32 * SUB], F32)
        load(are[:], are_d, "are"); load(aim[:], aim_d, "aim"); load(ldt[:], ldt_d, "ldt")
        load(bre[:], bre_d, "bre"); load(bim[:], bim_d, "bim"); load(tvec[:], tvec_d, "tvec")
        DT, ADT, TH, MAG, CS, SN, FRE, FIM, T1, T2, RC, RS = range(12)
        sv = lambda i: s5s[:, i, :]

        def sincos(out_ap, in_ap, n, shift, rk, wk):
            V("vector", lambda e: e.tensor_scalar(out=phr[:, 0:n], in0=in_ap, scalar1=shift + 64 * TWO_PI, scalar2=None, op0=ALU.add), rk, ["phr"])
            V("vector", lambda e: e.tensor_scalar(out=ti32[:, 0:n], in0=phr[:, 0:n], scalar1=1.0 / TWO_PI, scalar2=None, op0=ALU.mult), ["phr"], ["ti32"])
            V("vector", lambda e: e.tensor_copy(out=msk[:, 0:n], in_=ti32[:, 0:n]), ["ti32"], ["msk"])
            V("vector", lambda e: e.scalar_tensor_tensor(out=phr[:, 0:n], in0=msk[:, 0:n], scalar=-TWO_PI, in1=phr[:, 0:n], op0=ALU.mult, op1=ALU.add), ["msk", "phr"], ["phr"])
            V("vector", lambda e: e.tensor_single_scalar(out=msk[:, 0:n], in_=phr[:, 0:n], scalar=math.pi, op=ALU.is_gt), ["phr"], ["msk"])
            V("vector", lambda e: e.scalar_tensor_tensor(out=phr[:, 0:n], in0=msk[:, 0:n], scalar=-TWO_PI, in1=phr[:, 0:n], op0=ALU.mult, op1=ALU.add), ["msk", "phr"], ["phr"])
            V("vector", lambda e: e.tensor_single_scalar(out=msk[:, 0:n], in_=phr[:, 0:n], scalar=-math.pi, op=ALU.is_lt), ["phr"], ["msk"])
            V("vector", lambda e: e.scalar_tensor_tensor(out=phr[:, 0:n], in0=msk[:, 0:n], scalar=TWO_PI, in1=phr[:, 0:n], op0=ALU.mult, op1=ALU.add), ["msk", "phr"], ["phr"])
            V("scalar", lambda e: e.activation(out=out_ap, in_=phr[:, 0:n], func=AF.Sin), ["phr"], wk)

        V("scalar", lambda e: e.activation(out=sv(DT), in_=ldt[:], func=AF.Exp), ["ldt"], ["s5s"])
        V("vector", lambda e: e.tensor_tensor(out=sv(ADT), in0=are[:], in1=sv(DT), op=ALU.mult), ["are", "s5s"], ["s5s"])
        V("vector", lambda e: e.tensor_tensor(out=sv(TH), in0=aim[:], in1=sv(DT), op=ALU.mult), ["aim", "s5s"], ["s5s"])
        V("scalar", lambda e: e.activation(out=sv(MAG), in_=sv(ADT), func=AF.Exp), ["s5s"], ["s5s"])
        sincos(sv(SN), sv(TH), 32, 0.0, ["s5s"], ["s5s"])
        sincos(sv(CS), sv(TH), 32, math.pi / 2, ["s5s"], ["s5s"])
        V("vector", lambda e: e.tensor_tensor(out=sv(T1), in0=sv(MAG), in1=sv(CS), op=ALU.mult), ["s5s"], ["s5s"])
        V("vector", lambda e: e.tensor_tensor(out=sv(T2), in0=sv(MAG), in1=sv(SN), op=ALU.mult), ["s5s"], ["s5s"])
        V("vector", lambda e: e.tensor_scalar(out=sv(T1), in0=sv(T1), scalar1=-1.0, scalar2=None, op0=ALU.add), ["s5s"], ["s5s"])
        V("vector", lambda e: e.tensor_tensor(out=sv(RC), in0=are[:], in1=are[:], op=ALU.mult), ["are"], ["s5s"])
        V("vector", lambda e: e.tensor_tensor(out=sv(RS), in0=aim[:], in1=aim[:], op=ALU.mult), ["aim"], ["s5s"])
        V("vector", lambda e: e.tensor_tensor(out=sv(RC), in0=sv(RC), in1=sv(RS), op=ALU.add), ["s5s"], ["s5s"])
        V("vector", lambda e: e.reciprocal(out=sv(RS), in_=sv(RC)), ["s5s"], ["s5s"])
        V("vector", lambda e: e.tensor_tensor(out=sv(FRE), in0=sv(T1), in1=are[:], op=ALU.mult), ["s5s", "are"], ["s5s"])
        V("vector", lambda e: e.tensor_tensor(out=sv(RC), in0=sv(T2), in1=aim[:], op=ALU.mult), ["s5s", "aim"], ["s5s"])
        V("vector", lambda e: e.tensor_tensor(out=sv(FRE), in0=sv(FRE), in1=sv(RC), op=ALU.add), ["s5s"], ["s5s"])
        V("vector", lambda e: e.tensor_tensor(out=sv(FRE), in0=sv(FRE), in1=sv(RS), op=ALU.mult), ["s5s"], ["s5s"])
        V("vector", lambda e: e.tensor_tensor(out=sv(FIM), in0=sv(T2), in1=are[:], op=ALU.mult), ["s5s", "are"], ["s5s"])
        V("vector", lambda e: e.tensor_tensor(out=sv(RC), in0=sv(T1), in1=aim[:], op=ALU.mult), ["s5s", "aim"], ["s5s"])
        V("vector", lambda e: e.tensor_tensor(out=sv(FIM), in0=sv(FIM), in1=sv(RC), op=ALU.subtract), ["s5s"], ["s5s"])
        V("vector", lambda e: e.tensor_tensor(out=sv(FIM), in0=sv(FIM), in1=sv(RS), op=ALU.mult), ["s5s"], ["s5s"])
        bbr = sb("bbr", [128, 32, 16], F32); bbi = sb("bbi", [128, 32, 16], F32); bbt = sb("bbt", [128, 32, 16], F32)
        fre_b = s5s[:, FRE, :].unsqueeze(2).to_broadcast([128, 32, 16]); fim_b = s5s[:, FIM, :].unsqueeze(2).to_broadcast([128, 32, 16])
        V("vector", lambda e: e.tensor_tensor(out=bbr[:], in0=bre[:], in1=fre_b, op=ALU.mult), ["bre", "s5s"], ["bbr"])
        V("vector", lambda e: e.tensor_tensor(out=bbt[:], in0=bim[:], in1=fim_b, op=ALU.mult), ["bim", "s5s"], ["bbt"])
        V("vector", lambda e: e.tensor_tensor(out=bbr[:], in0=bbr[:], in1=bbt[:], op=ALU.subtract), ["bbr", "bbt"], ["bbr"])
        V("vector", lambda e: e.tensor_tensor(out=bbi[:], in0=bim[:], in1=fre_b, op=ALU.mult), ["bim", "s5s"], ["bbi"])
        V("vector", lambda e: e.tensor_tensor(out=bbt[:], in0=bre[:], in1=fim_b, op=ALU.mult), ["bre", "s5s", "bbr"], ["bbt"])
        V("vector", lambda e: e.tensor_tensor(out=bbi[:], in0=bbi[:], in1=bbt[:], op=ALU.add), ["bbi", "bbt"], ["bbi"])
        for (bb, BT, nm) in ((bbr, BTre, "BTre"), (bbi, BTim, "BTim")):
            bf = sb("bf_" + nm, [128, 32, 128], BF16)
            V("vector", lambda e, bf=bf: e.memset(bf[:], 0.0), [], ["bf" + nm])
            for g2 in range(2):
                for r in range(4):
                    def cp(e, bf=bf, bb=bb, g2=g2, r=r):
                        dst = bf[g2 * 64:(g2 + 1) * 64, :, r * 32 + g2 * 16: r * 32 + g2 * 16 + 16].rearrange("p (q r) h -> p q r h", r=4)[:, :, r, :]
                        src = bb[g2 * 64:(g2 + 1) * 64, :, :].rearrange("p (q r) h -> p q r h", r=4)[:, :, r, :]
                        return e.tensor_copy(out=dst, in_=src)
                    V("vector", cp, ["bbr", "bbi"], ["bf" + nm])
            for gp in range(32):
                tp_, tk = nxt("t")
                V("tensor", lambda e, tp_=tp_, bf=bf, gp=gp: e.transpose(tp_[:, 0:128], bf[:, gp, :], ident[:]), ["bf" + nm, "ident"], [tk])
                V("scalar", lambda e, tp_=tp_, BT=BT, gp=gp: e.activation(out=BT[:, gp, :], in_=tp_[:, 0:128], func=AF.Copy), [tk], [nm])
        phi3 = phi[:].rearrange("p (g t) -> p g t", t=SUB)
        V("vector", lambda e: e.tensor_tensor(out=phi3, in0=s5s[:, TH, :].unsqueeze(2).to_broadcast([128, 32, SUB]),
                                              in1=tvec[:].unsqueeze(1).to_broadcast([128, 32, SUB]), op=ALU.mult), ["s5s", "tvec"], ["phi"])
        sincos(stab[:].rearrange("p g t -> p (g t)"), phi[:], 32 * SUB, 0.0, ["phi"], ["stab"])
        sincos(ctab[:].rearrange("p g t -> p (g t)"), phi[:], 32 * SUB, math.pi / 2, ["phi"], ["ctab"])
        V("vector", lambda e: e.tensor_copy(out=rtab[:], in_=s5s[:, MAG, :].unsqueeze(2).to_broadcast([128, 32, SUB])), ["s5s"], ["rtab"])
        V("vector", lambda e: e.memset(rtab[:, :, 0:1], 0.0), ["rtab"], ["rtab"])
        V("vector", lambda e: e.tensor_tensor(out=sv(RC), in0=sv(MAG), in1=ctab[:, :, SUB - 1], op=ALU.mult), ["s5s", "ctab"], ["s5s"])
        V("vector", lambda e: e.tensor_tensor(out=sv(RS), in0=sv(MAG), in1=stab[:, :, SUB - 1], op=ALU.mult), ["s5s", "stab"], ["s5s"])

        es2.close()
        sb = sb_main
        S.barrier()
        xt = sb("xt", [128, 4, D], F32)
        hT = sb("hT", [128, 16, 512], BF16)
        ar = sb("arena", [128, 64, 512], BF16)
        XN0, V0, MG0 = 0, 0, 0
        QT0, QS0, UT0, YS0 = 16, 24, 16, 24
        KT0, KS0, SG0 = 32, 40, 32
        RT0, YB0 = 48, 48
        W50 = 32
        xn = ar[:, XN0:XN0 + 16, :].rearrange("p (c q) f -> p c (q f)", c=4)
        vtok = xn
        ysb = ar[:, YB0:YB0 + 16, :].rearrange("p (c q) f -> p c (q f)", c=4)
        ktil = ar[:, KS0:KS0 + 8, :].rearrange("p (c q) f -> p c (q f)", c=4)

        def norm_to_hT(acol, bcol):
            for c in range(4):
                V("scalar", lambda e, c=c: e.activation(out=xn[:, c, :], in_=xt[:, c, :], func=AF.Square, accum_out=st4[:, c:c + 1]),
                  [("xt", c)], blk(XN0 + 4 * c, 4) + ["st4"])
            V("vector", lambda e: e.tensor_scalar(out=st4[:, 4:8], in0=st4[:, 0:4], scalar1=1.0 / D, scalar2=1e-6, op0=ALU.mult, op1=ALU.add), ["st4"], ["st4"])
            V("scalar", lambda e: e.activation(out=st4[:, 4:8], in_=st4[:, 4:8], func=AF.Sqrt), ["st4"], ["st4"])
            V("vector", lambda e: e.reciprocal(out=st4[:, 8:12], in_=st4[:, 4:8]), ["st4"], ["st4"])
            for c in range(4):
                V("scalar", lambda e, c=c: e.activation(out=xn[:, c, :], in_=xt[:, c, :], func=AF.Identity, scale=st4[:, 8 + c:9 + c]),
                  [("xt", c), "st4"], blk(XN0 + 4 * c, 4))
            for kc in range(16):
                tp_, tk = nxt("t")

                def tr(e, tp_=tp_, kc=kc):
                    ins = None
                    for c in range(4):
                        ins = e.transpose(tp_[:, c * 128:(c + 1) * 128], xn[:, c, kc * 128:(kc + 1) * 128], ident[:])
                    ins = e.transpose(tp_[:, 384:512], xn[:, 3, kc * 128:(kc + 1) * 128], ident[:])
                    return ins
                V("tensor", tr, blk(XN0, 16) + ["ident"], [tk])
                V("scalar", lambda e, tp_=tp_, kc=kc: e.activation(out=hT[:, kc, :], in_=tp_[:, 0:512], func=AF.Identity,
                                                                    scale=AB[:, acol, kc:kc + 1], bias=AB[:, bcol, kc:kc + 1]), [tk, "AB"], [("hT", kc)])

        hT_keys = [("hT", kc) for kc in range(16)]

        def proj_fm(w2d, col0, ncols, KC, rhs_fn, rhs_keys, evac):
            for t0 in range(0, ncols, 256):
                wv, wk = wtile(w2d[:, col0 + t0: col0 + t0 + 256], KC, 256)
                for s in range(2):
                    bk, bkk = nxt("big")

                    def mm(e, wv=wv, s=s, bk=bk):
                        ins = None
                        for kc in range(KC):
                            ins = e.matmul(bk[:, 0:512], lhsT=wv[:, kc, s * 128:(s + 1) * 128], rhs=rhs_fn(kc), start=(kc == 0), stop=(kc == KC - 1))
                        return ins
                    V("tensor", mm, [wk] + rhs_keys, [bkk])
                    evac(bk, bkk, t0 // 128 + s)

        def proj_tm(w2d, col0, ncols, lhs_fn, lhs_keys, evac, KC=16):
            for t0 in range(0, ncols, 256):
                wv, wk = wtile(w2d[:, col0 + t0: col0 + t0 + 256], KC, 256)
                for c in range(4):
                    bk, bkk = nxt("big")

                    def mm(e, wv=wv, c=c, bk=bk):
                        ins = None
                        for kc in range(KC):
                            ins = e.matmul(bk[:, 0:256], lhsT=lhs_fn(kc, c), rhs=wv[:, kc, :], start=(kc == 0), stop=(kc == KC - 1))
                        return ins
                    V("tensor", mm, [wk] + lhs_keys, [bkk])
                    evac(bk, bkk, c, t0 // 256)

        def rope_evac(ci, si, dst_fn):
            cosT, sinT = ropet[ci], ropet[si]
            def ev(bk, bkk, c, ti):
                p4 = bk[:, 0:256].rearrange("p (h two d) -> p h two d", two=2, d=64)
                t1, t2 = p4[:, :, 0, :], p4[:, :, 1, :]
                cb = cosT[:, c, :].unsqueeze(1).to_broadcast([128, 2, 64]); sbb = sinT[:, c, :].unsqueeze(1).to_broadcast([128, 2, 64])
                A = tmpA[:, 0:128].rearrange("p (h d) -> p h d", d=64); B = tmpB[:, 0:128].rearrange("p (h d) -> p h d", d=64)
                o4 = tb1[:, 0:256].rearrange("p (h two d) -> p h two d", two=2, d=64)
                rk = [bkk, ("ropet", ci), ("ropet", si)]
                V("vector", lambda e: e.tensor_tensor(out=A, in0=t1, in1=cb, op=ALU.mult), rk, ["tmpA"])
                V("vector", lambda e: e.tensor_tensor(out=B, in0=t2, in1=sbb, op=ALU.mult), rk, ["tmpB"])
                V("vector", lambda e: e.tensor_tensor(out=o4[:, :, 0, :], in0=A, in1=B, op=ALU.subtract), ["tmpA", "tmpB"], ["tb1"])
                V("vector", lambda e: e.tensor_tensor(out=A, in0=t1, in1=sbb, op=ALU.mult), rk, ["tmpA"])
                V("vector", lambda e: e.tensor_tensor(out=B, in0=t2, in1=cb, op=ALU.mult), rk, ["tmpB"])
                V("vector", lambda e: e.tensor_tensor(out=o4[:, :, 1, :], in0=A, in1=B, op=ALU.add), ["tmpA", "tmpB"], ["tb1"])
                dst_fn(c, ti)
            return ev

        def s5_sub(ti_, sc, main):
            tc0 = sc * SUB
            WB = 32 * SUB // 512
            W = [ar[:, W50 + WB * i: W50 + WB * i + WB, :].rearrange("p a f -> p (a f)").rearrange("p (g t) -> p g t", t=SUB) for i in range(4)]
            Wk = [blk(W50 + WB * i, WB) for i in range(4)]
            for i in range(2):
                b0 = W50 + 4 * WB + 2 * WB * i
                W.append(ar[:, b0:b0 + 2 * WB, :].rearrange("p a f -> p (a f)").bitcast(F32).rearrange("p (g t) -> p g t", t=SUB))
                Wk.append(blk(b0, 2 * WB))
            bur, bui, ta, tb, wr_, wi_ = W
            for (BT, dst, dk, nm) in ((BTre, bur, Wk[0], "BTre"), (BTim, bui, Wk[1], "BTim")):
                for kc in range(8):
                    pb, pbk = nxt("aux6")

                    def mm(e, pb=pb, BT=BT, kc=kc):
                        ins = None
                        for r in range(4):
                            ins = e.matmul(pb[:, r * SUB:(r + 1) * SUB], lhsT=BT[:, 4 * kc + r, :], rhs=ar[:, UT0 + kc, tc0:tc0 + SUB], start=True, stop=True)
                        return ins
                    V("tensor", mm, [nm, ("ar", UT0 + kc)], [pbk])
                    V("scalar", lambda e, pb=pb, dst=dst, kc=kc: e.activation(out=dst[:, 4 * kc:4 * kc + 4, :], in_=pb[:, 0:4 * SUB].rearrange("p (g t) -> p g t", t=SUB), func=AF.Copy), [pbk], dk)
            TT_ = lambda o, a, b, op, rk, wk: V("vector", lambda e: e.tensor_tensor(out=o, in0=a, in1=b, op=op), rk, wk)
            TT_(ta[:], ctab[:], bur[:], ALU.mult, ["ctab"] + Wk[0], Wk[2])
            TT_(tb[:], stab[:], bui[:], ALU.mult, ["stab"] + Wk[1], Wk[3])
            TT_(ta[:], ta[:], tb[:], ALU.add, Wk[2] + Wk[3], Wk[2])
            TT_(tb[:], ctab[:], bui[:], ALU.mult, ["ctab"] + Wk[1], Wk[3])
            TT_(wr_[:], stab[:], bur[:], ALU.mult, ["stab"] + Wk[0], Wk[4])
            TT_(tb[:], tb[:], wr_[:], ALU.subtract, Wk[3] + Wk[4], Wk[3])
            TT_(ta[:, :, 0], ta[:, :, 0], car[:, 0, :], ALU.add, Wk[2] + ["car"], Wk[2])
            TT_(tb[:, :, 0], tb[:, :, 0], car[:, 1, :], ALU.add, Wk[3] + ["car"], Wk[3])
            fl = lambda a: a.rearrange("p g t -> p (g t)")
            V("vector", lambda e: e.tensor_tensor_scan(out=fl(wr_[:]), data0=fl(rtab[:]), data1=fl(ta[:]), initial=0.0, op0=ALU.mult, op1=ALU.add), ["rtab"] + Wk[2], Wk[4])
            V("vector", lambda e: e.tensor_tensor_scan(out=fl(wi_[:]), data0=fl(rtab[:]), data1=fl(tb[:]), initial=0.0, op0=ALU.mult, op1=ALU.add), ["rtab"] + Wk[3], Wk[5])
            we_r, we_i = wr_[:, :, SUB - 1], wi_[:, :, SUB - 1]
            TT_(sv(T1), sv(RC), we_r, ALU.mult, ["s5s"] + Wk[4], ["s5t"])
            TT_(sv(T2), sv(RS), we_i, ALU.mult, ["s5s"] + Wk[5], ["s5t2"])
            TT_(car[:, 0, :], sv(T1), sv(T2), ALU.subtract, ["s5t", "s5t2"], ["car"])
            TT_(sv(T1), sv(RS), we_r, ALU.mult, ["s5s"] + Wk[4], ["s5t"])
            TT_(sv(T2), sv(RC), we_i, ALU.mult, ["s5s"] + Wk[5], ["s5t2"])
            TT_(car[:, 1, :], sv(T1), sv(T2), ALU.add, ["s5t", "s5t2"], ["car"])
            if not main:
                return
            TT_(bur[:], ctab[:], wr_[:], ALU.mult, ["ctab"] + Wk[4], Wk[0])
            TT_(bui[:], stab[:], wi_[:], ALU.mult, ["stab"] + Wk[5], Wk[1])
            TT_(ta[:], bur[:], bui[:], ALU.subtract, Wk[0] + Wk[1], Wk[2])
            TT_(bur[:], stab[:], wr_[:], ALU.mult, ["stab"] + Wk[4], Wk[0])
            TT_(bui[:], ctab[:], wi_[:], ALU.mult, ["ctab"] + Wk[5], Wk[1])
            V("vector", lambda e: e.scalar_tensor_tensor(out=tb[:], in0=bur[:], scalar=-1.0, in1=bui[:], op0=ALU.mult, op1=ALU.subtract), Wk[0] + Wk[1], Wk[3])
            pb, pbk = nxt("aux6")

            def ym(e, pb=pb):
                ins = None
                for kc in range(8):
                    o = pb[:, kc * SUB:(kc + 1) * SUB]
                    for r in range(4):
                        gp = 4 * kc + r
                        e.matmul(o, lhsT=Cre[:, gp, :], rhs=ta[:, gp, :], start=(r == 0), stop=False)
                        e.matmul(o, lhsT=Cim[:, gp, :], rhs=tb[:, gp, :], start=False, stop=False)
                    ins = e.matmul(o, lhsT=Dfl[:, kc, :], rhs=ar[:, UT0 + kc, tc0:tc0 + SUB], start=False, stop=True)
                return ins
            V("tensor", ym, ["Cre", "Cim", "Dfl"] + Wk[2] + Wk[3] + blk(UT0, 8), [pbk])
            V("scalar", lambda e: e.activation(out=tmpA[:, 0:8 * SUB], in_=pb[:, 0:8 * SUB], func=AF.Square), [pbk], ["tmpA"])
            V("vector", lambda e: e.tensor_scalar(out=tmpA[:, 0:8 * SUB], in0=tmpA[:, 0:8 * SUB], scalar1=0.044715, scalar2=1.0, op0=ALU.mult, op1=ALU.add), ["tmpA"], ["tmpA"])
            V("vector", lambda e: e.tensor_tensor(out=tmpA[:, 0:8 * SUB], in0=tmpA[:, 0:8 * SUB], in1=pb[:, 0:8 * SUB], op=ALU.mult), ["tmpA", pbk], ["tmpA"])
            V("scalar", lambda e: e.activation(out=tmpB[:, 0:8 * SUB], in_=tmpA[:, 0:8 * SUB], func=AF.Sigmoid, scale=1.5957691216057308), ["tmpA"], ["tmpB"])
            V("vector", lambda e: e.tensor_tensor(out=ar[:, YS0:YS0 + 8, tc0:tc0 + SUB], in0=tmpB[:, 0:8 * SUB].rearrange("p (k t) -> p k t", t=SUB),
                                                  in1=pb[:, 0:8 * SUB].rearrange("p (k t) -> p k t", t=SUB), op=ALU.mult), ["tmpB", pbk], blk(YS0, 8))

        def post_norm_residual(gi, nparts):
            V("vector", lambda e: e.tensor_reduce(out=st4[:, 0:4], in_=ssy[:, :, 0:nparts], axis=mybir.AxisListType.X, op=ALU.add), ["ssy"], ["st4"])
            V("vector", lambda e: e.tensor_scalar(out=st4[:, 4:8], in0=st4[:, 0:4], scalar1=1.0 / D, scalar2=1e-6, op0=ALU.mult, op1=ALU.add), ["st4"], ["st4"])
            V("scalar", lambda e: e.activation(out=st4[:, 4:8], in_=st4[:, 4:8], func=AF.Sqrt), ["st4"], ["st4"])
            V("vector", lambda e: e.reciprocal(out=st4[:, 12:16], in_=st4[:, 4:8]), ["st4"], ["st4"])
            for c in range(4):
                V("vector", lambda e, c=c: e.scalar_tensor_tensor(out=ysb[:, c, :], in0=ysb[:, c, :], scalar=st4[:, 12 + c:13 + c], in1=Gtab[gi][:], op0=ALU.mult, op1=ALU.mult),
                  blk(YB0 + 4 * c, 4) + ["st4", ("Gtab", gi)], blk(YB0 + 4 * c, 4))
                V("vector", lambda e, c=c: e.tensor_tensor(out=xt[:, c, :], in0=xt[:, c, :], in1=ysb[:, c, :], op=ALU.add), blk(YB0 + 4 * c, 4) + [("xt", c)], [("xt", c)])

        def ytok_evac(bk, bkk, c, ti, width, col0, part):
            V("scalar", lambda e: e.activation(out=ysb[:, c, col0:col0 + width], in_=bk[:, 0:width], func=AF.Copy), [bkk], blk(YB0 + 4 * c, 4))
            V("scalar", lambda e: e.activation(out=tmpA[:, 0:width], in_=bk[:, 0:width], func=AF.Square, accum_out=ssy[:, c, part:part + 1]), [bkk], ["tmpA", "ssy"])

        def tile(ti_, main):
            src = x_d if main else xp_d
            gch0 = (16 if main else 0) + 4 * ti_
            for c in range(4):
                S.op("sync", lambda e, c=c: e.dma_start(out=xt[:, c, :], in_=src[(ti_ * 4 + c) * 128:(ti_ * 4 + c + 1) * 128, :]), writes=[("xt", c)], lane="x%d" % c)
            for i in range(4):
                S.op("sync", lambda e, i=i: e.dma_start(out=ropet[i][:], in_=rope_d[i][:, gch0:gch0 + 4, :]), writes=[("ropet", i)], lane="rp%d" % i)
            norm_to_hT(0, 1)
            hfn = lambda kc, c: hT[:, kc, c * 128:(c + 1) * 128]
            if main:
                def qdst(c, ti):
                    for hh in range(2):
                        h = 2 * ti + hh
                        tp_, tk = nxt("t")
                        V("tensor", lambda e, tp_=tp_, hh=hh: (e.transpose(tp_[:, 0:128], tb1[:, hh * 128:(hh + 1) * 128], ident[:]), e.transpose(tp_[:, 0:128], tb1[:, hh * 128:(hh + 1) * 128], ident[:]))[1], ["tb1", "ident"], [tk])
                        V("scalar", lambda e, tp_=tp_, h=h: e.activation(out=ar[:, QT0 + h, c * 128:(c + 1) * 128], in_=tp_[:, 0:128], func=AF.Copy), [tk], [("ar", QT0 + h)])
                        V("vector", lambda e, h=h: e.tensor_tensor(out=ar[:, QS0 + h, c * 128:(c + 1) * 128], in0=ar[:, QT0 + h, c * 128:(c + 1) * 128], in1=wcross[:, h, :], op=ALU.mult), [("ar", QT0 + h), "wcross"], [("ar", QS0 + h)])
                proj_tm(win_d, 0, 1024, hfn, hT_keys, rope_evac(0, 1, qdst))

            def kdst(c, ti):
                for hh in range(2):
                    h = 2 * ti + hh
                    if main:
                        tp_, tk = nxt("t")
                        V("tensor", lambda e, tp_=tp_, hh=hh: (e.transpose(tp_[:, 0:128], tb1[:, hh * 128:(hh + 1) * 128], ident[:]), e.transpose(tp_[:, 0:128], tb1[:, hh * 128:(hh + 1) * 128], ident[:]))[1], ["tb1", "ident"], [tk])
                        V("scalar", lambda e, tp_=tp_, h=h: e.activation(out=ar[:, KT0 + h, c * 128:(c + 1) * 128], in_=tp_[:, 0:128], func=AF.Copy), [tk], [("ar", KT0 + h)])
                    V("vector", lambda e, h=h, hh=hh: e.tensor_scalar(out=ktil[:, c, h * 128:(h + 1) * 128], in0=tb1[:, hh * 128:(hh + 1) * 128], scalar1=wstate[:, h:h + 1], scalar2=None, op0=ALU.mult),
                      ["tb1", "wstate"], blk(KS0 + 2 * c, 2))
            proj_tm(win_d, 1024, 1024, hfn, hT_keys, rope_evac(2, 3, kdst))
            if main and DBG and ti_ == 0:
                S.op("sync", lambda e: e.dma_start(out=dbg_q, in_=ar[:, QT0:QT0 + 16, :]), reads=blk(QT0, 16), lane="dbg6")
                S.op("sync", lambda e: e.dma_start(out=dbg_k, in_=ar[:, KT0:KT0 + 8, :]), reads=blk(KT0, 8), lane="dbg7")
            proj_tm(win_d, 2048, 2048, hfn, hT_keys,
                    lambda bk, bkk, c, ti: V("scalar", lambda e: e.activation(out=vtok[:, c, ti * 256:(ti + 1) * 256], in_=bk[:, 0:256], func=AF.Copy), [bkk], blk(V0 + 4 * c, 4)))
            for c in range(4):
                cs = slice(c * 128, (c + 1) * 128)
                for h in range(8):
                    if main:
                        pb, pbk = nxt("aux6")
                        V("tensor", lambda e, pb=pb, h=h, cs=cs: e.matmul(pb[:, 0:128], lhsT=ar[:, KT0 + h, cs], rhs=ar[:, QT0 + h, cs], start=True, stop=True), [("ar", KT0 + h), ("ar", QT0 + h)], [pbk])
                        sT = sTb[h % 2]; sTk = ("sTb", h % 2)
                        V("vector", lambda e, pb=pb, h=h, sT=sT: e.tensor_tensor(out=sT[:], in0=pb[:, 0:128], in1=decayT[:, h, :], op=ALU.mult), [pbk, "decayT"], [sTk])
                        pr, prk = nxt("aux6")

                        def rm(e, pr=pr, h=h, cs=cs, sT=sT, c=c):
                            ins = None
                            for vh in range(2):
                                o = pr[:, vh * 128:(vh + 1) * 128]
                                e.matmul(o, lhsT=vtok[:, c, h * 256 + vh * 128: h * 256 + (vh + 1) * 128], rhs=sT[:], start=True, stop=False)
                                ins = e.matmul(o, lhsT=Rb[:, h, vh * 128:(vh + 1) * 128], rhs=ar[:, QS0 + h, cs], start=False, stop=True)
                            return ins
                        V("tensor", rm, blk(V0 + 4 * c, 4) + [sTk, "Rb", ("ar", QS0 + h)], [prk])
                        V("scalar", lambda e, pr=pr: e.activation(out=tb1[:, 0:256], in_=pr[:, 0:256], func=AF.Copy), [prk], ["tb1"])
                        V("scalar", lambda e, pr=pr: e.activation(out=tb2[:, 0:256], in_=pr[:, 0:256], func=AF.Square), [prk], ["tb2"])
                        pq, pqk = nxt("aux6")

                        def sm(e, pq=pq):
                            e.matmul(pq[:, 0:128], lhsT=onesg[:], rhs=tb1[:, 0:128], start=True, stop=False)
                            e.matmul(pq[:, 0:128], lhsT=onesg[:], rhs=tb1[:, 128:256], start=False, stop=True)
                            e.matmul(pq[:, 128:256], lhsT=onesg[:], rhs=tb2[:, 0:128], start=True, stop=False)
                            return e.matmul(pq[:, 128:256], lhsT=onesg[:], rhs=tb2[:, 128:256], start=False, stop=True)
                        V("tensor", sm, ["onesg", "tb1", "tb2"], [pqk])
                        mA, vA = tmpA[:, 0:128], tmpA[:, 128:256]
                        V("vector", lambda e, pq=pq: e.tensor_copy(out=tmpA[:, 0:256], in_=pq[:, 0:256]), [pqk], ["tmpA"])
                        V("vector", lambda e: e.tensor_tensor(out=tmpB[:, 0:128], in0=mA, in1=mA, op=ALU.mult), ["tmpA"], ["tmpB"])
                        V("vector", lambda e: e.tensor_tensor(out=vA, in0=vA, in1=tmpB[:, 0:128], op=ALU.subtract), ["tmpA", "tmpB"], ["tmpA"])
                        V("vector", lambda e: e.tensor_scalar(out=vA, in0=vA, scalar1=1e-5, scalar2=None, op0=ALU.add), ["tmpA"], ["tmpA"])
                        V("scalar", lambda e: e.activation(out=vA, in_=vA, func=AF.Sqrt), ["tmpA"], ["tmpA"])
                        V("vector", lambda e: e.reciprocal(out=tmpB[:, 0:128], in_=vA), ["tmpA"], ["tmpB"])
                        for vh in range(2):
                            V("vector", lambda e, pr=pr, vh=vh: e.tensor_tensor(out=tmpB[:, 128 + vh * 128:256 + vh * 128], in0=pr[:, vh * 128:(vh + 1) * 128], in1=mA, op=ALU.subtract), [prk, "tmpA"], ["tmpB"])
                            V("vector", lambda e, vh=vh, h=h, cs=cs: e.tensor_tensor(out=ar[:, RT0 + 2 * h + vh, cs], in0=tmpB[:, 128 + vh * 128:256 + vh * 128], in1=tmpB[:, 0:128], op=ALU.mult),
                              ["tmpB", "tmpB"], [("ar", RT0 + 2 * h + vh)])
                for h2 in range(4):
                    pk, pkk = nxt("aux6")

                    def km(e, pk=pk, h2=h2, c=c):
                        ins = None
                        for hh in range(2):
                            h = 2 * h2 + hh
                            ins = e.matmul(pk[:, hh * 256:(hh + 1) * 256], lhsT=ktil[:, c, h * 128:(h + 1) * 128], rhs=vtok[:, c, h * 256:(h + 1) * 256], start=True, stop=True)
                        return ins
                    V("tensor", km, blk(KS0 + 2 * c, 2) + blk(V0 + 4 * c, 4), [pkk])
                    for hh in range(2):
                        h = 2 * h2 + hh
                        V("vector", lambda e, pk=pk, h=h, hh=hh: e.scalar_tensor_tensor(out=R[:, h, :], in0=R[:, h, :], scalar=cdec[:, h:h + 1], in1=pk[:, hh * 256:(hh + 1) * 256], op0=ALU.mult, op1=ALU.add),
                          [pkk, "R", "cdec"], ["R"])
                if main:
                    V("scalar", lambda e: e.activation(out=Rb[:], in_=R[:], func=AF.Copy), ["R"], ["Rb"])
            if main and DBG and ti_ == 0:
                S.op("sync", lambda e: e.dma_start(out=dbg_ret, in_=ar[:, RT0:RT0 + 16, :]), reads=blk(RT0, 16), lane="dbg0")
            if main:
                def gev(bk, bkk, j):
                    V("scalar", lambda e: e.activation(out=tb1[:], in_=bk[:, 0:512], func=AF.Sigmoid), [bkk], ["tb1"])
                    V("vector", lambda e: e.tensor_tensor(out=tb2[:], in0=bk[:, 0:512], in1=tb1[:], op=ALU.mult), [bkk, "tb1"], ["tb2"])
                    V("vector", lambda e: e.tensor_tensor(out=ar[:, RT0 + j, :], in0=ar[:, RT0 + j, :], in1=tb2[:], op=ALU.mult), ["tb2", ("ar", RT0 + j)], [("ar", RT0 + j)])
                proj_fm(win_d, 4096, 2048, 16, lambda kc: hT[:, kc, :], hT_keys, gev)
                proj_fm(win_d, 7168, 2048, 16, lambda kc: hT[:, kc, :], hT_keys,
                        lambda bk, bkk, j: V("scalar", lambda e: e.activation(out=ar[:, MG0 + j, :], in_=bk[:, 0:512], func=AF.Sigmoid), [bkk], [("ar", MG0 + j)]))
                proj_fm(wro_d, 0, 2048, 16, lambda kc: ar[:, RT0 + kc, :], blk(RT0, 16),
                        lambda bk, bkk, j: V("vector", lambda e: e.tensor_tensor(out=ar[:, MG0 + j, :], in0=ar[:, MG0 + j, :], in1=bk[:, 0:512], op=ALU.mult), [bkk, ("ar", MG0 + j)], [("ar", MG0 + j)]))
            proj_fm(win_d, 6144, 1024, 16, lambda kc: hT[:, kc, :], hT_keys,
                    lambda bk, bkk, j: V("scalar", lambda e: e.activation(out=ar[:, UT0 + j, :], in_=bk[:, 0:512], func=AF.Copy), [bkk], [("ar", UT0 + j)]))
            for sc in range(512 // SUB):
                s5_sub(ti_, sc, main)
            if not main:
                return
            if DBG and ti_ == 0:
                S.op("sync", lambda e: e.dma_start(out=dbg_ys, in_=ar[:, YS0:YS0 + 8, :]), reads=blk(YS0, 8), lane="dbg1")
            proj_fm(win_d, 9216, 2048, 16, lambda kc: hT[:, kc, :], hT_keys,
                    lambda bk, bkk, j: V("scalar", lambda e: e.activation(out=ar[:, SG0 + j, :], in_=bk[:, 0:512], func=AF.Sigmoid), [bkk], [("ar", SG0 + j)]))
            for t0 in range(0, 2048, 256):
                wa, wak = wtile(wglu_d[:, t0:t0 + 256], 8, 256)
                wb, wbk = wtile(wglu_d[:, 2048 + t0:2048 + t0 + 256], 8, 256)
                for s in range(2):
                    j = t0 // 128 + s
                    ba, bak = nxt("big"); bb_, bbk = nxt("big")

                    def mg(e, wa=wa, wb=wb, s=s, ba=ba, bb_=bb_):
                        ins = None
                        for kc in range(8):
                            e.matmul(ba[:, 0:512], lhsT=wa[:, kc, s * 128:(s + 1) * 128], rhs=ar[:, YS0 + kc, :], start=(kc == 0), stop=(kc == 7))
                        for kc in range(8):
                            ins = e.matmul(bb_[:, 0:512], lhsT=wb[:, kc, s * 128:(s + 1) * 128], rhs=ar[:, YS0 + kc, :], start=(kc == 0), stop=(kc == 7))
                        return ins
                    V("tensor", mg, [wak, wbk] + blk(YS0, 8), [bak, bbk])
                    V("scalar", lambda e, bb_=bb_: e.activation(out=tb1[:], in_=bb_[:, 0:512], func=AF.Sigmoid), [bbk], ["tb1"])
                    V("vector", lambda e, ba=ba: e.tensor_tensor(out=tb2[:], in0=ba[:, 0:512], in1=tb1[:], op=ALU.mult), [bak, "tb1"], ["tb2"])
                    V("vector", lambda e, j=j: e.tensor_tensor(out=tb2[:], in0=tb2[:], in1=ar[:, SG0 + j, :], op=ALU.mult), ["tb2", ("ar", SG0 + j)], ["tb2"])
                    V("vector", lambda e, j=j: e.tensor_tensor(out=ar[:, MG0 + j, :], in0=ar[:, MG0 + j, :], in1=tb2[:], op=ALU.add), ["tb2", ("ar", MG0 + j)], [("ar", MG0 + j)])
            if DBG and ti_ == 0:
                S.op("sync", lambda e: e.dma_start(out=dbg_mg, in_=ar[:, MG0:MG0 + 16, :]), reads=blk(MG0, 16), lane="dbg2")
            proj_tm(wout_d, 0, 2048, lambda kc, c: ar[:, MG0 + kc, c * 128:(c + 1) * 128], blk(MG0, 16),
                    lambda bk, bkk, c, ti: ytok_evac(bk, bkk, c, ti, 256, ti * 256, ti))
            post_norm_residual(0, 8)
            if DBG and ti_ == 0:
                S.op("sync", lambda e: e.dma_start(out=dbg_x1, in_=xt[:]), reads=[("xt", c_) for c_ in range(4)], lane="dbg3")
            norm_to_hT(2, 3)
            for t0 in range(0, DFF, 256):
                wa, wak = wtile(wfi_d[:, t0:t0 + 256], 16, 256)
                wb, wbk = wtile(wfi_d[:, DFF + t0:DFF + t0 + 256], 16, 256)
                for s in range(2):
                    j = t0 // 128 + s
                    ba, bak = nxt("big"); bb_, bbk = nxt("big")

                    def mf(e, wa=wa, wb=wb, s=s, ba=ba, bb_=bb_):
                        ins = None
                        for kc in range(16):
                            e.matmul(ba[:, 0:512], lhsT=wa[:, kc, s * 128:(s + 1) * 128], rhs=hT[:, kc, :], start=(kc == 0), stop=(kc == 15))
                        for kc in range(16):
                            ins = e.matmul(bb_[:, 0:512], lhsT=wb[:, kc, s * 128:(s + 1) * 128], rhs=hT[:, kc, :], start=(kc == 0), stop=(kc == 15))
                        return ins
                    V("tensor", mf, [wak, wbk] + hT_keys, [bak, bbk])
                    V("scalar", lambda e, ba=ba: e.activation(out=tb1[:], in_=ba[:, 0:512], func=AF.Sigmoid), [bak], ["tb1"])
                    V("vector", lambda e, ba=ba: e.tensor_tensor(out=tb2[:], in0=ba[:, 0:512], in1=tb1[:], op=ALU.mult), [bak, "tb1"], ["tb2"])
                    V("vector", lambda e, bb_=bb_, j=j: e.tensor_tensor(out=ar[:, j, :], in0=tb2[:], in1=bb_[:, 0:512], op=ALU.mult), ["tb2", bbk], [("ar", j)])
            for ns in range(4):
                banks = [nxt("big") for _ in range(4)]
                for kg in range(11):
                    wv, wk = wtile(wfo_d[kg * 512:(kg + 1) * 512, ns * 512:(ns + 1) * 512], 4, 512)
                    for c in range(4):
                        bk, bkk = banks[c]

                        def mo(e, wv=wv, bk=bk, c=c, kg=kg):
                            ins = None
                            for i in range(4):
                                ins = e.matmul(bk[:, 0:512], lhsT=ar[:, kg * 4 + i, c * 128:(c + 1) * 128], rhs=wv[:, i, :], start=(kg == 0 and i == 0), stop=(kg == 10 and i == 3))
                            return ins
                        V("tensor", mo, [wk] + blk(kg * 4, 4), [bkk])
                for c in range(4):
                    bk, bkk = banks[c]
                    ytok_evac(bk, bkk, c, 0, 512, ns * 512, ns)
            post_norm_residual(1, 4)
            for c in range(4):
                S.op("sync", lambda e, c=c: e.dma_start(out=out_d[(ti_ * 4 + c) * 128:(ti_ * 4 + c + 1) * 128, :], in_=xt[:, c, :]), reads=[("xt", c)], lane="o%d" % c)

        for tp in range(NT):
            tile(tp, False)
        V("vector", lambda e: e.tensor_scalar(out=R[:].rearrange("p h v -> p (h v)"), in0=R[:].rearrange("p h v -> p (h v)"), scalar1=flag[:, 0:1], scalar2=None, op0=ALU.mult), ["R", "flag"], ["R"])
        V("vector", lambda e: e.tensor_scalar(out=car[:].rearrange("p a g -> p (a g)"), in0=car[:].rearrange("p a g -> p (a g)"), scalar1=flag[:, 0:1], scalar2=None, op0=ALU.mult), ["car", "flag"], ["car"])
        V("scalar", lambda e: e.activation(out=Rb[:], in_=R[:], func=AF.Copy), ["R"], ["Rb"])
        for tm in range(NT):
            tile(tm, True)
        if DBG:
            S.op("sync", lambda e: e.dma_start(out=dbg_dec, in_=decayT[:]), reads=["decayT"], lane="dbg5")
        print("sbuf bytes remaining:", nc.sbuf_bytes_remaining)
        S.emit(nc)
    return nc


_CACHE = {}


def _consts():
    H, C, dk = 8, 128, 128
    log_g = np.log1p(-np.exp2(-5.0 - np.arange(H, dtype=np.float64)))
    idx = np.arange(C, dtype=np.float64)
    rel = idx[None, :] - idx[:, None]
    decayT = np.where(rel[:, None, :] >= 0, np.exp(np.maximum(rel, 0.0)[:, None, :] * log_g[None, :, None]), 0.0)
    wcross = np.broadcast_to(np.exp((idx + 1.0)[None, None, :] * log_g[None, :, None]), (128, H, C))
    wstate = np.exp((C - 1.0 - idx)[:, None] * log_g[None, :])
    cdec = np.broadcast_to(np.exp(C * log_g)[None, :], (128, H))
    pos = np.arange(4096, dtype=np.float32)
    inv_freq = (np.float32(10000.0) ** (-np.arange(64, dtype=np.float32) * np.float32(2.0 / 128))).astype(np.float32)
    ang = (pos[:, None] * inv_freq[None, :]).astype(np.float32)
    cos = np.cos(ang).astype(np.float32).reshape(32, 128, 64).transpose(1, 0, 2)
    sin = np.sin(ang).astype(np.float32).reshape(32, 128, 64).transpose(1, 0, 2)
    ks = np.float32(dk ** -0.5)
    f = lambda a: np.ascontiguousarray(a, dtype=np.float32)
    return dict(decayT=f(decayT), wcrossF=f(wcross), wstate=f(wstate), cdec=f(cdec), cos=f(cos), sin=f(sin), cosk=f(cos * ks), sink=f(sin * ks),
                ident=f(np.eye(128)), tvec=f(np.broadcast_to(np.arange(1, SUB + 1, dtype=np.float32)[None, :], (128, SUB))))


def kernel(x, c, w_ada, b_ada, norm_gains, w_in, w_ret_out, ssm_a_re, ssm_a_im, ssm_log_dt,
           ssm_b_re, ssm_b_im, ssm_c_re, ssm_c_im, ssm_d, w_s5_glu, w_out, w_ffn_in, w_ffn_out):
    f = lambda a: np.ascontiguousarray(np.asarray(a), dtype=np.float32)
    x = f(x); c = f(c)
    if "nc" not in _CACHE:
        _CACHE["nc"] = build_program()
        _CACHE["k"] = _consts()
    nc, K = _CACHE["nc"], _CACHE["k"]

    def gl(a):
        return f(np.asarray(a).reshape(32, 2, 64).transpose(1, 2, 0).reshape(128, 32))
    are, aim = gl(ssm_a_re[0]), gl(ssm_a_im[0])
    ldt = gl(np.broadcast_to(np.asarray(ssm_log_dt[0])[:, None], (64, 64)))
    bl = lambda a: f(np.asarray(a).reshape(32, 2, 64, 16).transpose(1, 2, 0, 3).reshape(128, 32, 16))
    bre, bim = bl(ssm_b_re[0]), bl(ssm_b_im[0])

    def cl(a):
        a = np.asarray(a).reshape(32, 2, 16, 64)
        o = np.zeros((2, 64, 32, 128), np.float32)
        for gp in range(32):
            for g2 in range(2):
                c0 = (gp % 4) * 32 + g2 * 16
                o[g2, :, gp, c0:c0 + 16] = a[gp, g2].T
        return o.reshape(128, 32, 128)
    cre, cim = cl(ssm_c_re[0]), cl(ssm_c_im[0])
    dfull = np.zeros((128, 8, 128), np.float32)
    dv = np.asarray(ssm_d[0]).reshape(8, 128)
    for kc in range(8):
        dfull[np.arange(128), kc, np.arange(128)] = dv[kc]
    gains = f(np.asarray(norm_gains[0]).reshape(4, 16, 128).transpose(2, 0, 1))
    badaT = f(np.asarray(b_ada[0]).reshape(96, 128).T)
    shared = dict(w_ada=f(w_ada[0]), badaT=badaT, gains=gains, w_in=f(w_in[0]), w_ret_out=f(w_ret_out[0]), w_s5_glu=f(w_s5_glu[0]),
                  w_out=f(w_out[0]), w_ffn_in=f(w_ffn_in[0]), w_ffn_out=f(w_ffn_out[0]),
                  decayT=K["decayT"], wcrossF=K["wcrossF"], wstate=K["wstate"], cdec=K["cdec"], ident=K["ident"], tvec=K["tvec"],
                  are=are, aim=aim, ldt=ldt, bre=bre, bim=bim, cre=cre, cim=cim, dfull=dfull)
    in_maps = []
    for core in range(8):
        b, half = core // 2, core % 2
        m = dict(shared)
        m["x"] = x[b, half * 2048:(half + 1) * 2048]
        m["xp"] = x[b, 0:2048]
        m["flag"] = np.full((128, 1), float(half), np.float32)
        m["cT"] = f(c[b].reshape(16, 128).T)
        for i, nm in enumerate(("cos", "sin", "cosk", "sink")):
            t = K[nm]
            r = np.empty((128, 32, 64), np.float32)
            r[:, 0:16] = t[:, 0:16]
            r[:, 16:32] = t[:, half * 16:(half + 1) * 16]
            m["rope%d" % i] = r
        in_maps.append(m)
    res = run_bass_kernel_spmd(nc, in_maps, core_ids=list(range(8)))
    _CACHE["res"] = res
    out = np.empty((4, 4096, D), np.float32)
    for core in range(8):
        b, half = core // 2, core % 2
        out[b, half * 2048:(half + 1) * 2048] = res.results[core]["out"]
    return out
```
